# Optimizing a Trainium2 kernel written in Bass

```python
import math
import jax, jax.numpy as jnp
from jax import lax
import numpy as np

D_MODEL = 1024
BATCH = 8
SEQ = 2048
DEPTH = 1
DEC_BATCH = 128
DEC_SEQ = 8
PAST_LEN = 16384
PAGE_SIZE = 128

M_HEADS = 4
M_HEAD_DIM = D_MODEL // M_HEADS
M_WIDTH = M_HEADS * M_HEAD_DIM
CONV_W = 4
G_HEADS = 4
G_DK = D_MODEL // (2 * G_HEADS)
G_DV = D_MODEL // G_HEADS
G_KW = G_HEADS * G_DK
G_VW = G_HEADS * G_DV
G_RANK = 16
G_TAU = 16.0
D_FF = 2816
CHUNK = 64
EPS = 1e-6

kernel_name = "hybrid_mlstm_gla_macaron_step"

IN_SPLITS = (M_WIDTH, M_WIDTH, M_WIDTH, 2 * M_HEADS, G_KW, G_KW, G_VW, G_VW, G_RANK, D_MODEL, D_MODEL)
IN_WIDTH = 3 * M_WIDTH + 2 * M_HEADS + 2 * G_KW + 2 * G_VW + G_RANK + 2 * D_MODEL


def _rmsnorm(x, g):
    xf = x.astype(jnp.float32)
    y = xf * lax.rsqrt(jnp.mean(xf * xf, axis=-1, keepdims=True) + EPS)
    return (y * g.astype(jnp.float32)).astype(x.dtype)


def _head_norm(h, g, dtype):
    y = h * lax.rsqrt(jnp.mean(h * h, axis=-1, keepdims=True) + EPS)
    B, H, T, d = h.shape
    y = y.transpose(0, 2, 1, 3).reshape(B, T, H * d)
    return (y * g.astype(jnp.float32)).astype(dtype)


def _swiglu(h, w_up, w_down):
    a, g = jnp.split(h @ w_up, 2, axis=-1)
    return (jax.nn.silu(g) * a) @ w_down


def _causal_conv(u, buf, w, b):
    full = jnp.concatenate([buf.astype(u.dtype), u], axis=1)
    T = u.shape[1]
    y = b
    for j in range(CONV_W):
        y = y + full[:, j:j + T] * w[j]
    return y, full[:, full.shape[1] - (CONV_W - 1):]


def _chunk_len(T):
    return CHUNK if T % CHUNK == 0 else T


def _to_chunks(a, L):
    B, H, T = a.shape[:3]
    a = a.reshape(B, H, T // L, L, *a.shape[3:])
    return jnp.moveaxis(a, 2, 0)


def _mlstm_scan(q, k, v, ig, lf, C0, n0, m0):
    B, H, T, d = q.shape
    L = _chunk_len(T)
    xs = tuple(_to_chunks(a, L) for a in (q, k, v, ig, lf))
    causal = jnp.tril(jnp.ones((L, L), dtype=bool))

    def step(carry, inp):
        C, n, m = carry
        qb, kb, vb, ib, fb = inp
        b = jnp.cumsum(fb, axis=-1)
        m_t = b + jnp.maximum(m[..., None], lax.cummax(ib - b, axis=2))
        inter = jnp.exp(b + m[..., None] - m_t)
        logD = b[..., :, None] - b[..., None, :] + ib[..., None, :] - m_t[..., :, None]
        Dm = jnp.exp(jnp.where(causal, logD, -jnp.inf))
        s = jnp.einsum('bhtd,bhsd->bhts', qb, kb) * Dm
        num = inter[..., None] * jnp.einsum('bhtd,bhed->bhte', qb, C) + jnp.einsum('bhts,bhse->bhte', s, vb)
        den = inter * jnp.einsum('bhtd,bhd->bht', qb, n) + jnp.sum(s, axis=-1)
        h = num / jnp.maximum(jnp.abs(den), jnp.exp(-m_t))[..., None]
        m_L = m_t[..., -1]
        w = jnp.exp(b[..., -1:] - b + ib - m_L[..., None])
        decay = jnp.exp(b[..., -1] + m - m_L)
        C_new = decay[..., None, None] * C + jnp.einsum('bhs,bhse,bhsd->bhed', w, vb, kb)
        n_new = decay[..., None] * n + jnp.einsum('bhs,bhsd->bhd', w, kb)
        return (C_new, n_new, m_L), h

    (C, n, m), hc = lax.scan(step, (C0, n0, m0), xs)
    h = jnp.moveaxis(hc, 0, 2).reshape(B, H, T, d)
    return h, C, n, m


def _gla_scan(q, k, v, la, S0):
    B, H, T, _ = q.shape
    dv = v.shape[-1]
    L = _chunk_len(T)
    xs = tuple(_to_chunks(a, L) for a in (q, k, v, la))
    causal = jnp.tril(jnp.ones((L, L), dtype=bool))[:, :, None]

    def step(S, inp):
        qb, kb, vb, ab = inp
        A = jnp.cumsum(ab, axis=2)
        inter = jnp.einsum('bhtd,bhde->bhte', qb * jnp.exp(A), S)
        diff = jnp.where(causal, A[:, :, :, None, :] - A[:, :, None, :, :], -jnp.inf)
        att = jnp.einsum('bhtd,bhtsd,bhsd->bhts', qb, jnp.exp(diff), kb)
        o = inter + jnp.einsum('bhts,bhse->bhte', att, vb)
        A_L = A[:, :, -1]
        S_new = jnp.exp(A_L)[..., None] * S + jnp.einsum('bhsd,bhse->bhde', kb * jnp.exp(A_L[:, :, None] - A), vb)
        return S_new, o

    S, oc = lax.scan(step, S0, xs)
    o = jnp.moveaxis(oc, 0, 2).reshape(B, H, T, dv)
    return o, S


def _mixer(h, conv_buf, C0, n0, m0, S0, W, l):
    B, T, _ = h.shape
    dt = h.dtype
    f32 = jnp.float32
    proj = h @ W['w_in'][l]
    idx, acc = [], 0
    for wdt in IN_SPLITS[:-1]:
        acc += wdt
        idx.append(acc)
    u_m, v_m, o_m, if_m, q_g, k_g, v_g, r_g, a_g, g_a, g_b = jnp.split(proj, idx, axis=-1)

    c, conv_new = _causal_conv(u_m, conv_buf, W['conv_w'][l], W['conv_b'][l])
    ch = jax.nn.silu(c).reshape(B, T, M_HEADS, M_HEAD_DIM)
    q = jnp.einsum('bthd,hde->bhte', ch, W['w_mq'][l]).astype(f32)
    k = (jnp.einsum('bthd,hde->bhte', ch, W['w_mk'][l]) * (M_HEAD_DIM ** -0.5)).astype(f32)
    v = v_m.reshape(B, T, M_HEADS, M_HEAD_DIM).transpose(0, 2, 1, 3).astype(f32)
    gates = (if_m.reshape(B, T, 2, M_HEADS) + W['b_if'][l]).astype(f32)
    ig = gates[:, :, 0].transpose(0, 2, 1)
    lf = jax.nn.log_sigmoid(gates[:, :, 1]).transpose(0, 2, 1)
    hm, C, n, m = _mlstm_scan(q, k, v, ig, lf, C0.astype(f32), n0.astype(f32), m0.astype(f32))
    hm = jax.nn.sigmoid(o_m) * _head_norm(hm, W['g_mhead'][l], dt)

    qg = (q_g.reshape(B, T, G_HEADS, G_DK).transpose(0, 2, 1, 3) * (G_DK ** -0.5)).astype(f32)
    kg = k_g.reshape(B, T, G_HEADS, G_DK).transpose(0, 2, 1, 3).astype(f32)
    vg = v_g.reshape(B, T, G_HEADS, G_DV).transpose(0, 2, 1, 3).astype(f32)
    la = jax.nn.log_sigmoid((a_g @ W['w_a2'][l] + W['b_a'][l]).astype(f32)) / G_TAU
    la = la.reshape(B, T, G_HEADS, G_DK).transpose(0, 2, 1, 3)
    og, S = _gla_scan(qg, kg, vg, la, S0.astype(f32))
    og = jax.nn.silu(r_g) * _head_norm(og, W['g_ghead'][l], dt)

    y = jax.nn.sigmoid(g_a) * (hm @ W['w_pa'][l]) + jax.nn.sigmoid(g_b) * (og @ W['w_pb'][l])
    out = y @ W['w_o'][l]
    return out, conv_new.astype(dt), C.astype(dt), n.astype(dt), m.astype(dt), S.astype(dt)


def _trunk(x, st_conv, st_C, st_n, st_m, st_S, W):
    convs, Cs, ns, ms, Ss = [], [], [], [], []
    for l in range(DEPTH):
        x = x + 0.5 * _swiglu(_rmsnorm(x, W['g_ffn1'][l]), W['w_ffn1_up'][l], W['w_ffn1_down'][l])
        mix, cv, C, n, m, S = _mixer(_rmsnorm(x, W['g_mix'][l]), st_conv[l], st_C[l], st_n[l], st_m[l], st_S[l], W, l)
        x = x + mix
        x = x + 0.5 * _swiglu(_rmsnorm(x, W['g_ffn2'][l]), W['w_ffn2_up'][l], W['w_ffn2_down'][l])
        convs.append(cv); Cs.append(C); ns.append(n); ms.append(m); Ss.append(S)
    y = _rmsnorm(x, W['g_final'])
    return y, jnp.stack(convs), jnp.stack(Cs), jnp.stack(ns), jnp.stack(ms), jnp.stack(Ss)


def setup_inputs(seed: int = 0) -> dict:
    key = jax.random.key(seed)
    ks = iter(jax.random.split(key, 40))
    nrm = lambda shape, s: jax.random.normal(next(ks), shape, jnp.float32) * s
    gain = lambda shape: 1.0 + nrm(shape, 0.05)
    f_bias = jnp.linspace(3.0, 6.0, M_HEADS, dtype=jnp.float32)
    b_if = jnp.stack([nrm((DEPTH, M_HEADS), 0.1), f_bias + nrm((DEPTH, M_HEADS), 0.1)], axis=1)
    return {
        "x_prompt": nrm((BATCH, SEQ, D_MODEL), 1.0),
        "x_sample": nrm((DEC_BATCH, DEC_SEQ, D_MODEL), 1.0),
        "state_conv": nrm((DEPTH, DEC_BATCH, CONV_W - 1, M_WIDTH), 1.0),
        "state_mlstm_C": nrm((DEPTH, DEC_BATCH, M_HEADS, M_HEAD_DIM, M_HEAD_DIM), 0.1),
        "state_mlstm_n": nrm((DEPTH, DEC_BATCH, M_HEADS, M_HEAD_DIM), 0.1),
        "state_mlstm_m": nrm((DEPTH, DEC_BATCH, M_HEADS), 1.0),
        "state_gla_S": nrm((DEPTH, DEC_BATCH, G_HEADS, G_DK, G_DV), 0.1),
        "g_ffn1": gain((DEPTH, D_MODEL)),
        "w_ffn1_up": nrm((DEPTH, D_MODEL, 2 * D_FF), D_MODEL ** -0.5),
        "w_ffn1_down": nrm((DEPTH, D_FF, D_MODEL), D_FF ** -0.5),
        "g_mix": gain((DEPTH, D_MODEL)),
        "w_in": nrm((DEPTH, D_MODEL, IN_WIDTH), D_MODEL ** -0.5),
        "conv_w": nrm((DEPTH, CONV_W, M_WIDTH), CONV_W ** -0.5),
        "conv_b": nrm((DEPTH, M_WIDTH), 0.02),
        "w_mq": nrm((DEPTH, M_HEADS, M_HEAD_DIM, M_HEAD_DIM), M_HEAD_DIM ** -0.5),
        "w_mk": nrm((DEPTH, M_HEADS, M_HEAD_DIM, M_HEAD_DIM), M_HEAD_DIM ** -0.5),
        "b_if": b_if,
        "g_mhead": gain((DEPTH, M_WIDTH)),
        "w_a2": nrm((DEPTH, G_RANK, G_KW), G_RANK ** -0.5),
        "b_a": nrm((DEPTH, G_KW), 0.1),
        "g_ghead": gain((DEPTH, G_VW)),
        "w_pa": nrm((DEPTH, M_WIDTH, D_MODEL), M_WIDTH ** -0.5),
        "w_pb": nrm((DEPTH, G_VW, D_MODEL), G_VW ** -0.5),
        "w_o": nrm((DEPTH, D_MODEL, D_MODEL), D_MODEL ** -0.5),
        "g_ffn2": gain((DEPTH, D_MODEL)),
        "w_ffn2_up": nrm((DEPTH, D_MODEL, 2 * D_FF), D_MODEL ** -0.5),
        "w_ffn2_down": nrm((DEPTH, D_FF, D_MODEL), D_FF ** -0.5),
        "g_final": gain((D_MODEL,)),
    }


def reference(x_prompt, x_sample, state_conv, state_mlstm_C, state_mlstm_n, state_mlstm_m, state_gla_S,
              g_ffn1, w_ffn1_up, w_ffn1_down, g_mix, w_in, conv_w, conv_b, w_mq, w_mk, b_if, g_mhead,
              w_a2, b_a, g_ghead, w_pa, w_pb, w_o, g_ffn2, w_ffn2_up, w_ffn2_down, g_final):
    W = {
        'g_ffn1': g_ffn1, 'w_ffn1_up': w_ffn1_up, 'w_ffn1_down': w_ffn1_down, 'g_mix': g_mix,
        'w_in': w_in, 'conv_w': conv_w, 'conv_b': conv_b, 'w_mq': w_mq, 'w_mk': w_mk, 'b_if': b_if,
        'g_mhead': g_mhead, 'w_a2': w_a2, 'b_a': b_a, 'g_ghead': g_ghead, 'w_pa': w_pa, 'w_pb': w_pb,
        'w_o': w_o, 'g_ffn2': g_ffn2, 'w_ffn2_up': w_ffn2_up, 'w_ffn2_down': w_ffn2_down, 'g_final': g_final,
    }
    B = x_prompt.shape[0]
    dt = x_prompt.dtype
    z_conv = jnp.zeros((DEPTH, B, CONV_W - 1, M_WIDTH), dt)
    z_C = jnp.zeros((DEPTH, B, M_HEADS, M_HEAD_DIM, M_HEAD_DIM), dt)
    z_n = jnp.zeros((DEPTH, B, M_HEADS, M_HEAD_DIM), dt)
    z_m = jnp.zeros((DEPTH, B, M_HEADS), dt)
    z_S = jnp.zeros((DEPTH, B, G_HEADS, G_DK, G_DV), dt)
    y_prompt, conv_p, C_p, n_p, m_p, S_p = _trunk(x_prompt, z_conv, z_C, z_n, z_m, z_S, W)
    y_sample, conv_s, C_s, n_s, m_s, S_s = _trunk(x_sample, state_conv, state_mlstm_C, state_mlstm_n,
                                                  state_mlstm_m, state_gla_S, W)
    return (y_prompt, y_sample, conv_p, C_p, n_p, m_p, S_p, conv_s, C_s, n_s, m_s, S_s)
```

```python
import numpy as np
from contextlib import ExitStack
import concourse.bass as bass
import concourse.mybir as mybir
from concourse.bass_utils import run_bass_kernel_spmd

F32 = mybir.dt.float32
BF16 = mybir.dt.bfloat16
AF = mybir.ActivationFunctionType
ALU = mybir.AluOpType

D = 1024
SEQ = 2048
NS = 16
DS = 8
DFF = 2816
NFF = DFF // 128
EPS = 1e-6
BIG = 30000.0
TMAX = 512
WSLOT = 4096
NWS = 3
ENG = ['pe', 'act', 'dve', 'pool', 'sp']

C_U, C_V, C_O, C_IF, C_QG, C_KG, C_VG, C_RG, C_AG, C_GA, C_GB = 0, 1024, 2048, 3072, 3080, 3592, 4104, 5128, 6152, 6168, 7192


class Sched:
    def __init__(self, nc, es):
        self.nc, self.es = nc, es
        self.stream = {e: [] for e in ENG}
        self.semobj = {}
        self.cnt = {}
        for e in ['pe', 'act', 'dve', 'pool']:
            self.semobj['E' + e] = es.enter_context(nc.semaphore('sem_' + e))
            self.cnt['E' + e] = 0
        self.waited = {e: {} for e in ENG}
        self.lastw = {}
        self.readers = {}
        self.ranges = {}
        self.rkeys = []
        self.curT = 0

    def register(self, key, lo, hi):
        if key not in self.ranges:
            self.ranges[key] = (lo, hi)
            self.rkeys.append(key)

    def _expand(self, writes):
        out = []
        for k in writes:
            out.append(k)
            if k in self.ranges:
                lo, hi = self.ranges[k]
                for k2 in self.rkeys:
                    if k2 != k:
                        l2, h2 = self.ranges[k2]
                        if l2 < hi and lo < h2:
                            out.append(k2)
        return out

    def _deps(self, e, reads, writes):
        need = {}

        def add(t):
            if t is not None and need.get(t[0], 0) < t[1]:
                need[t[0]] = t[1]
        for b in reads:
            add(self.lastw.get(b))
        for b in writes:
            add(self.lastw.get(b))
            for sk, v in self.readers.get(b, {}).items():
                add((sk, v))
        out = []
        for sk, v in need.items():
            if self.waited[e].get(sk, 0) < v:
                self.waited[e][sk] = v
                out.append((sk, v))
        return out

    def _commit(self, tok, reads, writes):
        for b in writes:
            self.lastw[b] = tok
            self.readers[b] = {}
        for b in reads:
            r = self.readers.setdefault(b, {})
            if r.get(tok[0], 0) < tok[1]:
                r[tok[0]] = tok[1]

    def _map(self, keys):
        sfx = '_%d' % self.curT
        return [(k + sfx) if (k + sfx) in self.ranges else k for k in keys]

    def op(self, e, fn, reads=(), writes=()):
        writes = self._expand(self._map(writes))
        reads = self._map(reads)
        waits = self._deps(e, reads, writes)
        sk = 'E' + e
        self.cnt[sk] += 1
        tok = (sk, self.cnt[sk])
        sem = self.semobj[sk]
        so = self.semobj

        def run(eng):
            for (k, v) in waits:
                eng.wait_ge(so[k], v)
            fn(eng).then_inc(sem, 1)
        self.stream[e].append(run)
        self._commit(tok, reads, writes)

    def dma(self, q, out, in_, reads, writes, skey, **kw):
        writes = self._expand(self._map(writes))
        reads = self._map(reads)
        waits = self._deps(q, reads, writes)
        sk = 'D' + skey
        if sk not in self.semobj:
            self.semobj[sk] = self.es.enter_context(self.nc.semaphore('dsem_' + skey))
            self.cnt[sk] = 0
        self.cnt[sk] += 16
        tok = (sk, self.cnt[sk])
        sem = self.semobj[sk]
        so = self.semobj

        def run(eng):
            for (k, v) in waits:
                eng.wait_ge(so[k], v)
            eng.dma_start(out=out, in_=in_, **kw).then_inc(sem, 16)
        self.stream[q].append(run)
        self._commit(tok, reads, writes)

    def finish(self):
        so, cnt = self.semobj, dict(self.cnt)

        def run(eng):
            for k, v in cnt.items():
                if v > 0:
                    eng.wait_ge(so[k], v)
        self.stream['sp'].append(run)


def build_program(debug=False):
    nc = bass.Bass("TRN2", target_bir_lowering=False)
    es = ExitStack()
    dbg = {}

    def din(name, shape):
        return nc.dram_tensor(name, list(shape), F32, kind="ExternalInput").ap()

    def dout(name, shape):
        return nc.dram_tensor(name, list(shape), F32, kind="ExternalOutput").ap()

    xp = din("xp", [SEQ, D]); xs = din("xs", [128, D])
    sconv = din("sconv", [48, D]); sC = din("sC", [NS, 4, 256, 256]); sn = din("sn", [NS, D])
    sm = din("sm", [NS, 4]); sS = din("sS", [NS, 4, 128, 256])
    g_ffn1 = din("g_ffn1", [D]); w1u = din("w_ffn1_up", [D, 2 * DFF]); w1d = din("w_ffn1_down", [DFF, D])
    g_mix = din("g_mix", [D]); w_in = din("w_in", [D, 8216]); conv_w = din("conv_w", [4, D]); conv_b = din("conv_b", [D])
    w_mq = din("w_mq", [4, 256, 256]); w_mk = din("w_mk", [4, 256, 256]); b_if = din("b_if", [2, 4])
    g_mhead = din("g_mhead", [D]); w_a2 = din("w_a2", [16, 512]); b_a = din("b_a", [512]); g_ghead = din("g_ghead", [D])
    w_pa = din("w_pa", [D, D]); w_pb = din("w_pb", [D, D]); w_o = din("w_o", [D, D])
    g_ffn2 = din("g_ffn2", [D]); w2u = din("w_ffn2_up", [D, 2 * DFF]); w2d = din("w_ffn2_down", [DFF, D])
    g_final = din("g_final", [D]); cst = din("consts", [128, 1792])

    yp = dout("yp", [SEQ, D]); ys = dout("ys", [128, D])
    o_convp = dout("conv_p", [3, D]); o_Cp = dout("C_p", [4, 256, 256]); o_np = dout("n_p", [4, 256])
    o_mp = dout("m_p", [1, 4]); o_Sp = dout("S_p", [4, 128, 256])
    o_convs = dout("conv_s", [48, D]); o_Cs = dout("C_s", [NS, 4, 256, 256]); o_ns = dout("n_s", [NS, D])
    o_ms = dout("m_s", [NS, 4]); o_Ss = dout("S_s", [NS, 4, 128, 256])

    with es:
        S = Sched(nc, es)

        def sb(name, shape, dt=F32):
            return es.enter_context(nc.sbuf_tensor(name, list(shape), dt))

        xT = sb("xT", [128, 8, TMAX])
        hT = sb("hT", [128, 8, TMAX], BF16)
        wsl = [sb("wsl%d" % i, [128, WSLOT], BF16) for i in range(NWS)]
        CST = sb("CST", [128, 1792])
        ident = CST[:, 0:128]; maskbigP = CST[:, 128:256]; maskbigS = CST[:, 256:384]
        mask01P = CST[:, 384:512]; mask01S = CST[:, 512:640]
        onesf = CST[:, 1280:1792]
        segm = CST[:, 640:656]
        SEL = sb("SEL", [4, 4, 128])
        identb = sb("identb", [128, 128], BF16); SELb = sb("SELb", [4, 4, 128], BF16)
        maskbP = sb("maskbP", [128, 128], BF16); maskbS = sb("maskbS", [128, 128], BF16)
        onesb = sb("onesb", [128, 128], BF16)
        gbc_m = sb("gbc_m", [128, D]); gbc_g = sb("gbc_g", [128, D]); gbc_f = sb("gbc_f", [128, D])
        gc1 = sb("gc1", [128, 8]); gcm = sb("gcm", [128, 8]); gc2 = sb("gc2", [128, 8])
        cw = sb("cw", [128, 4, 8]); cb = sb("cb", [128, 8])
        negba = sb("negba", [128, 4]); wa2 = sb("wa2", [16, 512])
        bif = sb("bif", [4, 2]); negbf = sb("negbf", [4, 1])
        epsc = sb("epsc", [128, 1])
        yout = sb("yout", [128, D])
        sqb = [sb("sqb%d" % i, [128, TMAX], BF16) for i in range(2)]
        srt = sb("srt", [128, TMAX]); rstd = sb("rstd", [128, TMAX])
        hmT = sb("hmT", [128, 8, TMAX], BF16); ogT = sb("ogT", [128, 8, TMAX], BF16)
        CT32 = sb("CT32", [128, 4, 2, 257]); CTb = sb("CTb", [128, 4, 2, 257], BF16)
        S32 = sb("S32", [128, 4, 256]); Sb = sb("Sb", [128, 4, 256], BF16)
        uhist = sb("uhist", [128, 8, 3]); uhS = sb("uhS", [128, 8, NS, 3]); uhS2 = sb("uhS2", [128, 8, NS, 3])
        carF = sb("carF", [4, 1]); carM = sb("carM", [4, 1]); carG = sb("carG", [4, 1])
        m0T = sb("m0T", [4, NS])
        colsTM = sb("colsTM", [128, 4, 20]); decbc = sb("decbc", [128, 4, 16]); dectm = sb("dectm", [NS, 4])
        DTsL = [sb("DTs%d" % i, [128, 128]) for i in range(2)]; sDbL = [sb("sDb%d" % i, [128, 128], BF16) for i in range(4)]
        junkb = sb("junkb", [128, 256], BF16)
        smL = [sb("smL%d" % i, [128, 8]) for i in range(2)]; smgL = [sb("smg%d" % i, [128, 8]) for i in range(2)]
        smF = [sb("smF%d" % i, [128, 4]) for i in range(2)]
        hgL = [sb("hg%d" % i, [128, 256], BF16) for i in range(4)]; kwL = [sb("kw%d" % i, [128, 256], BF16) for i in range(4)]
        attbL = [sb("attb%d" % i, [128, 128], BF16) for i in range(4)]; ogbL = [sb("ogb%d" % i, [128, 256], BF16) for i in range(4)]
        kgtokL = [sb("kgtok%d" % i, [128, 128], BF16) for i in range(4)]
        stmp = sb("stmp", [128, 256])
        Cn = [sb("Cn%d" % i, [128, 2, 256]) for i in range(1)]
        nnew = sb("nnew", [NS, D])
        sc48 = yout[0:48, :]; msout = sb("msout", [NS, 4])
        ARENA = 34816
        AR = sb("AR", [128, ARENA], BF16)

        PS = [es.enter_context(nc.psum_tensor("ps%d" % i, [128, 512], F32)) for i in range(8)]
        bankctr = [0]

        def bank():
            b = bankctr[0] % 6
            bankctr[0] += 1
            return b
        bmc = [0]; bgc = [0]

        def bankM():
            b = bmc[0] % 4
            bmc[0] += 1
            return b

        def bankG():
            b = 4 + bgc[0] % 2
            bgc[0] += 1
            return b

        class Arena:
            def __init__(self, T):
                self.off = 0
                self.T = T

            def get(self, name, n, dt=F32):
                name = '%s_%d' % (name, self.T)
                ne = n * (2 if dt == F32 else 1)
                o = self.off
                self.off += ne + (ne % 2)
                assert self.off <= ARENA, (name, self.off)
                ap = AR[:, o:o + ne]
                if dt == F32:
                    ap = ap.bitcast(F32)
                S.register(name, o, o + ne)
                return ap

        def P(b, n=512):
            return PS[b][:, 0:n]

        def mm(b, out, pairs, reads):
            def fn(pe):
                n = len(pairs)
                for i, (l, r) in enumerate(pairs):
                    ins = pe.matmul(out, l, r, start=(i == 0), stop=(i == n - 1))
                return ins
            S.op('pe', fn, reads=reads, writes=['ps%d' % b])

        def tr(b, outs_ins, idn, reads):
            def fn(pe):
                for (o, i) in outs_ins:
                    ins = pe.transpose(o, i, idn)
                return ins
            S.op('pe', fn, reads=reads, writes=['ps%d' % b])

        def act(out, in_, func, reads, writes, bias=None, scale=None, accum=None):
            kw_ = {}
            if bias is not None:
                kw_['bias'] = bias
            if scale is not None:
                kw_['scale'] = scale
            if accum is not None:
                kw_['accum_out'] = accum
            S.op('act', lambda e: e.activation(out=out, in_=in_, func=func, **kw_), reads, writes)

        def tt(out, a, b_, op, reads, writes, eng='dve'):
            S.op(eng, lambda e: e.tensor_tensor(out=out, in0=a, in1=b_, op=op), reads, writes)

        def stt(out, a, sc, b_, op0, op1, reads, writes):
            S.op('dve', lambda e: e.scalar_tensor_tensor(out=out, in0=a, scalar=sc, in1=b_, op0=op0, op1=op1), reads, writes)

        def ts(out, a, s1, op0, reads, writes, s2=None, op1=None, eng='dve'):
            if op1 is None:
                S.op(eng, lambda e: e.tensor_scalar(out=out, in0=a, scalar1=s1, scalar2=None, op0=op0), reads, writes)
            else:
                S.op(eng, lambda e: e.tensor_scalar(out=out, in0=a, scalar1=s1, scalar2=s2, op0=op0, op1=op1), reads, writes)

        def cp(out, in_, reads, writes, eng='dve'):
            if eng == 'act':
                S.op(eng, lambda e: e.activation(out=out, in_=in_, func=AF.Copy), reads, writes)
            else:
                S.op(eng, lambda e: e.tensor_copy(out=out, in_=in_), reads, writes)

        def recip(out, in_, reads, writes):
            S.op('dve', lambda e: e.reciprocal(out=out, in_=in_), reads, writes)

        def sig3(out, in_, reads, key):
            act(out, in_, AF.Exp, reads, [key], scale=-1.0)
            act(out, out, AF.Ln, [key], [key], bias=1.0)
            act(out, out, AF.Exp, [key], [key], scale=-1.0)

        def scan(out, d0, d1, init, op0, op1, reads, writes):
            S.op('dve', lambda e: e.tensor_tensor_scan(out=out, data0=d0, data1=d1, initial=init, op0=op0, op1=op1), reads, writes)

        def mset(ap, v, writes, eng='dve'):
            S.op(eng, lambda e: e.memset(ap, v), [], writes)

        uq = [0]

        def sdma(out, in_, reads, writes, skey, q='sp'):
            if skey in ('c2', 'c3'):
                uq[0] += 1
                skey = 'k%d' % uq[0]
            S.dma(q, out, in_, reads, writes, skey, allow_slow_non_contiguous=True)

        def dump(name, ap, reads):
            if not debug or name in dbg:
                return
            dbg[name] = nc.dram_tensor("dbg_" + name, list(ap.shape), ap.dtype, kind="ExternalOutput").ap()
            S.dma('sp', dbg[name], ap, reads, [], 'dbg', allow_slow_non_contiguous=True)

        wctr = [0]

        def wload(src, K, ncol):
            i = wctr[0] % NWS
            wctr[0] += 1
            view = wsl[i][:, 0:K * ncol].rearrange("p (k c) -> p k c", c=ncol)
            S.dma('pool', view, src.rearrange("(k p) c -> p k c", p=128), [], ['wsl%d' % i], 'w%d' % i)
            return view, 'wsl%d' % i

        sdma(CST[:], cst, [], ['CST'], 'c0')
        for h in range(4):
            pass
        sdma(SEL[:].rearrange("k h m -> k (h m)"), cst[0:4, 656:656 + 512], [], ['SEL'], 'c1')
        cp(identb[:], ident, ['CST'], ['identb'])
        cp(SELb[:], SEL[:], ['SEL'], ['SELb'])
        cp(maskbP[:], maskbigP, ['CST'], ['maskb'])
        cp(maskbS[:], maskbigS, ['CST'], ['maskb'])
        mset(onesb[:], 1.0, ['onesb'])
        mset(epsc[:], EPS, ['epsc'])
        for (t_, g_) in ((gbc_m, g_mhead), (gbc_g, g_ghead), (gbc_f, g_final)):
            sdma(t_[:], g_.rearrange("(o d) -> o d", o=1).broadcast_to([128, D]), [], [t_.name], 'c2')
        for (t_, g_) in ((gc1, g_ffn1), (gcm, g_mix), (gc2, g_ffn2), (cb, conv_b)):
            sdma(t_[:], g_.rearrange("(k p) -> p k", p=128), [], [t_.name], 'c3')
        sdma(cw[:], conv_w.rearrange("j (k p) -> p j k", p=128), [], ['cw'], 'c3')
        sdma(negba[:], b_a.rearrange("(h p) -> p h", p=128), [], ['negba'], 'c3')
        ts(negba[:], negba[:], -1.0, ALU.mult, ['negba'], ['negba'])
        sdma(wa2[:], w_a2, [], ['wa2'], 'c3')
        sdma(bif[:], b_if.rearrange("t h -> h t"), [], ['bif'], 'c3')
        ts(negbf[:], bif[:, 1:2], -1.0, ALU.mult, ['bif'], ['negbf'])
        mset(uhist[:], 0.0, ['uhist'])
        mset(carF[:], 0.0, ['carF']); mset(carM[:], 0.0, ['carM']); mset(carG[:], 0.0, ['carG'])
        mset(CT32[:], 0.0, ['CT32']); mset(CTb[:], 0.0, ['CTb'])
        mset(S32[:], 0.0, ['S32']); mset(Sb[:], 0.0, ['Sb'])

        XIN_OFF = 12288

        def xin_bufs(T):
            A_ = Arena(T)
            A_.off = XIN_OFF
            return [A_.get('xinA%d' % i, D) for i in range(T // 128)]

        def prefetch_x(src, T):
            S.curT = T
            bufs = xin_bufs(T)
            for i in range(T // 128):
                sdma(bufs[i], src[i * 128:(i + 1) * 128, :], [], ['xinA%d' % i], 'xinA%d' % i)

        def load_x(src, T, prefetched):
            nt = T // 128
            bufs = xin_bufs(T)
            if not prefetched:
                for i in range(nt):
                    sdma(bufs[i], src[i * 128:(i + 1) * 128, :], [], ['xinA%d' % i], 'xinA%d' % i)
            for i in range(nt):
                for half in range(2):
                    b = bank()
                    tr(b, [(PS[b][:, kk * 128:(kk + 1) * 128], bufs[i][:, (half * 4 + kk) * 128:(half * 4 + kk + 1) * 128]) for kk in range(4)],
                       ident, ['xinA%d' % i, 'CST'])
                    cp(xT[:, half * 4:half * 4 + 4, i * 128:(i + 1) * 128],
                       PS[b][:, :].rearrange("p (k c) -> p k c", c=128), ['ps%d' % b], ['xT%d' % (half * 4 + k) for k in range(4)],
                       eng='act' if half else 'dve')

        XK = ['xT%d' % k for k in range(8)]
        HK = ['hT%d' % k for k in range(8)]

        ncnt = [0]

        def norm_partial(k, T):
            n = ncnt[0] % 8
            ncnt[0] += 1
            q = sqb[n % 2]
            act(q[:, 0:T], xT[:, k, 0:T], AF.Square, ['xT%d' % k], ['sqb%d' % (n % 2)])
            S.op('pe', (lambda nn, qq, TT: (lambda pe: pe.matmul(PS[7][:, 0:TT], onesb[:], qq[:, 0:TT], start=(nn == 0), stop=(nn == 7))))(n, q, T),
                 ['sqb%d' % (n % 2), 'onesb'], ['ps7'])

        def norm(gc, gname, T, partial_done=False):
            b = 7
            if not partial_done:
                for k in range(8):
                    norm_partial(k, T)
            act(srt[:, 0:T], P(b, T), AF.Ln, ['ps%d' % b, 'epsc'], ['srt'], bias=epsc[:], scale=1.0 / D)
            act(rstd[:, 0:T], srt[:, 0:T], AF.Exp, ['srt'], ['rstd'], scale=-0.5)
            for k in range(8):
                stt(hT[:, k, 0:T], xT[:, k, 0:T], gc[:, k:k + 1], rstd[:, 0:T], ALU.mult, ALU.mult,
                    ['xT%d' % k, 'rstd', gname], ['hT%d' % k])

        def ffn(wu, wd, gc, gname, T, A, partial_done=False, next_norm=True):
            norm(gc, gname, T, partial_done)
            actb = A.get('act', NFF * T, BF16).rearrange("p (j t) -> p j t", t=T)
            for part in (1, 0):
                c0 = part * DFF
                for (uo, nc_) in [(i * 512, 512) for i in range(5)] + [(2560, 256)]:
                    wv, wk = wload(wu[:, c0 + uo:c0 + uo + nc_], 8, nc_)
                    for jj in range(nc_ // 128):
                        j = uo // 128 + jj
                        b = bank()
                        mm(b, P(b, T), [(wv[:, k, jj * 128:(jj + 1) * 128], hT[:, k, 0:T]) for k in range(8)], HK + [wk])
                        if part == 1:
                            act(actb[:, j, :], P(b, T), AF.Silu, ['ps%d' % b], ['act'])
                        else:
                            tt(actb[:, j, :], P(b, T), actb[:, j, :], ALU.mult, ['ps%d' % b, 'act'], ['act'])
            for f in range(8):
                wv, wk = wload(wd[:, f * 128:(f + 1) * 128], NFF, 128)
                b = bank()
                mm(b, P(b, T), [(wv[:, k, :], actb[:, k, :]) for k in range(NFF)], ['act', wk])
                stt(xT[:, f, 0:T], P(b, T), 0.5, xT[:, f, 0:T], ALU.mult, ALU.add, ['ps%d' % b, 'xT%d' % f], ['xT%d' % f])
                if next_norm:
                    norm_partial(f, T)

        def final_out(dst, T):
            nt = T // 128
            A_ = Arena(T)
            xtk = [A_.get('xtokA%d' % i, D) for i in range(nt)]
            yo = [A_.get('youtA%d' % i, D) for i in range(2)]
            for i in range(nt):
                for half in range(2):
                    b = bank()
                    tr(b, [(PS[b][:, kk * 128:(kk + 1) * 128], xT[:, half * 4 + kk, i * 128:(i + 1) * 128]) for kk in range(4)],
                       ident, XK + ['CST'])
                    cp(xtk[i][:, half * 512:(half + 1) * 512], PS[b][:, :], ['ps%d' % b], ['xtokA%d' % i], eng='act' if half else 'dve')
            for i in range(nt):
                y_ = yo[i % 2]; yk = 'youtA%d' % (i % 2); sm_ = smF[i % 2]; sk = 'smF%d_' % (i % 2)
                act(y_, xtk[i], AF.Square, ['xtokA%d' % i], [yk, sk + '0'], accum=sm_[:, 0:1])
                act(sm_[:, 1:2], sm_[:, 0:1], AF.Ln, [sk + '0', 'epsc'], [sk + '1'], bias=epsc[:], scale=1.0 / D)
                act(sm_[:, 2:3], sm_[:, 1:2], AF.Exp, [sk + '1'], [sk + '2'], scale=-0.5)
                stt(y_, xtk[i], sm_[:, 2:3], gbc_f[:], ALU.mult, ALU.mult, ['xtokA%d' % i, sk + '2', 'gbc_f'], [yk])
                sdma(dst[i * 128:(i + 1) * 128, :], y_, [yk], [], 'youtA%d' % (i % 2))

        def mixer(T, kind, last, A):
            nt = T // 128
            L = 128 if kind == 'P' else DS
            nseg = T // L
            mbigb = maskbP[:] if kind == 'P' else maskbS[:]
            m01 = mask01P if kind == 'P' else mask01S
            norm(gcm, 'gcm', T, True)
            G = {n: A.get('g_' + n, T) for n in ('gi', 'lf', 'F', 'm', 'G', 'a', 'inter', 'w', 'em', 'em2', 't1', 't2')}

            def g4(n):
                return G[n][0:4, :]

            def g3(n):
                return G[n][0:4, :].rearrange("p (n c) -> p n c", c=L)
            wv, wk = wload(w_in[:, C_IF:C_IF + 8], 8, 8)
            bi = bank(); bf_ = bank()
            mm(bi, PS[bi][0:4, 0:T], [(wv[:, k, 0:4], hT[:, k, 0:T]) for k in range(8)], HK + [wk])
            mm(bf_, PS[bf_][0:4, 0:T], [(wv[:, k, 4:8], hT[:, k, 0:T]) for k in range(8)], HK + [wk])
            act(g4('gi'), PS[bi][0:4, 0:T], AF.Identity, ['ps%d' % bi, 'bif'], ['g_gi'], bias=bif[:, 0:1])
            act(g4('t1'), PS[bf_][0:4, 0:T], AF.Exp, ['ps%d' % bf_, 'negbf'], ['g_t1'], bias=negbf[:], scale=-1.0)
            act(g4('t2'), g4('t1'), AF.Ln, ['g_t1'], ['g_t2'], bias=1.0)
            ts(g4('lf'), g4('t2'), -1.0, ALU.mult, ['g_t2'], ['g_lf'])
            gst = A.get('g_gst', 16); gl = A.get('g_gl', 16); dec = A.get('g_dec', 16)
            if kind == 'S':
                qz = A.get('qz', 2 * 2176, BF16).rearrange("p (e c) -> p e c", c=2176)
                qgz = A.get('qgz', 2176, BF16)
                C0 = [A.get('C0_%d' % i, 512).rearrange("p (e d) -> p e d", d=256) for i in range(4)]
                C0Tb = [A.get('C0Tb%d' % i, 2 * 258, BF16).rearrange("p (e d) -> p e d", d=258)[:, :, 0:257] for i in range(4)]
                S0 = [A.get('S0_%d' % i, 256) for i in range(4)]
                S0b = [A.get('S0b%d' % i, 256, BF16) for i in range(4)]
                n0 = A.get('n0', D)[0:NS, :]
                n0T = A.get('n0T', 8 * NS).rearrange("p (k s) -> p k s", s=NS)
                wz = A.get('wz', NS, BF16)
                kwz = [A.get('kwz%d' % i, 256, BF16) for i in range(2)]
                kgz = [A.get('kgz%d' % i, 128, BF16) for i in range(2)]
                mset(qz, 0.0, ['qz'], eng='pool')
                mset(qgz, 0.0, ['qgz'], eng='pool')
            if kind == 'P':
                scan(g4('F'), onesf[0:4, 0:T], g4('lf'), carF[:], ALU.mult, ALU.add, ['CST', 'g_lf', 'carF'], ['g_F'])
                scan(g4('m'), g4('lf'), g4('gi'), carM[:], ALU.add, ALU.max, ['g_lf', 'g_gi', 'carM'], ['g_m'])
            else:
                sdma(m0T[:], sm.rearrange("s h -> h s"), [], ['m0T'], 'st_m')
                for s in range(NS):
                    sl = slice(s * DS, (s + 1) * DS)
                    scan(G['F'][0:4, sl], onesf[0:4, 0:DS], G['lf'][0:4, sl], 0.0, ALU.mult, ALU.add, ['CST', 'g_lf'], ['g_F'])
                    scan(G['m'][0:4, sl], G['lf'][0:4, sl], G['gi'][0:4, sl], m0T[:, s:s + 1], ALU.add, ALU.max,
                         ['g_lf', 'g_gi', 'm0T'], ['g_m'])
            tt(g4('G'), g4('m'), g4('F'), ALU.subtract, ['g_m', 'g_F'], ['g_G'])
            tt(g4('a'), g4('gi'), g4('F'), ALU.subtract, ['g_gi', 'g_F'], ['g_a'])
            Ghi = A.get('g_Ghi', T, BF16); Gmid = A.get('g_Gmid', T, BF16); Glo = A.get('g_Glo', T, BF16)
            act(g4('em'), g4('m'), AF.Exp, ['g_m'], ['g_em'], scale=-1.0)
            act(g4('em2'), g4('m'), AF.Exp, ['g_m'], ['g_em2'], scale=-2.0)
            cp(gl[0:4, 0:nseg], g3('G')[:, :, L - 1], ['g_G'], ['g_gl'])
            if kind == 'P':
                cp(gst[0:4, 0:1], carG[:], ['carG'], ['g_gst'])
                if nseg > 1:
                    cp(gst[0:4, 1:nseg], gl[0:4, 0:nseg - 1], ['g_gl', 'g_gst'], ['g_gst'])
            else:
                cp(gst[0:4, 0:nseg], m0T[:], ['m0T'], ['g_gst'])
            tt(g3('t1'), gst[0:4, 0:nseg].unsqueeze(2).broadcast_to([4, nseg, L]), g3('G'), ALU.subtract, ['g_gst', 'g_G'], ['g_t1'])
            act(g4('inter'), g4('t1'), AF.Exp, ['g_t1'], ['g_inter'])
            tt(g3('t2'), g3('a'), gl[0:4, 0:nseg].unsqueeze(2).broadcast_to([4, nseg, L]), ALU.subtract, ['g_a', 'g_gl'], ['g_t2'])
            act(g4('w'), g4('t2'), AF.Exp, ['g_t2'], ['g_w'])
            tt(dec[0:4, 0:nseg], gst[0:4, 0:nseg], gl[0:4, 0:nseg], ALU.subtract, ['g_gst', 'g_gl'], ['g_dec'])
            act(dec[0:4, 0:nseg], dec[0:4, 0:nseg], AF.Exp, ['g_dec'], ['g_dec'])
            cp(Ghi[0:4, :], g4('G'), ['g_G'], ['g_Ghi'])
            tt(g4('t1'), g4('G'), Ghi[0:4, :], ALU.subtract, ['g_G', 'g_Ghi', 'g_inter'], ['g_t1'])
            cp(Gmid[0:4, :], g4('t1'), ['g_t1'], ['g_Gmid'])
            tt(g4('t2'), g4('t1'), Gmid[0:4, :], ALU.subtract, ['g_t1', 'g_Gmid', 'g_w'], ['g_t2'])
            cp(Glo[0:4, :], g4('t2'), ['g_t2'], ['g_Glo'])
            if kind == 'P':
                cp(carF[:], G['F'][0:4, T - 1:T], ['g_F', 'carF'], ['carF'])
                cp(carM[:], G['m'][0:4, T - 1:T], ['g_m', 'carM'], ['carM'])
                cp(carG[:], G['G'][0:4, T - 1:T], ['g_G', 'carG'], ['carG'])
            b = bank()
            pairs = []
            for i in range(nt):
                for qi, qn in enumerate(('a', 'inter', 'w', 'em', 'em2')):
                    pairs.append((PS[b][:, i * 20 + qi * 4:i * 20 + qi * 4 + 4], G[qn][0:4, i * 128:(i + 1) * 128]))

            def fnc(pe, pairs=pairs):
                for (o, l) in pairs:
                    ins = pe.matmul(o, l, ident[0:4, 0:4], start=True, stop=True)
                return ins
            S.op('pe', fnc, ['g_a', 'g_inter', 'g_w', 'g_em', 'g_em2', 'CST'], ['ps%d' % b])
            cp(colsTM[:, 0:nt, :], PS[b][:, 0:nt * 20].rearrange("p (i c) -> p i c", c=20), ['ps%d' % b], ['colsTM'])
            b = bank()

            def fnd(pe, b=b, nseg=nseg, dec=dec):
                for h in range(4):
                    ins = pe.matmul(PS[b][:, h * 16:h * 16 + nseg], SEL[:, h, :], dec[0:4, 0:nseg], start=True, stop=True)
                return ins
            S.op('pe', fnd, ['g_dec', 'SEL'], ['ps%d' % b])
            cp(decbc[:, :, 0:nseg], PS[b][:, 0:64].rearrange("p (h c) -> p h c", c=16)[:, :, 0:nseg], ['ps%d' % b], ['decbc'])
            dump('h' + kind, hT[:, :, 0:T], HK)
            for nme in ('gi', 'lf', 'F', 'm', 'G', 'a', 'inter', 'w', 'em'):
                dump(nme + kind, g4(nme), ['g_' + nme])
            dump('colsTM' + kind, colsTM[:, 0:nt, :], ['colsTM'])
            dump('decbc' + kind, decbc[:, :, 0:nseg], ['decbc'])
            if kind == 'S':
                b = bank()
                mm(b, PS[b][0:NS, 0:4], [(dec[0:4, 0:NS], ident[0:4, 0:4])], ['g_dec', 'CST'])
                cp(dectm[:], PS[b][0:NS, 0:4], ['ps%d' % b], ['dectm'])
                b = bank()
                mm(b, PS[b][0:NS, 0:4], [(g3('m')[:, :, DS - 1], ident[0:4, 0:4])], ['g_m', 'CST'])
                cp(msout[:], PS[b][0:NS, 0:4], ['ps%d' % b], ['msout'])
                sdma(o_ms, msout[:], ['msout'], [], 'st_ms')
                sdma(sc48[:], sconv, [], ['yout'], 'st_c')
                for k in range(8):
                    b = bank()
                    tr(b, [(PS[b][:, 0:48], sc48[:, k * 128:(k + 1) * 128])], ident[0:48, 0:48], ['yout', 'CST'])
                    cp(uhS[:, k, :, :], PS[b][:, 0:48].rearrange("p (s j) -> p s j", j=3), ['ps%d' % b], ['uhS'])
                sdma(n0, sn, [], ['n0'], 'st_n')
                for k in range(8):
                    b = bank()
                    tr(b, [(PS[b][:, 0:NS], n0[:, k * 128:(k + 1) * 128])], ident[0:NS, 0:NS], ['n0', 'CST'])
                    cp(n0T[:, k, :], PS[b][:, 0:NS], ['ps%d' % b], ['n0T'])
            elif last:
                b = bank()
                mm(b, PS[b][0:1, 0:4], [(G['m'][0:4, T - 1:T], ident[0:4, 0:4])], ['g_m', 'CST'])
                cp(msout[0:1, :], PS[b][0:1, 0:4], ['ps%d' % b], ['msout'])
                sdma(o_mp, msout[0:1, :], ['msout'], [], 'st_ms')

            a_mark = A.off
            if kind == 'P':
                u = A.get('u', 2 * (T + 3)).rearrange("p (e t) -> p e t", t=T + 3)
            else:
                u4 = A.get('u', 2 * NS * 11).rearrange("p (e s t) -> p e s t", s=NS, t=11)
            cbuf = A.get('cbuf', T); ctmp = A.get('ctmp', T)
            ch = A.get('ch', 2 * T, BF16).rearrange("p (e t) -> p e t", t=T)
            qT = A.get('qT', 2 * T, BF16).rearrange("p (e t) -> p e t", t=T)
            kT = A.get('kT', 2 * T, BF16).rearrange("p (e t) -> p e t", t=T)
            qiT = A.get('qiT', 2 * T, BF16).rearrange("p (e t) -> p e t", t=T)
            vm = A.get('vm', nt * 257, BF16).rearrange("p (i c) -> p i c", c=257)
            ogm = A.get('ogm', nt * 256, BF16).rearrange("p (i c) -> p i c", c=256)
            sgt = A.get('sgt', 256)
            agT = A.get('agT', T)
            t1 = A.get('gt1', T); t2 = A.get('gt2', T); Cs = A.get('Cs', T); eal = A.get('eal', 16)
            qgT = A.get('qgT', T, BF16); kgT = A.get('kgT', T, BF16)
            vg = A.get('vg', nt * 256, BF16).rearrange("p (i c) -> p i c", c=256)
            rg = A.get('rg', nt * 256, BF16).rearrange("p (i c) -> p i c", c=256)
            sgt2 = A.get('sgt2', 256)
            if kind == 'P':
                CTs_ = [A.get('CTsnap%d' % j, 2 * 258, BF16).rearrange("p (e c) -> p e c", c=258) for j in range(nt - 1)]
                Ssnap = [A.get('Ssnap%d' % j, 256, BF16) for j in range(nt - 1)]
            NSL = 4

            def interleave(gens, ratios):
                gens = list(gens); ratios = list(ratios)
                while gens:
                    for gi in range(len(gens) - 1, -1, -1):
                        pass
                    alive = []
                    for g_, r_ in zip(gens, ratios):
                        ok = True
                        for _ in range(r_):
                            try:
                                next(g_)
                            except StopIteration:
                                ok = False
                                break
                        if ok:
                            alive.append((g_, r_))
                    gens = [a for a, _ in alive]; ratios = [b_ for _, b_ in alive]
                    yield

            def ml_head(h):
                wvu, wku = wload(w_in[:, C_U + h * 256:C_U + (h + 1) * 256], 8, 256)
                wins = []
                for ec in range(2):
                    c = 2 * h + ec
                    b = bankM()
                    mm(b, P(b, T), [(wvu[:, k, ec * 128:(ec + 1) * 128], hT[:, k, 0:T]) for k in range(8)], HK + [wku])
                    if kind == 'P':
                        cp(u[:, ec, 0:3], uhist[:, c, :], ['uhist'], ['u'])
                        cp(u[:, ec, 3:3 + T], P(b, T), ['ps%d' % b], ['u'], eng='act')
                        cp(uhist[:, c, :], u[:, ec, T:T + 3], ['u'], ['uhist'])
                        wins.append(([u[:, ec, 3 - j:3 - j + T] for j in range(4)], cbuf[:, 0:T]))
                    else:
                        cp(u4[:, ec, :, 0:3], uhS[:, c, :, :], ['uhS'], ['u'])
                        cp(u4[:, ec, :, 3:11], P(b, T).rearrange("p (s t) -> p s t", t=DS), ['ps%d' % b], ['u'], eng='act')
                        cp(uhS2[:, c, :, :], u4[:, ec, :, 8:11], ['u'], ['uhS2'])
                        wins.append(([u4[:, ec, :, 3 - j:11 - j] for j in range(4)], cbuf[:, 0:T].rearrange("p (s t) -> p s t", t=DS)))
                    yield

                def conv_chain():
                    for ec in range(2):
                        c = 2 * h + ec
                        win, cv = wins[ec]
                        act(cv, win[0], AF.Identity, ['u', 'cw', 'cb'], ['cbuf'], bias=cb[:, c:c + 1], scale=cw[:, 3, c:c + 1])
                        yield
                        for j in (1, 2, 3):
                            stt(cv, win[j], cw[:, 3 - j, c:c + 1], cv, ALU.mult, ALU.add, ['u', 'cw', 'cbuf'], ['cbuf'])
                            yield
                        act(ctmp[:, 0:T], cbuf[:, 0:T], AF.Exp, ['cbuf'], ['ctmp'], scale=-1.0)
                        yield
                        act(ctmp[:, 0:T], ctmp[:, 0:T], AF.Ln, ['ctmp'], ['ctmp'], bias=1.0)
                        yield
                        act(ctmp[:, 0:T], ctmp[:, 0:T], AF.Exp, ['ctmp'], ['ctmp'], scale=-1.0)
                        yield
                        tt(ch[:, ec, :], cbuf[:, 0:T], ctmp[:, 0:T], ALU.mult, ['cbuf', 'ctmp'], ['ch'])
                        yield

                def vo_proj():
                    wv, wk = wload(w_in[:, C_V + h * 256:C_V + (h + 1) * 256], 8, 256)
                    mset(vm[:, :, 256:257], 1.0, ['vm'])
                    for i in range(nt):
                        b = bankM()
                        mm(b, P(b, 256), [(hT[:, k, i * 128:(i + 1) * 128], wv[:, k, :]) for k in range(8)], HK + [wk])
                        cp(vm[:, i, 0:256], P(b, 256), ['ps%d' % b], ['vm'], eng='act')
                        yield
                    wv, wk = wload(w_in[:, C_O + h * 256:C_O + (h + 1) * 256], 8, 256)
                    for i in range(nt):
                        b = bankM()
                        mm(b, P(b, 256), [(hT[:, k, i * 128:(i + 1) * 128], wv[:, k, :]) for k in range(8)], HK + [wk])
                        sig3(sgt[:, :], P(b, 256), ['ps%d' % b], 'sgt')
                        tt(ogm[:, i, :], sgt[:, :], gbc_m[:, h * 256:(h + 1) * 256], ALU.mult, ['sgt', 'gbc_m'], ['ogm'])
                        yield

                yield from interleave([conv_chain(), vo_proj()], [2, 1])
                for (wsrc, dst, dk_, scl) in ((w_mq, qT, 'qT', 1.0), (w_mk, kT, 'kT', 1.0 / 16.0)):
                    wv, wk = wload(wsrc[h], 2, 256)
                    for ec in range(2):
                        b = bankM()
                        mm(b, P(b, T), [(wv[:, kc, ec * 128:(ec + 1) * 128], ch[:, kc, :]) for kc in range(2)], ['ch', wk])
                        act(dst[:, ec, :], P(b, T), AF.Copy, ['ps%d' % b], [dk_], scale=scl)
                    yield
                b = bankM()
                mm(b, P(b, T), [(SEL[:, h, :], G['inter'][0:4, 0:T])], ['SEL', 'g_inter'])
                for ec in range(2):
                    tt(qiT[:, ec, :], P(b, T), qT[:, ec, :], ALU.mult, ['ps%d' % b, 'qT'], ['qiT'])
                yield
                if h == 0:
                    if kind == 'P':
                        dump('u' + kind, u, ['u'])
                    dump('ch' + kind, ch, ['ch']); dump('qT' + kind, qT, ['qT']); dump('kT' + kind, kT, ['kT'])
                    dump('vm' + kind, vm, ['vm']); dump('ogm' + kind, ogm, ['ogm'])
                def finishA(i, bP):
                    pk = 'ps%d' % bP
                    sm_ = smL[i % 2]; sq_ = 'sm%d_' % (i % 2); hg_ = hgL[i]; hk_ = 'hg%d' % i
                    act(sm_[:, 0:1], PS[bP][:, 256:257], AF.Square, [pk], [sq_ + '0'])
                    act(junkb[:], PS[bP][:, 0:256], AF.Square, [pk], ['junkb', sq_ + '2'], accum=sm_[:, 2:3])
                    ts(sm_[:, 1:2], sm_[:, 0:1], colsTM[:, i, 16 + h:17 + h], ALU.max, [sq_ + '0', 'colsTM'], [sq_ + '1'], s2=EPS, op1=ALU.mult)
                    act(sm_[:, 3:4], sm_[:, 2:3], AF.Ln, [sq_ + '2', sq_ + '1'], [sq_ + '3'], bias=sm_[:, 1:2], scale=1.0 / 256)
                    act(sm_[:, 4:5], sm_[:, 3:4], AF.Exp, [sq_ + '3'], [sq_ + '4'], scale=-0.5)
                    stt(hg_[:], PS[bP][:, 0:256], sm_[:, 4:5], ogm[:, i, :], ALU.mult, ALU.mult, [pk, sq_ + '4', 'ogm'], [hk_])
                    if h == 0 and i == 0:
                        dump('DTs' + kind, DTsL[0][:], ['DTs0']); dump('sDb' + kind, sDbL[0][:], ['sDb0'])
                        dump('hg' + kind, hg_[:], [hk_])

                def finishB(i):
                    tsl = slice(i * 128, (i + 1) * 128)
                    hg_ = hgL[i]; hk_ = 'hg%d' % i
                    bt = bankM()
                    tr(bt, [(PS[bt][:, :].bitcast(BF16)[:, ec * 128:(ec + 1) * 128], hg_[:, ec * 128:(ec + 1) * 128]) for ec in range(2)],
                       identb[:], [hk_, 'identb'])
                    cp(hmT[:, 2 * h:2 * h + 2, tsl], PS[bt][:, :].bitcast(BF16)[:, 0:256].rearrange("p (e c) -> p e c", c=128),
                       ['ps%d' % bt], ['hmT%d' % h], eng='act')

                for i in range(nt):
                    tsl = slice(i * 128, (i + 1) * 128)
                    b1 = bankM()
                    mm(b1, PS[b1][:, 0:128], [(kT[:, ec, tsl], qT[:, ec, tsl]) for ec in range(2)], ['kT', 'qT'])
                    b2 = bankM()
                    mm(b2, PS[b2][:, 0:128], [(SELb[:, h, :], Ghi[0:4, tsl]), (SELb[:, h, :], Gmid[0:4, tsl]), (SELb[:, h, :], Glo[0:4, tsl]),
                                              (identb[:], mbigb)], ['SELb', 'g_Ghi', 'g_Gmid', 'g_Glo', 'identb', 'maskb'])
                    D_ = DTsL[i % 2]; dk_ = 'DTs%d' % (i % 2)
                    act(D_[:], PS[b2][:, 0:128], AF.Exp, ['ps%d' % b2, 'colsTM'], [dk_], bias=colsTM[:, i, h:h + 1], scale=-1.0)
                    tt(sDbL[i][:], PS[b1][:, 0:128], D_[:], ALU.mult, ['ps%d' % b1, dk_], ['sDb%d' % i])
                    yield
                if kind == 'P':
                    snaps = [CTb[:, h, :, :]] + [CTs_[j][:, :, 0:257] for j in range(nt - 1)]
                    snapk = ['CTb'] + ['CTsnap%d' % j for j in range(nt - 1)]
                    for i in range(nt):
                        tsl = slice(i * 128, (i + 1) * 128)
                        bk = bankM()
                        tr(bk, [(PS[bk][:, :].bitcast(BF16)[:, ec * 128:(ec + 1) * 128], kT[:, ec, tsl]) for ec in range(2)], identb[:], ['kT', 'identb'])
                        ts(kwL[i][:], PS[bk][:, :].bitcast(BF16)[:, 0:256], colsTM[:, i, 8 + h:9 + h], ALU.mult, ['ps%d' % bk, 'colsTM'], ['kw%d' % i])
                        yield
                    for i in range(nt):
                        kw_ = kwL[i]; kk_ = 'kw%d' % i
                        for dc in range(2):
                            bu = bankM()
                            mm(bu, PS[bu][:, 0:257], [(kw_[:, dc * 128:(dc + 1) * 128], vm[:, i, :])], [kk_, 'vm'])
                            stt(CT32[:, h, dc, :], CT32[:, h, dc, :], decbc[:, h, i:i + 1], PS[bu][:, 0:257], ALU.mult, ALU.add,
                                ['CT32', 'decbc', 'ps%d' % bu], ['CT32'])
                        if i < nt - 1:
                            cp(snaps[i + 1], CT32[:, h, :, :], ['CT32'], [snapk[i + 1]], eng='act')
                        if h == 0 and i == 0:
                            dump('CT0' + kind, CT32[:, 0, :, :], ['CT32']); dump('kw' + kind, kw_[:], [kk_])
                        yield
                    for i in range(nt):
                        tsl = slice(i * 128, (i + 1) * 128)
                        bP = bankM()
                        mm(bP, PS[bP][:, 0:257], [(sDbL[i][:], vm[:, i, :])] + [(qiT[:, ec, tsl], snaps[i][:, ec, :]) for ec in range(2)],
                           ['sDb%d' % i, 'vm', 'qiT', snapk[i]])
                        finishA(i, bP)
                        yield
                    for i in range(nt):
                        finishB(i)
                        yield
                    cp(CTb[:, h, :, :], CT32[:, h, :, :], ['CT32'], ['CTb'], eng='act')
                    yield
                else:
                    i = 0
                    tsl = slice(0, 128)
                    kw = kwL[0]
                    sDb = sDbL[0]
                    bP = 6
                    for ec in range(2):
                        cp(qz[:, ec, :].rearrange("p (j c) -> p j c", c=136)[:, :, 0:8],
                           qiT[:, ec, :].rearrange("p (j c) -> p j c", c=8), ['qiT'], ['qz'])
                    bk = bankM()
                    tr(bk, [(PS[bk][:, :].bitcast(BF16)[:, ec * 128:(ec + 1) * 128], kT[:, ec, tsl]) for ec in range(2)], identb[:], ['kT', 'identb'])
                    cp(kw[:], PS[bk][:, :].bitcast(BF16)[:, 0:256], ['ps%d' % bk], ['kw0'])
                    ts(wz, segm, colsTM[:, 0, 8 + h:9 + h], ALU.mult, ['CST', 'colsTM'], ['wz'])

                    def loadC(s_):
                        k_ = s_ % NSL
                        sdma(C0[k_], sC[s_, h].rearrange("(e p) d -> p e d", p=128), [], ['C0_%d' % k_], 'C0_%d' % k_)
                    for s in range(NSL):
                        loadC(s)
                    yield

                    def st1(s):
                        sl_ = s % NSL
                        for dc in range(2):
                            bt = bankM()
                            tr(bt, [(PS[bt][:, e2 * 128:(e2 + 1) * 128], C0[sl_][:, e2, dc * 128:(dc + 1) * 128]) for e2 in range(2)],
                               ident, ['C0_%d' % sl_, 'CST'])
                            cp(C0Tb[sl_][:, dc, 0:256], PS[bt][:, 0:256], ['ps%d' % bt], ['C0Tb%d' % sl_], eng='act')
                            cp(C0Tb[sl_][:, dc, 256:257], n0T[:, 2 * h + dc, s:s + 1], ['n0T'], ['C0Tb%d' % sl_])

                    def st2(s):
                        sl_ = s % NSL
                        S.op('pe', (lambda s_, sl__: (lambda pe: [pe.matmul(PS[6][:, 0:257], qz[:, 0, s_ * 128:(s_ + 1) * 128], C0Tb[sl__][:, 0, :], start=(s_ == 0), stop=False),
                                                                 pe.matmul(PS[6][:, 0:257], qz[:, 1, s_ * 128:(s_ + 1) * 128], C0Tb[sl__][:, 1, :], start=False, stop=False)][-1]))(s, sl_),
                             ['qz', 'C0Tb%d' % sl_], ['ps6'])
                        ts(kwz[s % 2], kw[:], wz[:, s:s + 1], ALU.mult, ['kw0', 'wz'], ['kwz%d' % (s % 2)])

                    def st3(s):
                        sl_ = s % NSL
                        kz_ = kwz[s % 2]
                        for e2 in range(2):
                            bu = bankM()
                            mm(bu, PS[bu][:, 0:256], [(vm[:, 0, e2 * 128:(e2 + 1) * 128], kz_)], ['vm', 'kwz%d' % (s % 2)])
                            stt(C0[sl_][:, e2, :], C0[sl_][:, e2, :], decbc[:, h, s:s + 1], PS[bu][:, 0:256], ALU.mult, ALU.add,
                                ['C0_%d' % sl_, 'decbc', 'ps%d' % bu], ['C0_%d' % sl_])
                        sdma(o_Cs[s, h].rearrange("(e p) d -> p e d", p=128), C0[sl_], ['C0_%d' % sl_], [], 'C0s_%d' % sl_)
                        if s + NSL < NS:
                            loadC(s + NSL)

                    for k_it in range(NS + 2):
                        if k_it < NS:
                            st1(k_it)
                        if 0 <= k_it - 1 < NS:
                            st2(k_it - 1)
                        if 0 <= k_it - 2 < NS:
                            st3(k_it - 2)
                        yield
                    bn = bankM()
                    mm(bn, PS[bn][0:NS, 0:256], [(wz, kw[:])], ['wz', 'kw0'])
                    stt(nnew[:, h * 256:(h + 1) * 256], n0[:, h * 256:(h + 1) * 256], dectm[:, h:h + 1], PS[bn][0:NS, 0:256],
                        ALU.mult, ALU.add, ['n0', 'dectm', 'ps%d' % bn], ['nnew'])
                    S.op('pe', lambda pe: pe.matmul(PS[6][:, 0:257], sDbL[0][:], vm[:, 0, :], start=False, stop=True), ['sDb0', 'vm'], ['ps6'])
                    finishA(0, 6)
                    yield
                    finishB(0)
                    yield

            def gla_head(h):
                def decay_chain():
                    b = bankG()
                    mm(b, P(b, T), [(wa2[:, h * 128:(h + 1) * 128], agT[0:16, :])], ['wa2', 'agT'])
                    act(t1[:, :], P(b, T), AF.Exp, ['ps%d' % b, 'negba'], ['gt1'], bias=negba[:, h:h + 1], scale=-1.0)
                    yield
                    act(t2[:, :], t1[:, :], AF.Ln, ['gt1'], ['gt2'], bias=1.0)
                    yield
                    for s in range(nseg):
                        sl = slice(s * L, (s + 1) * L)
                        scan(Cs[:, sl], onesf[:, 0:L], t2[:, sl], 0.0, ALU.mult, ALU.add, ['CST', 'gt2'], ['Cs'])
                        if kind == 'P' or s % 4 == 3:
                            yield
                    act(eal[:, 0:nseg], Cs[:, :].rearrange("p (n c) -> p n c", c=L)[:, :, L - 1], AF.Exp, ['Cs'], ['eal'], scale=-1.0 / 16)
                    yield
                    act(t1[:, :], Cs[:, :], AF.Exp, ['Cs'], ['gt1'], scale=-1.0 / 16)
                    yield
                    act(t2[:, :], Cs[:, :], AF.Exp, ['Cs'], ['gt2'], scale=1.0 / 16)
                    yield

                def vr_proj():
                    wv, wk = wload(w_in[:, C_VG + h * 256:C_VG + (h + 1) * 256], 8, 256)
                    for i in range(nt):
                        b = bankG()
                        mm(b, P(b, 256), [(hT[:, k, i * 128:(i + 1) * 128], wv[:, k, :]) for k in range(8)], HK + [wk])
                        cp(vg[:, i, :], P(b, 256), ['ps%d' % b], ['vg'], eng='act')
                        yield
                    wv, wk = wload(w_in[:, C_RG + h * 256:C_RG + (h + 1) * 256], 8, 256)
                    for i in range(nt):
                        b = bankG()
                        mm(b, P(b, 256), [(hT[:, k, i * 128:(i + 1) * 128], wv[:, k, :]) for k in range(8)], HK + [wk])
                        sig3(sgt2[:, :], P(b, 256), ['ps%d' % b], 'sgt2')
                        tt(sgt2[:, :], sgt2[:, :], gbc_g[:, h * 256:(h + 1) * 256], ALU.mult, ['sgt2', 'gbc_g'], ['sgt2'])
                        tt(rg[:, i, :], P(b, 256), sgt2[:, :], ALU.mult, ['ps%d' % b, 'sgt2'], ['rg'])
                        yield

                yield from interleave([decay_chain(), vr_proj()], [1, 1])
                wv, wk = wload(w_in[:, C_QG + h * 128:C_QG + (h + 1) * 128], 8, 128)
                b = bankG()
                mm(b, P(b, T), [(wv[:, k, :], hT[:, k, 0:T]) for k in range(8)], HK + [wk])
                stt(qgT[:, :], P(b, T), 128.0 ** -0.5, t1[:, :], ALU.mult, ALU.mult, ['ps%d' % b, 'gt1'], ['qgT'])
                yield
                wv, wk = wload(w_in[:, C_KG + h * 128:C_KG + (h + 1) * 128], 8, 128)
                b = bankG()
                mm(b, P(b, T), [(wv[:, k, :], hT[:, k, 0:T]) for k in range(8)], HK + [wk])
                tt(kgT[:, :], P(b, T), t2[:, :], ALU.mult, ['ps%d' % b, 'gt2'], ['kgT'])
                yield
                def gfinishA(i, b2):
                    sm_ = smgL[i % 2]; sq_ = 'sg%d_' % (i % 2); og_ = ogbL[i]; ok_ = 'ogb%d' % i
                    act(junkb[:], PS[b2][:, 0:256], AF.Square, ['ps%d' % b2], ['junkb', sq_ + '0'], accum=sm_[:, 0:1])
                    act(sm_[:, 1:2], sm_[:, 0:1], AF.Ln, [sq_ + '0', 'epsc'], [sq_ + '1'], bias=epsc[:], scale=1.0 / 256)
                    act(sm_[:, 2:3], sm_[:, 1:2], AF.Exp, [sq_ + '1'], [sq_ + '2'], scale=-0.5)
                    stt(og_[:], PS[b2][:, 0:256], sm_[:, 2:3], rg[:, i, :], ALU.mult, ALU.mult, ['ps%d' % b2, sq_ + '2', 'rg'], [ok_])

                def gfinishB(i):
                    tsl = slice(i * 128, (i + 1) * 128)
                    og_ = ogbL[i]; ok_ = 'ogb%d' % i
                    bt = bankG()
                    tr(bt, [(PS[bt][:, :].bitcast(BF16)[:, ec * 128:(ec + 1) * 128], og_[:, ec * 128:(ec + 1) * 128]) for ec in range(2)],
                       identb[:], [ok_, 'identb'])
                    cp(ogT[:, 2 * h:2 * h + 2, tsl], PS[bt][:, :].bitcast(BF16)[:, 0:256].rearrange("p (e c) -> p e c", c=128),
                       ['ps%d' % bt], ['ogT%d' % h], eng='act')

                for i in range(nt):
                    tsl = slice(i * 128, (i + 1) * 128)
                    b1 = bankG()
                    mm(b1, PS[b1][:, 0:128], [(kgT[:, tsl], qgT[:, tsl])], ['kgT', 'qgT'])
                    tt(attbL[i][:], PS[b1][:, 0:128], m01, ALU.mult, ['ps%d' % b1, 'CST'], ['attb%d' % i])
                    bk = bankG()
                    tr(bk, [(PS[bk][:, :].bitcast(BF16)[:, 0:128], kgT[:, tsl])], identb[:], ['kgT', 'identb'])
                    cp(kgtokL[i][:], PS[bk][:, :].bitcast(BF16)[:, 0:128], ['ps%d' % bk], ['kgtok%d' % i], eng='act')
                    yield
                if kind == 'P':
                    ssn = [Sb[:, h, :]] + [Ssnap[j] for j in range(nt - 1)]
                    ssk = ['Sb'] + ['Ssnap%d' % j for j in range(nt - 1)]
                    for i in range(nt):
                        bu = bankG()
                        mm(bu, PS[bu][:, 0:256], [(kgtokL[i][:], vg[:, i, :])], ['kgtok%d' % i, 'vg'])
                        tt(stmp[:], S32[:, h, :], PS[bu][:, 0:256], ALU.add, ['S32', 'ps%d' % bu], ['stmp'])
                        ts(S32[:, h, :], stmp[:], eal[:, i:i + 1], ALU.mult, ['stmp', 'eal'], ['S32'])
                        if i < nt - 1:
                            cp(ssn[i + 1], S32[:, h, :], ['S32'], [ssk[i + 1]], eng='act')
                        yield
                    for i in range(nt):
                        tsl = slice(i * 128, (i + 1) * 128)
                        b2 = bankG()
                        mm(b2, PS[b2][:, 0:256], [(attbL[i][:], vg[:, i, :]), (qgT[:, tsl], ssn[i])], ['attb%d' % i, 'vg', 'qgT', ssk[i]])
                        gfinishA(i, b2)
                        yield
                    for i in range(nt):
                        gfinishB(i)
                        yield
                    cp(Sb[:, h, :], S32[:, h, :], ['S32'], ['Sb'], eng='act')
                    yield
                else:
                    attb = attbL[0]; kgtok = kgtokL[0]
                    cp(qgz[:, :].rearrange("p (j c) -> p j c", c=136)[:, :, 0:8], qgT[:, :].rearrange("p (j c) -> p j c", c=8), ['qgT'], ['qgz'])
                    S.op('pe', lambda pe: pe.matmul(PS[7][:, 0:256], attbL[0][:], vg[:, 0, :], start=True, stop=False), ['attb0', 'vg'], ['ps7'])

                    def loadS(s_):
                        k_ = s_ % NSL
                        sdma(S0[k_], sS[s_, h], [], ['S0_%d' % k_], 'S0_%d' % k_)
                    for s in range(NSL):
                        loadS(s)
                    yield

                    def gs1(s):
                        sl_ = s % NSL
                        cp(S0b[sl_], S0[sl_], ['S0_%d' % sl_], ['S0b%d' % sl_], eng='act')
                        ts(kgz[s % 2], kgtok[:], segm[:, s:s + 1], ALU.mult, ['kgtok0', 'CST'], ['kgz%d' % (s % 2)])

                    def gs2(s):
                        sl_ = s % NSL
                        S.op('pe', (lambda s_, sl__: (lambda pe: pe.matmul(PS[7][:, 0:256], qgz[:, s_ * 128:(s_ + 1) * 128], S0b[sl__][:], start=False, stop=(s_ == NS - 1))))(s, sl_),
                             ['qgz', 'S0b%d' % sl_], ['ps7'])
                        bu = bankG()
                        mm(bu, PS[bu][:, 0:256], [(kgz[s % 2], vg[:, 0, :])], ['kgz%d' % (s % 2), 'vg'])
                        tt(stmp[:], S0[sl_], PS[bu][:, 0:256], ALU.add, ['S0_%d' % sl_, 'ps%d' % bu], ['stmp'])
                        ts(S0[sl_], stmp[:], eal[:, s:s + 1], ALU.mult, ['stmp', 'eal'], ['S0_%d' % sl_])
                        sdma(o_Ss[s, h], S0[sl_], ['S0_%d' % sl_], [], 'S0s_%d' % sl_)
                        if s + NSL < NS:
                            loadS(s + NSL)

                    for k_it in range(NS + 1):
                        if k_it < NS:
                            gs1(k_it)
                        if 0 <= k_it - 1 < NS:
                            gs2(k_it - 1)
                        yield
                    gfinishA(0, 7)
                    yield
                    gfinishB(0)
                    yield

            wv, wk = wload(w_in[:, C_AG:C_AG + 16], 8, 16)
            b = bank()
            mm(b, PS[b][0:16, 0:T], [(wv[:, k, 0:16], hT[:, k, 0:T]) for k in range(8)], HK + [wk])
            cp(agT[0:16, :], PS[b][0:16, 0:T], ['ps%d' % b], ['agT'])
            for h in range(4):
                gens = [ml_head(h), gla_head(h)]
                while gens:
                    for g_ in list(gens):
                        try:
                            next(g_)
                        except StopIteration:
                            gens.remove(g_)

            dump('hmT' + kind, hmT[:, :, 0:T], ['hmT%d' % h_ for h_ in range(4)])
            dump('ogT' + kind, ogT[:, :, 0:T], ['ogT%d' % h_ for h_ in range(4)])
            A.off = a_mark
            sgA = A.get('sgA', 4 * T).rearrange("p (f t) -> p f t", t=T)
            yA = A.get('yA', 4 * T).rearrange("p (f t) -> p f t", t=T)
            ytmp = A.get('ytmp', T)
            yT = A.get('yT', 8 * T, BF16).rearrange("p (f t) -> p f t", t=T)
            HM = ['hmT%d' % h for h in range(4)]
            OG = ['ogT%d' % h for h in range(4)]
            for g in range(2):
                for (gcol0, wp, src, sk_, first) in ((C_GA, w_pa, hmT, HM, True), (C_GB, w_pb, ogT, OG, False)):
                    wv, wk = wload(w_in[:, gcol0 + g * 512:gcol0 + (g + 1) * 512], 8, 512)
                    for fc in range(4):
                        b = bank()
                        mm(b, P(b, T), [(wv[:, k, fc * 128:(fc + 1) * 128], hT[:, k, 0:T]) for k in range(8)], HK + [wk])
                        act(sgA[:, fc, :], P(b, T), AF.Sigmoid, ['ps%d' % b], ['sgA'])
                    wv, wk = wload(wp[:, g * 512:(g + 1) * 512], 8, 512)
                    for fc in range(4):
                        b = bank()
                        mm(b, P(b, T), [(wv[:, k, fc * 128:(fc + 1) * 128], src[:, k, 0:T]) for k in range(8)], sk_ + [wk])
                        if first:
                            tt(yA[:, fc, :], P(b, T), sgA[:, fc, :], ALU.mult, ['ps%d' % b, 'sgA'], ['yA'])
                        else:
                            tt(ytmp[:, :], P(b, T), sgA[:, fc, :], ALU.mult, ['ps%d' % b, 'sgA'], ['ytmp'])
                            tt(yT[:, g * 4 + fc, :], ytmp[:, :], yA[:, fc, :], ALU.add, ['ytmp', 'yA'], ['yT'])
            dump('yT' + kind, yT, ['yT'])
            for g in range(2):
                wv, wk = wload(w_o[:, g * 512:(g + 1) * 512], 8, 512)
                for fc in range(4):
                    f = g * 4 + fc
                    b = bank()
                    mm(b, P(b, T), [(wv[:, k, fc * 128:(fc + 1) * 128], yT[:, k, :]) for k in range(8)], ['yT', wk])
                    tt(xT[:, f, 0:T], P(b, T), xT[:, f, 0:T], ALU.add, ['ps%d' % b, 'xT%d' % f], ['xT%d' % f])
                    norm_partial(f, T)

            if kind == 'S':
                sdma(o_ns, nnew[:], ['nnew'], [], 'st_nn')
                for k in range(8):
                    b = bank()
                    tr(b, [(PS[b][0:48, 0:128], uhS2[:, k, :, :].rearrange("p s j -> p (s j)"))], ident, ['uhS2', 'CST'])
                    cp(sc48[:, k * 128:(k + 1) * 128], PS[b][0:48, 0:128], ['ps%d' % b], ['yout'])
                sdma(o_convs, sc48[:], ['yout'], [], 'st_c')
            elif last:
                for k in range(8):
                    b = bank()
                    tr(b, [(PS[b][0:3, 0:128], uhist[:, k, :])], ident, ['uhist', 'CST'])
                    cp(sc48[0:3, k * 128:(k + 1) * 128], PS[b][0:3, 0:128], ['ps%d' % b], ['yout'])
                sdma(o_convp, sc48[0:3, :], ['yout'], [], 'st_c')
                for h in range(4):
                    for dc in range(2):
                        for e2 in range(2):
                            b = bank()
                            tr(b, [(PS[b][:, 0:128], CT32[:, h, dc, e2 * 128:(e2 + 1) * 128])], ident, ['CT32', 'CST'])
                            cp(Cn[0][:, e2, dc * 128:(dc + 1) * 128], PS[b][:, 0:128], ['ps%d' % b], ['Cn0'])
                        b = bank()
                        tr(b, [(PS[b][0:1, 0:128], CT32[:, h, dc, 256:257])], ident, ['CT32', 'CST'])
                        cp(nnew[0:1, h * 256 + dc * 128:h * 256 + (dc + 1) * 128], PS[b][0:1, 0:128], ['ps%d' % b], ['nnew'])
                    sdma(o_Cp[h].rearrange("(e p) d -> p e d", p=128), Cn[0][:], ['Cn0'], [], 'Cn0')
                    sdma(o_Sp[h], S32[:, h, :], ['S32'], [], 'st_sp')
                sdma(o_np.rearrange("(o h) d -> o (h d)", o=1), nnew[0:1, :], ['nnew'], [], 'st_nn')

        blocks = [('S', xs, ys, 128, False)] + [('P', xp[i * 512:(i + 1) * 512, :], yp[i * 512:(i + 1) * 512, :], 512, i == 3) for i in range(4)]
        for bi, (kind, src, dst, T, last) in enumerate(blocks):
            S.curT = T
            A = Arena(T)
            load_x(src, T, prefetched=(bi > 0))
            ffn(w1u, w1d, gc1, 'gc1', T, A)
            dump('x1' + kind, xT[:, :, 0:T], XK)
            A = Arena(T)
            mixer(T, kind, last, A)
            dump('x2' + kind, xT[:, :, 0:T], XK)
            if bi + 1 < len(blocks):
                prefetch_x(blocks[bi + 1][1], blocks[bi + 1][3])
                S.curT = T
            A = Arena(T)
            ffn(w2u, w2d, gc2, 'gc2', T, A, partial_done=True, next_norm=False)
            final_out(dst, T)
        S.finish()

        with nc.Block() as block:
            @block.tensor
            def _(e):
                for f in S.stream['pe']:
                    f(e)

            @block.scalar
            def _(e):
                for f in S.stream['act']:
                    f(e)

            @block.vector
            def _(e):
                for f in S.stream['dve']:
                    f(e)

            @block.gpsimd
            def _(e):
                for f in S.stream['pool']:
                    f(e)

            @block.sync
            def _(e):
                for f in S.stream['sp']:
                    f(e)
    return nc


def make_consts():
    c = np.zeros((128, 1792), np.float32)
    idx = np.arange(128)
    c[:, 0:128] = np.eye(128, dtype=np.float32)
    s, t = idx[:, None], idx[None, :]
    causal = s <= t
    same = (s // DS) == (t // DS)
    c[:, 128:256] = np.where(causal, 0.0, BIG)
    c[:, 256:384] = np.where(causal & same, 0.0, BIG)
    c[:, 384:512] = causal.astype(np.float32)
    c[:, 512:640] = (causal & same).astype(np.float32)
    c[:, 640:656] = ((idx[:, None] // DS) == np.arange(NS)[None, :]).astype(np.float32)
    for k in range(4):
        c[k, 656 + k * 128:656 + (k + 1) * 128] = 1.0
    c[:, 1280:1792] = 1.0
    return c


_NC_CACHE = {}


def kernel(x_prompt, x_sample, state_conv, state_mlstm_C, state_mlstm_n, state_mlstm_m, state_gla_S,
           g_ffn1, w_ffn1_up, w_ffn1_down, g_mix, w_in, conv_w, conv_b, w_mq, w_mk, b_if, g_mhead,
           w_a2, b_a, g_ghead, w_pa, w_pb, w_o, g_ffn2, w_ffn2_up, w_ffn2_down, g_final):
    f = lambda a: np.ascontiguousarray(np.asarray(a, dtype=np.float32))
    shared = {
        "g_ffn1": f(g_ffn1).reshape(D), "w_ffn1_up": f(w_ffn1_up).reshape(D, 2 * DFF), "w_ffn1_down": f(w_ffn1_down).reshape(DFF, D),
        "g_mix": f(g_mix).reshape(D), "w_in": f(w_in).reshape(D, 8216), "conv_w": f(conv_w).reshape(4, D), "conv_b": f(conv_b).reshape(D),
        "w_mq": f(w_mq).reshape(4, 256, 256), "w_mk": f(w_mk).reshape(4, 256, 256), "b_if": f(b_if).reshape(2, 4),
        "g_mhead": f(g_mhead).reshape(D), "w_a2": f(w_a2).reshape(16, 512), "b_a": f(b_a).reshape(512), "g_ghead": f(g_ghead).reshape(D),
        "w_pa": f(w_pa).reshape(D, D), "w_pb": f(w_pb).reshape(D, D), "w_o": f(w_o).reshape(D, D),
        "g_ffn2": f(g_ffn2).reshape(D), "w_ffn2_up": f(w_ffn2_up).reshape(D, 2 * DFF), "w_ffn2_down": f(w_ffn2_down).reshape(DFF, D),
        "g_final": f(g_final).reshape(D), "consts": make_consts(),
    }
    xp_, xs_ = f(x_prompt), f(x_sample)
    sc_, sC_, sn_, sm_, sS_ = f(state_conv)[0], f(state_mlstm_C)[0], f(state_mlstm_n)[0], f(state_mlstm_m)[0], f(state_gla_S)[0]
    in_maps = []
    for c in range(8):
        sl = slice(c * NS, (c + 1) * NS)
        m = dict(shared)
        m.update({
            "xp": xp_[c], "xs": xs_[sl].reshape(128, D), "sconv": sc_[sl].reshape(48, D), "sC": sC_[sl],
            "sn": sn_[sl].reshape(NS, D), "sm": sm_[sl], "sS": sS_[sl],
        })
        in_maps.append({k: np.ascontiguousarray(v) for k, v in m.items()})
    dbgm = bool(_NC_CACHE.get('debug'))
    key = 'nc_dbg' if dbgm else 'nc'
    if key not in _NC_CACHE:
        _NC_CACHE[key] = build_program(debug=dbgm)
    res = run_bass_kernel_spmd(_NC_CACHE[key], in_maps, core_ids=list(range(8)))
    R = res.results
    _NC_CACHE['raw'] = R if dbgm else None
    cat = lambda k: np.stack([np.asarray(r[k]) for r in R], 0)
    y_prompt = cat("yp").reshape(8, SEQ, D)
    y_sample = cat("ys").reshape(128, DS, D)
    conv_p = cat("conv_p").reshape(1, 8, 3, D)
    C_p = cat("C_p").reshape(1, 8, 4, 256, 256)
    n_p = cat("n_p").reshape(1, 8, 4, 256)
    m_p = cat("m_p").reshape(1, 8, 4)
    S_p = cat("S_p").reshape(1, 8, 4, 128, 256)
    conv_s = cat("conv_s").reshape(1, 128, 3, D)
    C_s = cat("C_s").reshape(1, 128, 4, 256, 256)
    n_s = cat("n_s").reshape(1, 128, 4, 256)
    m_s = cat("m_s").reshape(1, 128, 4)
    S_s = cat("S_s").reshape(1, 128, 4, 128, 256)
    outs = (y_prompt, y_sample, conv_p, C_p, n_p, m_p, S_p, conv_s, C_s, n_s, m_s, S_s)
    return tuple(np.ascontiguousarray(o, dtype=np.float32) for o in outs)
```

```python
import numpy as np
from contextlib import ExitStack
import concourse.bass as bass
import concourse.mybir as mybir
from concourse.bass_utils import run_bass_kernel_spmd

F32 = mybir.dt.float32
BF16 = mybir.dt.bfloat16
AF = mybir.ActivationFunctionType
ALU = mybir.AluOpType

D = 1024
SEQ = 2048
NS = 16
DS = 8
DFF = 2816
NFF = DFF // 128
EPS = 1e-6
BIG = 30000.0
TMAX = 512
WSLOT = 4096
NWS = 3
ENG = ['pe', 'act', 'dve', 'pool', 'sp']

C_U, C_V, C_O, C_IF, C_QG, C_KG, C_VG, C_RG, C_AG, C_GA, C_GB = 0, 1024, 2048, 3072, 3080, 3592, 4104, 5128, 6152, 6168, 7192


class Sched:
    def __init__(self, nc, es):
        self.nc, self.es = nc, es
        self.stream = {e: [] for e in ENG}
        self.semobj = {}
        self.cnt = {}
        for e in ['pe', 'act', 'dve', 'pool']:
            self.semobj['E' + e] = es.enter_context(nc.semaphore('sem_' + e))
            self.cnt['E' + e] = 0
        self.waited = {e: {} for e in ENG}
        self.lastw = {}
        self.readers = {}
        self.ranges = {}
        self.rkeys = []
        self.curT = 0

    def register(self, key, lo, hi):
        if key not in self.ranges:
            self.ranges[key] = (lo, hi)
            self.rkeys.append(key)

    def _expand(self, writes):
        out = []
        for k in writes:
            out.append(k)
            if k in self.ranges:
                lo, hi = self.ranges[k]
                for k2 in self.rkeys:
                    if k2 != k:
                        l2, h2 = self.ranges[k2]
                        if l2 < hi and lo < h2:
                            out.append(k2)
        return out

    def _deps(self, e, reads, writes):
        need = {}

        def add(t):
            if t is not None and need.get(t[0], 0) < t[1]:
                need[t[0]] = t[1]
        for b in reads:
            add(self.lastw.get(b))
        for b in writes:
            add(self.lastw.get(b))
            for sk, v in self.readers.get(b, {}).items():
                add((sk, v))
        out = []
        for sk, v in need.items():
            if self.waited[e].get(sk, 0) < v:
                self.waited[e][sk] = v
                out.append((sk, v))
        return out

    def _commit(self, tok, reads, writes):
        for b in writes:
            self.lastw[b] = tok
            self.readers[b] = {}
        for b in reads:
            r = self.readers.setdefault(b, {})
            if r.get(tok[0], 0) < tok[1]:
                r[tok[0]] = tok[1]

    def _map(self, keys):
        sfx = '_%d' % self.curT
        return [(k + sfx) if (k + sfx) in self.ranges else k for k in keys]

    def op(self, e, fn, reads=(), writes=()):
        writes = self._expand(self._map(writes))
        reads = self._map(reads)
        waits = self._deps(e, reads, writes)
        sk = 'E' + e
        self.cnt[sk] += 1
        tok = (sk, self.cnt[sk])
        sem = self.semobj[sk]
        so = self.semobj

        def run(eng):
            for (k, v) in waits:
                eng.wait_ge(so[k], v)
            fn(eng).then_inc(sem, 1)
        self.stream[e].append(run)
        self._commit(tok, reads, writes)

    def dma(self, q, out, in_, reads, writes, skey, **kw):
        writes = self._expand(self._map(writes))
        reads = self._map(reads)
        waits = self._deps(q, reads, writes)
        sk = 'D' + skey
        if sk not in self.semobj:
            self.semobj[sk] = self.es.enter_context(self.nc.semaphore('dsem_' + skey))
            self.cnt[sk] = 0
        self.cnt[sk] += 16
        tok = (sk, self.cnt[sk])
        sem = self.semobj[sk]
        so = self.semobj

        def run(eng):
            for (k, v) in waits:
                eng.wait_ge(so[k], v)
            eng.dma_start(out=out, in_=in_, **kw).then_inc(sem, 16)
        self.stream[q].append(run)
        self._commit(tok, reads, writes)

    def finish(self):
        so, cnt = self.semobj, dict(self.cnt)

        def run(eng):
            for k, v in cnt.items():
                if v > 0:
                    eng.wait_ge(so[k], v)
        self.stream['sp'].append(run)


def build_program(debug=False):
    nc = bass.Bass("TRN2", target_bir_lowering=False)
    es = ExitStack()
    dbg = {}

    def din(name, shape):
        return nc.dram_tensor(name, list(shape), F32, kind="ExternalInput").ap()

    def dout(name, shape):
        return nc.dram_tensor(name, list(shape), F32, kind="ExternalOutput").ap()

    xp = din("xp", [SEQ, D]); xs = din("xs", [128, D])
    sconv = din("sconv", [48, D]); sC = din("sC", [NS, 4, 256, 256]); sn = din("sn", [NS, D])
    sm = din("sm", [NS, 4]); sS = din("sS", [NS, 4, 128, 256])
    g_ffn1 = din("g_ffn1", [D]); w1u = din("w_ffn1_up", [D, 2 * DFF]); w1d = din("w_ffn1_down", [DFF, D])
    g_mix = din("g_mix", [D]); w_in = din("w_in", [D, 8216]); conv_w = din("conv_w", [4, D]); conv_b = din("conv_b", [D])
    w_mq = din("w_mq", [4, 256, 256]); w_mk = din("w_mk", [4, 256, 256]); b_if = din("b_if", [2, 4])
    g_mhead = din("g_mhead", [D]); w_a2 = din("w_a2", [16, 512]); b_a = din("b_a", [512]); g_ghead = din("g_ghead", [D])
    w_pa = din("w_pa", [D, D]); w_pb = din("w_pb", [D, D]); w_o = din("w_o", [D, D])
    g_ffn2 = din("g_ffn2", [D]); w2u = din("w_ffn2_up", [D, 2 * DFF]); w2d = din("w_ffn2_down", [DFF, D])
    g_final = din("g_final", [D]); cst = din("consts", [128, 1792])

    yp = dout("yp", [SEQ, D]); ys = dout("ys", [128, D])
    o_convp = dout("conv_p", [3, D]); o_Cp = dout("C_p", [4, 256, 256]); o_np = dout("n_p", [4, 256])
    o_mp = dout("m_p", [1, 4]); o_Sp = dout("S_p", [4, 128, 256])
    o_convs = dout("conv_s", [48, D]); o_Cs = dout("C_s", [NS, 4, 256, 256]); o_ns = dout("n_s", [NS, D])
    o_ms = dout("m_s", [NS, 4]); o_Ss = dout("S_s", [NS, 4, 128, 256])

    with es:
        S = Sched(nc, es)

        def sb(name, shape, dt=F32):
            return es.enter_context(nc.sbuf_tensor(name, list(shape), dt))

        xT = sb("xT", [128, 8, TMAX])
        hT = sb("hT", [128, 8, TMAX], BF16)
        wsl = [sb("wsl%d" % i, [128, WSLOT], BF16) for i in range(NWS)]
        CST = sb("CST", [128, 1792])
        ident = CST[:, 0:128]; maskbigP = CST[:, 128:256]; maskbigS = CST[:, 256:384]
        mask01P = CST[:, 384:512]; mask01S = CST[:, 512:640]
        onesf = CST[:, 1280:1792]
        segm = CST[:, 640:656]
        SEL = sb("SEL", [4, 4, 128])
        identb = sb("identb", [128, 128], BF16); SELb = sb("SELb", [4, 4, 128], BF16)
        maskbP = sb("maskbP", [128, 128], BF16); maskbS = sb("maskbS", [128, 128], BF16)
        onesb = sb("onesb", [128, 128], BF16)
        gbc_m = sb("gbc_m", [128, D]); gbc_g = sb("gbc_g", [128, D]); gbc_f = sb("gbc_f", [128, D])
        gc1 = sb("gc1", [128, 8]); gcm = sb("gcm", [128, 8]); gc2 = sb("gc2", [128, 8])
        cw = sb("cw", [128, 4, 8]); cb = sb("cb", [128, 8])
        negba = sb("negba", [128, 4]); wa2 = sb("wa2", [16, 512])
        bif = sb("bif", [4, 2]); negbf = sb("negbf", [4, 1])
        epsc = sb("epsc", [128, 1])
        yout = sb("yout", [128, D])
        sqb = [sb("sqb%d" % i, [128, TMAX], BF16) for i in range(2)]
        srt = sb("srt", [128, TMAX]); rstd = sb("rstd", [128, TMAX])
        hmT = sb("hmT", [128, 8, TMAX], BF16); ogT = sb("ogT", [128, 8, TMAX], BF16)
        CT32 = sb("CT32", [128, 4, 2, 257]); CTb = sb("CTb", [128, 4, 2, 257], BF16)
        S32 = sb("S32", [128, 4, 256]); Sb = sb("Sb", [128, 4, 256], BF16)
        uhist = sb("uhist", [128, 8, 3]); uhS = sb("uhS", [128, 8, NS, 3]); uhS2 = sb("uhS2", [128, 8, NS, 3])
        carF = sb("carF", [4, 1]); carM = sb("carM", [4, 1]); carG = sb("carG", [4, 1])
        m0T = sb("m0T", [4, NS])
        colsTM = sb("colsTM", [128, 4, 20]); decbc = sb("decbc", [128, 4, 16]); dectm = sb("dectm", [NS, 4])
        DTsL = [sb("DTs%d" % i, [128, 128]) for i in range(2)]; sDbL = [sb("sDb%d" % i, [128, 128], BF16) for i in range(4)]
        junkb = sb("junkb", [128, 256], BF16)
        smL = [sb("smL%d" % i, [128, 8]) for i in range(2)]; smgL = [sb("smg%d" % i, [128, 8]) for i in range(2)]
        smF = [sb("smF%d" % i, [128, 4]) for i in range(2)]
        hgL = [sb("hg%d" % i, [128, 256], BF16) for i in range(4)]; kwL = [sb("kw%d" % i, [128, 256], BF16) for i in range(4)]
        attbL = [sb("attb%d" % i, [128, 128], BF16) for i in range(4)]; ogbL = [sb("ogb%d" % i, [128, 256], BF16) for i in range(4)]
        kgtokL = [sb("kgtok%d" % i, [128, 128], BF16) for i in range(4)]
        stmp = sb("stmp", [128, 256])
        Cn = [sb("Cn%d" % i, [128, 2, 256]) for i in range(1)]
        nnew = sb("nnew", [NS, D])
        sc48 = yout[0:48, :]; msout = sb("msout", [NS, 4])
        ARENA = 34816
        AR = sb("AR", [128, ARENA], BF16)

        PS = [es.enter_context(nc.psum_tensor("ps%d" % i, [128, 512], F32)) for i in range(8)]
        bankctr = [0]

        def bank():
            b = bankctr[0] % 6
            bankctr[0] += 1
            return b
        bmc = [0]; bgc = [0]

        def bankM():
            b = bmc[0] % 4
            bmc[0] += 1
            return b

        def bankG():
            b = 4 + bgc[0] % 2
            bgc[0] += 1
            return b

        class Arena:
            def __init__(self, T):
                self.off = 0
                self.T = T

            def get(self, name, n, dt=F32):
                name = '%s_%d' % (name, self.T)
                ne = n * (2 if dt == F32 else 1)
                o = self.off
                self.off += ne + (ne % 2)
                assert self.off <= ARENA, (name, self.off)
                ap = AR[:, o:o + ne]
                if dt == F32:
                    ap = ap.bitcast(F32)
                S.register(name, o, o + ne)
                return ap

        def P(b, n=512):
            return PS[b][:, 0:n]

        def mm(b, out, pairs, reads):
            def fn(pe):
                n = len(pairs)
                for i, (l, r) in enumerate(pairs):
                    ins = pe.matmul(out, l, r, start=(i == 0), stop=(i == n - 1))
                return ins
            S.op('pe', fn, reads=reads, writes=['ps%d' % b])

        def tr(b, outs_ins, idn, reads):
            def fn(pe):
                for (o, i) in outs_ins:
                    ins = pe.transpose(o, i, idn)
                return ins
            S.op('pe', fn, reads=reads, writes=['ps%d' % b])

        def act(out, in_, func, reads, writes, bias=None, scale=None, accum=None):
            kw_ = {}
            if bias is not None:
                kw_['bias'] = bias
            if scale is not None:
                kw_['scale'] = scale
            if accum is not None:
                kw_['accum_out'] = accum
            S.op('act', lambda e: e.activation(out=out, in_=in_, func=func, **kw_), reads, writes)

        def tt(out, a, b_, op, reads, writes, eng='dve'):
            S.op(eng, lambda e: e.tensor_tensor(out=out, in0=a, in1=b_, op=op), reads, writes)

        def stt(out, a, sc, b_, op0, op1, reads, writes):
            S.op('dve', lambda e: e.scalar_tensor_tensor(out=out, in0=a, scalar=sc, in1=b_, op0=op0, op1=op1), reads, writes)

        def ts(out, a, s1, op0, reads, writes, s2=None, op1=None, eng='dve'):
            if op1 is None:
                S.op(eng, lambda e: e.tensor_scalar(out=out, in0=a, scalar1=s1, scalar2=None, op0=op0), reads, writes)
            else:
                S.op(eng, lambda e: e.tensor_scalar(out=out, in0=a, scalar1=s1, scalar2=s2, op0=op0, op1=op1), reads, writes)

        def cp(out, in_, reads, writes, eng='dve'):
            if eng == 'act':
                S.op(eng, lambda e: e.activation(out=out, in_=in_, func=AF.Copy), reads, writes)
            else:
                S.op(eng, lambda e: e.tensor_copy(out=out, in_=in_), reads, writes)

        def recip(out, in_, reads, writes):
            S.op('dve', lambda e: e.reciprocal(out=out, in_=in_), reads, writes)

        def sig3(out, in_, reads, key):
            act(out, in_, AF.Exp, reads, [key], scale=-1.0)
            act(out, out, AF.Ln, [key], [key], bias=1.0)
            act(out, out, AF.Exp, [key], [key], scale=-1.0)

        def scan(out, d0, d1, init, op0, op1, reads, writes):
            S.op('dve', lambda e: e.tensor_tensor_scan(out=out, data0=d0, data1=d1, initial=init, op0=op0, op1=op1), reads, writes)

        def mset(ap, v, writes, eng='dve'):
            S.op(eng, lambda e: e.memset(ap, v), [], writes)

        uq = [0]

        def sdma(out, in_, reads, writes, skey, q='sp'):
            if skey in ('c2', 'c3'):
                uq[0] += 1
                skey = 'k%d' % uq[0]
            S.dma(q, out, in_, reads, writes, skey, allow_slow_non_contiguous=True)

        def dump(name, ap, reads):
            if not debug or name in dbg:
                return
            dbg[name] = nc.dram_tensor("dbg_" + name, list(ap.shape), ap.dtype, kind="ExternalOutput").ap()
            S.dma('sp', dbg[name], ap, reads, [], 'dbg', allow_slow_non_contiguous=True)

        wctr = [0]

        def wload(src, K, ncol):
            i = wctr[0] % NWS
            wctr[0] += 1
            view = wsl[i][:, 0:K * ncol].rearrange("p (k c) -> p k c", c=ncol)
            S.dma('pool', view, src.rearrange("(k p) c -> p k c", p=128), [], ['wsl%d' % i], 'w%d' % i)
            return view, 'wsl%d' % i

        sdma(CST[:], cst, [], ['CST'], 'c0')
        for h in range(4):
            pass
        sdma(SEL[:].rearrange("k h m -> k (h m)"), cst[0:4, 656:656 + 512], [], ['SEL'], 'c1')
        cp(identb[:], ident, ['CST'], ['identb'])
        cp(SELb[:], SEL[:], ['SEL'], ['SELb'])
        cp(maskbP[:], maskbigP, ['CST'], ['maskb'])
        cp(maskbS[:], maskbigS, ['CST'], ['maskb'])
        mset(onesb[:], 1.0, ['onesb'])
        mset(epsc[:], EPS, ['epsc'])
        for (t_, g_) in ((gbc_m, g_mhead), (gbc_g, g_ghead), (gbc_f, g_final)):
            sdma(t_[:], g_.rearrange("(o d) -> o d", o=1).broadcast_to([128, D]), [], [t_.name], 'c2')
        for (t_, g_) in ((gc1, g_ffn1), (gcm, g_mix), (gc2, g_ffn2), (cb, conv_b)):
            sdma(t_[:], g_.rearrange("(k p) -> p k", p=128), [], [t_.name], 'c3')
        sdma(cw[:], conv_w.rearrange("j (k p) -> p j k", p=128), [], ['cw'], 'c3')
        sdma(negba[:], b_a.rearrange("(h p) -> p h", p=128), [], ['negba'], 'c3')
        ts(negba[:], negba[:], -1.0, ALU.mult, ['negba'], ['negba'])
        sdma(wa2[:], w_a2, [], ['wa2'], 'c3')
        sdma(bif[:], b_if.rearrange("t h -> h t"), [], ['bif'], 'c3')
        ts(negbf[:], bif[:, 1:2], -1.0, ALU.mult, ['bif'], ['negbf'])
        mset(uhist[:], 0.0, ['uhist'])
        mset(carF[:], 0.0, ['carF']); mset(carM[:], 0.0, ['carM']); mset(carG[:], 0.0, ['carG'])
        mset(CT32[:], 0.0, ['CT32']); mset(CTb[:], 0.0, ['CTb'])
        mset(S32[:], 0.0, ['S32']); mset(Sb[:], 0.0, ['Sb'])

        XIN_OFF = 12288

        def xin_bufs(T):
            A_ = Arena(T)
            A_.off = XIN_OFF
            return [A_.get('xinA%d' % i, D) for i in range(T // 128)]

        def prefetch_x(src, T):
            S.curT = T
            bufs = xin_bufs(T)
            for i in range(T // 128):
                sdma(bufs[i], src[i * 128:(i + 1) * 128, :], [], ['xinA%d' % i], 'xinA%d' % i)

        def load_x(src, T, prefetched):
            nt = T // 128
            bufs = xin_bufs(T)
            if not prefetched:
                for i in range(nt):
                    sdma(bufs[i], src[i * 128:(i + 1) * 128, :], [], ['xinA%d' % i], 'xinA%d' % i)
            for i in range(nt):
                for half in range(2):
                    b = bank()
                    tr(b, [(PS[b][:, kk * 128:(kk + 1) * 128], bufs[i][:, (half * 4 + kk) * 128:(half * 4 + kk + 1) * 128]) for kk in range(4)],
                       ident, ['xinA%d' % i, 'CST'])
                    cp(xT[:, half * 4:half * 4 + 4, i * 128:(i + 1) * 128],
                       PS[b][:, :].rearrange("p (k c) -> p k c", c=128), ['ps%d' % b], ['xT%d' % (half * 4 + k) for k in range(4)],
                       eng='act' if half else 'dve')

        XK = ['xT%d' % k for k in range(8)]
        HK = ['hT%d' % k for k in range(8)]

        ncnt = [0]

        def norm_partial(k, T):
            n = ncnt[0] % 8
            ncnt[0] += 1
            q = sqb[n % 2]
            act(q[:, 0:T], xT[:, k, 0:T], AF.Square, ['xT%d' % k], ['sqb%d' % (n % 2)])
            S.op('pe', (lambda nn, qq, TT: (lambda pe: pe.matmul(PS[7][:, 0:TT], onesb[:], qq[:, 0:TT], start=(nn == 0), stop=(nn == 7))))(n, q, T),
                 ['sqb%d' % (n % 2), 'onesb'], ['ps7'])

        def norm(gc, gname, T, partial_done=False):
            b = 7
            if not partial_done:
                for k in range(8):
                    norm_partial(k, T)
            act(srt[:, 0:T], P(b, T), AF.Ln, ['ps%d' % b, 'epsc'], ['srt'], bias=epsc[:], scale=1.0 / D)
            act(rstd[:, 0:T], srt[:, 0:T], AF.Exp, ['srt'], ['rstd'], scale=-0.5)
            for k in range(8):
                stt(hT[:, k, 0:T], xT[:, k, 0:T], gc[:, k:k + 1], rstd[:, 0:T], ALU.mult, ALU.mult,
                    ['xT%d' % k, 'rstd', gname], ['hT%d' % k])

        def ffn(wu, wd, gc, gname, T, A, partial_done=False, next_norm=True):
            norm(gc, gname, T, partial_done)
            actb = A.get('act', NFF * T, BF16).rearrange("p (j t) -> p j t", t=T)
            for part in (1, 0):
                c0 = part * DFF
                for (uo, nc_) in [(i * 512, 512) for i in range(5)] + [(2560, 256)]:
                    wv, wk = wload(wu[:, c0 + uo:c0 + uo + nc_], 8, nc_)
                    for jj in range(nc_ // 128):
                        j = uo // 128 + jj
                        b = bank()
                        mm(b, P(b, T), [(wv[:, k, jj * 128:(jj + 1) * 128], hT[:, k, 0:T]) for k in range(8)], HK + [wk])
                        if part == 1:
                            act(actb[:, j, :], P(b, T), AF.Silu, ['ps%d' % b], ['act'])
                        else:
                            tt(actb[:, j, :], P(b, T), actb[:, j, :], ALU.mult, ['ps%d' % b, 'act'], ['act'])
            for f in range(8):
                wv, wk = wload(wd[:, f * 128:(f + 1) * 128], NFF, 128)
                b = bank()
                mm(b, P(b, T), [(wv[:, k, :], actb[:, k, :]) for k in range(NFF)], ['act', wk])
                stt(xT[:, f, 0:T], P(b, T), 0.5, xT[:, f, 0:T], ALU.mult, ALU.add, ['ps%d' % b, 'xT%d' % f], ['xT%d' % f])
                if next_norm:
                    norm_partial(f, T)

        def final_out(dst, T):
            nt = T // 128
            A_ = Arena(T)
            xtk = [A_.get('xtokA%d' % i, D) for i in range(nt)]
            yo = [A_.get('youtA%d' % i, D) for i in range(2)]
            for i in range(nt):
                for half in range(2):
                    b = bank()
                    tr(b, [(PS[b][:, kk * 128:(kk + 1) * 128], xT[:, half * 4 + kk, i * 128:(i + 1) * 128]) for kk in range(4)],
                       ident, XK + ['CST'])
                    cp(xtk[i][:, half * 512:(half + 1) * 512], PS[b][:, :], ['ps%d' % b], ['xtokA%d' % i], eng='act' if half else 'dve')
            for i in range(nt):
                y_ = yo[i % 2]; yk = 'youtA%d' % (i % 2); sm_ = smF[i % 2]; sk = 'smF%d_' % (i % 2)
                act(y_, xtk[i], AF.Square, ['xtokA%d' % i], [yk, sk + '0'], accum=sm_[:, 0:1])
                act(sm_[:, 1:2], sm_[:, 0:1], AF.Ln, [sk + '0', 'epsc'], [sk + '1'], bias=epsc[:], scale=1.0 / D)
                act(sm_[:, 2:3], sm_[:, 1:2], AF.Exp, [sk + '1'], [sk + '2'], scale=-0.5)
                stt(y_, xtk[i], sm_[:, 2:3], gbc_f[:], ALU.mult, ALU.mult, ['xtokA%d' % i, sk + '2', 'gbc_f'], [yk])
                sdma(dst[i * 128:(i + 1) * 128, :], y_, [yk], [], 'youtA%d' % (i % 2))

        def mixer(T, kind, last, A):
            nt = T // 128
            L = 128 if kind == 'P' else DS
            nseg = T // L
            mbigb = maskbP[:] if kind == 'P' else maskbS[:]
            m01 = mask01P if kind == 'P' else mask01S
            norm(gcm, 'gcm', T, True)
            G = {n: A.get('g_' + n, T) for n in ('gi', 'lf', 'F', 'm', 'G', 'a', 'inter', 'w', 'em', 'em2', 't1', 't2')}

            def g4(n):
                return G[n][0:4, :]

            def g3(n):
                return G[n][0:4, :].rearrange("p (n c) -> p n c", c=L)
            wv, wk = wload(w_in[:, C_IF:C_IF + 8], 8, 8)
            bi = bank(); bf_ = bank()
            mm(bi, PS[bi][0:4, 0:T], [(wv[:, k, 0:4], hT[:, k, 0:T]) for k in range(8)], HK + [wk])
            mm(bf_, PS[bf_][0:4, 0:T], [(wv[:, k, 4:8], hT[:, k, 0:T]) for k in range(8)], HK + [wk])
            act(g4('gi'), PS[bi][0:4, 0:T], AF.Identity, ['ps%d' % bi, 'bif'], ['g_gi'], bias=bif[:, 0:1])
            act(g4('t1'), PS[bf_][0:4, 0:T], AF.Exp, ['ps%d' % bf_, 'negbf'], ['g_t1'], bias=negbf[:], scale=-1.0)
            act(g4('t2'), g4('t1'), AF.Ln, ['g_t1'], ['g_t2'], bias=1.0)
            ts(g4('lf'), g4('t2'), -1.0, ALU.mult, ['g_t2'], ['g_lf'])
            gst = A.get('g_gst', 16); gl = A.get('g_gl', 16); dec = A.get('g_dec', 16)
            if kind == 'S':
                qz = A.get('qz', 2 * 2176, BF16).rearrange("p (e c) -> p e c", c=2176)
                qgz = A.get('qgz', 2176, BF16)
                C0 = [A.get('C0_%d' % i, 512).rearrange("p (e d) -> p e d", d=256) for i in range(4)]
                C0Tb = [A.get('C0Tb%d' % i, 2 * 258, BF16).rearrange("p (e d) -> p e d", d=258)[:, :, 0:257] for i in range(4)]
                S0 = [A.get('S0_%d' % i, 256) for i in range(4)]
                S0b = [A.get('S0b%d' % i, 256, BF16) for i in range(4)]
                Cout = [A.get('Cout%d' % i, 512).rearrange("p (e d) -> p e d", d=256) for i in range(2)]
                Sout = [A.get('Sout%d' % i, 256) for i in range(2)]
                n0 = A.get('n0', D)[0:NS, :]
                n0T = A.get('n0T', 8 * NS).rearrange("p (k s) -> p k s", s=NS)
                wz = A.get('wz', NS, BF16)
                kwz = [A.get('kwz%d' % i, 256, BF16) for i in range(2)]
                kgz = [A.get('kgz%d' % i, 128, BF16) for i in range(2)]
                mset(qz, 0.0, ['qz'], eng='pool')
                mset(qgz, 0.0, ['qgz'], eng='pool')
            if kind == 'P':
                scan(g4('F'), onesf[0:4, 0:T], g4('lf'), carF[:], ALU.mult, ALU.add, ['CST', 'g_lf', 'carF'], ['g_F'])
                scan(g4('m'), g4('lf'), g4('gi'), carM[:], ALU.add, ALU.max, ['g_lf', 'g_gi', 'carM'], ['g_m'])
            else:
                sdma(m0T[:], sm.rearrange("s h -> h s"), [], ['m0T'], 'st_m')
                for s in range(NS):
                    sl = slice(s * DS, (s + 1) * DS)
                    scan(G['F'][0:4, sl], onesf[0:4, 0:DS], G['lf'][0:4, sl], 0.0, ALU.mult, ALU.add, ['CST', 'g_lf'], ['g_F'])
                    scan(G['m'][0:4, sl], G['lf'][0:4, sl], G['gi'][0:4, sl], m0T[:, s:s + 1], ALU.add, ALU.max,
                         ['g_lf', 'g_gi', 'm0T'], ['g_m'])
            tt(g4('G'), g4('m'), g4('F'), ALU.subtract, ['g_m', 'g_F'], ['g_G'])
            tt(g4('a'), g4('gi'), g4('F'), ALU.subtract, ['g_gi', 'g_F'], ['g_a'])
            Ghi = A.get('g_Ghi', T, BF16); Gmid = A.get('g_Gmid', T, BF16); Glo = A.get('g_Glo', T, BF16)
            act(g4('em'), g4('m'), AF.Exp, ['g_m'], ['g_em'], scale=-1.0)
            act(g4('em2'), g4('m'), AF.Exp, ['g_m'], ['g_em2'], scale=-2.0)
            cp(gl[0:4, 0:nseg], g3('G')[:, :, L - 1], ['g_G'], ['g_gl'])
            if kind == 'P':
                cp(gst[0:4, 0:1], carG[:], ['carG'], ['g_gst'])
                if nseg > 1:
                    cp(gst[0:4, 1:nseg], gl[0:4, 0:nseg - 1], ['g_gl', 'g_gst'], ['g_gst'])
            else:
                cp(gst[0:4, 0:nseg], m0T[:], ['m0T'], ['g_gst'])
            tt(g3('t1'), gst[0:4, 0:nseg].unsqueeze(2).broadcast_to([4, nseg, L]), g3('G'), ALU.subtract, ['g_gst', 'g_G'], ['g_t1'])
            act(g4('inter'), g4('t1'), AF.Exp, ['g_t1'], ['g_inter'])
            tt(g3('t2'), g3('a'), gl[0:4, 0:nseg].unsqueeze(2).broadcast_to([4, nseg, L]), ALU.subtract, ['g_a', 'g_gl'], ['g_t2'])
            act(g4('w'), g4('t2'), AF.Exp, ['g_t2'], ['g_w'])
            tt(dec[0:4, 0:nseg], gst[0:4, 0:nseg], gl[0:4, 0:nseg], ALU.subtract, ['g_gst', 'g_gl'], ['g_dec'])
            act(dec[0:4, 0:nseg], dec[0:4, 0:nseg], AF.Exp, ['g_dec'], ['g_dec'])
            cp(Ghi[0:4, :], g4('G'), ['g_G'], ['g_Ghi'])
            tt(g4('t1'), g4('G'), Ghi[0:4, :], ALU.subtract, ['g_G', 'g_Ghi', 'g_inter'], ['g_t1'])
            cp(Gmid[0:4, :], g4('t1'), ['g_t1'], ['g_Gmid'])
            tt(g4('t2'), g4('t1'), Gmid[0:4, :], ALU.subtract, ['g_t1', 'g_Gmid', 'g_w'], ['g_t2'])
            cp(Glo[0:4, :], g4('t2'), ['g_t2'], ['g_Glo'])
            if kind == 'P':
                cp(carF[:], G['F'][0:4, T - 1:T], ['g_F', 'carF'], ['carF'])
                cp(carM[:], G['m'][0:4, T - 1:T], ['g_m', 'carM'], ['carM'])
                cp(carG[:], G['G'][0:4, T - 1:T], ['g_G', 'carG'], ['carG'])
            b = bank()
            pairs = []
            for i in range(nt):
                for qi, qn in enumerate(('a', 'inter', 'w', 'em', 'em2')):
                    pairs.append((PS[b][:, i * 20 + qi * 4:i * 20 + qi * 4 + 4], G[qn][0:4, i * 128:(i + 1) * 128]))

            def fnc(pe, pairs=pairs):
                for (o, l) in pairs:
                    ins = pe.matmul(o, l, ident[0:4, 0:4], start=True, stop=True)
                return ins
            S.op('pe', fnc, ['g_a', 'g_inter', 'g_w', 'g_em', 'g_em2', 'CST'], ['ps%d' % b])
            cp(colsTM[:, 0:nt, :], PS[b][:, 0:nt * 20].rearrange("p (i c) -> p i c", c=20), ['ps%d' % b], ['colsTM'])
            b = bank()

            def fnd(pe, b=b, nseg=nseg, dec=dec):
                for h in range(4):
                    ins = pe.matmul(PS[b][:, h * 16:h * 16 + nseg], SEL[:, h, :], dec[0:4, 0:nseg], start=True, stop=True)
                return ins
            S.op('pe', fnd, ['g_dec', 'SEL'], ['ps%d' % b])
            cp(decbc[:, :, 0:nseg], PS[b][:, 0:64].rearrange("p (h c) -> p h c", c=16)[:, :, 0:nseg], ['ps%d' % b], ['decbc'])
            dump('h' + kind, hT[:, :, 0:T], HK)
            for nme in ('gi', 'lf', 'F', 'm', 'G', 'a', 'inter', 'w', 'em'):
                dump(nme + kind, g4(nme), ['g_' + nme])
            dump('colsTM' + kind, colsTM[:, 0:nt, :], ['colsTM'])
            dump('decbc' + kind, decbc[:, :, 0:nseg], ['decbc'])
            if kind == 'S':
                b = bank()
                mm(b, PS[b][0:NS, 0:4], [(dec[0:4, 0:NS], ident[0:4, 0:4])], ['g_dec', 'CST'])
                cp(dectm[:], PS[b][0:NS, 0:4], ['ps%d' % b], ['dectm'])
                b = bank()
                mm(b, PS[b][0:NS, 0:4], [(g3('m')[:, :, DS - 1], ident[0:4, 0:4])], ['g_m', 'CST'])
                cp(msout[:], PS[b][0:NS, 0:4], ['ps%d' % b], ['msout'])
                sdma(o_ms, msout[:], ['msout'], [], 'st_ms')
                sdma(sc48[:], sconv, [], ['yout'], 'st_c')
                for k in range(8):
                    b = bank()
                    tr(b, [(PS[b][:, 0:48], sc48[:, k * 128:(k + 1) * 128])], ident[0:48, 0:48], ['yout', 'CST'])
                    cp(uhS[:, k, :, :], PS[b][:, 0:48].rearrange("p (s j) -> p s j", j=3), ['ps%d' % b], ['uhS'])
                sdma(n0, sn, [], ['n0'], 'st_n')
                for k in range(8):
                    b = bank()
                    tr(b, [(PS[b][:, 0:NS], n0[:, k * 128:(k + 1) * 128])], ident[0:NS, 0:NS], ['n0', 'CST'])
                    cp(n0T[:, k, :], PS[b][:, 0:NS], ['ps%d' % b], ['n0T'])
            elif last:
                b = bank()
                mm(b, PS[b][0:1, 0:4], [(G['m'][0:4, T - 1:T], ident[0:4, 0:4])], ['g_m', 'CST'])
                cp(msout[0:1, :], PS[b][0:1, 0:4], ['ps%d' % b], ['msout'])
                sdma(o_mp, msout[0:1, :], ['msout'], [], 'st_ms')

            a_mark = A.off
            if kind == 'P':
                u = A.get('u', 2 * (T + 3)).rearrange("p (e t) -> p e t", t=T + 3)
            else:
                u4 = A.get('u', 2 * NS * 11).rearrange("p (e s t) -> p e s t", s=NS, t=11)
            cbuf = A.get('cbuf', T); ctmp = A.get('ctmp', T)
            ch = A.get('ch', 2 * T, BF16).rearrange("p (e t) -> p e t", t=T)
            qT = A.get('qT', 2 * T, BF16).rearrange("p (e t) -> p e t", t=T)
            kT = A.get('kT', 2 * T, BF16).rearrange("p (e t) -> p e t", t=T)
            qiT = A.get('qiT', 2 * T, BF16).rearrange("p (e t) -> p e t", t=T)
            vm = A.get('vm', nt * 257, BF16).rearrange("p (i c) -> p i c", c=257)
            ogm = A.get('ogm', nt * 256, BF16).rearrange("p (i c) -> p i c", c=256)
            sgt = A.get('sgt', 256)
            agT = A.get('agT', T)
            t1 = A.get('gt1', T); t2 = A.get('gt2', T); Cs = A.get('Cs', T); eal = A.get('eal', 16)
            qgT = A.get('qgT', T, BF16); kgT = A.get('kgT', T, BF16)
            vg = A.get('vg', nt * 256, BF16).rearrange("p (i c) -> p i c", c=256)
            rg = A.get('rg', nt * 256, BF16).rearrange("p (i c) -> p i c", c=256)
            sgt2 = A.get('sgt2', 256)
            if kind == 'P':
                CTs_ = [A.get('CTsnap%d' % j, 2 * 258, BF16).rearrange("p (e c) -> p e c", c=258) for j in range(nt - 1)]
                Ssnap = [A.get('Ssnap%d' % j, 256, BF16) for j in range(nt - 1)]
            NSL = 4

            def interleave(gens, ratios):
                gens = list(gens); ratios = list(ratios)
                while gens:
                    for gi in range(len(gens) - 1, -1, -1):
                        pass
                    alive = []
                    for g_, r_ in zip(gens, ratios):
                        ok = True
                        for _ in range(r_):
                            try:
                                next(g_)
                            except StopIteration:
                                ok = False
                                break
                        if ok:
                            alive.append((g_, r_))
                    gens = [a for a, _ in alive]; ratios = [b_ for _, b_ in alive]
                    yield

            def ml_head(h):
                wvu, wku = wload(w_in[:, C_U + h * 256:C_U + (h + 1) * 256], 8, 256)
                wins = []
                for ec in range(2):
                    c = 2 * h + ec
                    b = bankM()
                    mm(b, P(b, T), [(wvu[:, k, ec * 128:(ec + 1) * 128], hT[:, k, 0:T]) for k in range(8)], HK + [wku])
                    if kind == 'P':
                        cp(u[:, ec, 0:3], uhist[:, c, :], ['uhist'], ['u'])
                        cp(u[:, ec, 3:3 + T], P(b, T), ['ps%d' % b], ['u'], eng='act')
                        cp(uhist[:, c, :], u[:, ec, T:T + 3], ['u'], ['uhist'])
                        wins.append(([u[:, ec, 3 - j:3 - j + T] for j in range(4)], cbuf[:, 0:T]))
                    else:
                        cp(u4[:, ec, :, 0:3], uhS[:, c, :, :], ['uhS'], ['u'])
                        cp(u4[:, ec, :, 3:11], P(b, T).rearrange("p (s t) -> p s t", t=DS), ['ps%d' % b], ['u'], eng='act')
                        cp(uhS2[:, c, :, :], u4[:, ec, :, 8:11], ['u'], ['uhS2'])
                        wins.append(([u4[:, ec, :, 3 - j:11 - j] for j in range(4)], cbuf[:, 0:T].rearrange("p (s t) -> p s t", t=DS)))
                    yield

                def conv_chain():
                    for ec in range(2):
                        c = 2 * h + ec
                        win, cv = wins[ec]
                        act(cv, win[0], AF.Identity, ['u', 'cw', 'cb'], ['cbuf'], bias=cb[:, c:c + 1], scale=cw[:, 3, c:c + 1])
                        yield
                        for j in (1, 2, 3):
                            stt(cv, win[j], cw[:, 3 - j, c:c + 1], cv, ALU.mult, ALU.add, ['u', 'cw', 'cbuf'], ['cbuf'])
                            yield
                        act(ctmp[:, 0:T], cbuf[:, 0:T], AF.Exp, ['cbuf'], ['ctmp'], scale=-1.0)
                        yield
                        act(ctmp[:, 0:T], ctmp[:, 0:T], AF.Ln, ['ctmp'], ['ctmp'], bias=1.0)
                        yield
                        act(ctmp[:, 0:T], ctmp[:, 0:T], AF.Exp, ['ctmp'], ['ctmp'], scale=-1.0)
                        yield
                        tt(ch[:, ec, :], cbuf[:, 0:T], ctmp[:, 0:T], ALU.mult, ['cbuf', 'ctmp'], ['ch'])
                        yield

                def vo_proj():
                    wv, wk = wload(w_in[:, C_V + h * 256:C_V + (h + 1) * 256], 8, 256)
                    mset(vm[:, :, 256:257], 1.0, ['vm'])
                    for i in range(nt):
                        b = bankM()
                        mm(b, P(b, 256), [(hT[:, k, i * 128:(i + 1) * 128], wv[:, k, :]) for k in range(8)], HK + [wk])
                        cp(vm[:, i, 0:256], P(b, 256), ['ps%d' % b], ['vm'], eng='act')
                        yield
                    wv, wk = wload(w_in[:, C_O + h * 256:C_O + (h + 1) * 256], 8, 256)
                    for i in range(nt):
                        b = bankM()
                        mm(b, P(b, 256), [(hT[:, k, i * 128:(i + 1) * 128], wv[:, k, :]) for k in range(8)], HK + [wk])
                        sig3(sgt[:, :], P(b, 256), ['ps%d' % b], 'sgt')
                        tt(ogm[:, i, :], sgt[:, :], gbc_m[:, h * 256:(h + 1) * 256], ALU.mult, ['sgt', 'gbc_m'], ['ogm'])
                        yield

                yield from interleave([conv_chain(), vo_proj()], [2, 1])
                for (wsrc, dst, dk_, scl) in ((w_mq, qT, 'qT', 1.0), (w_mk, kT, 'kT', 1.0 / 16.0)):
                    wv, wk = wload(wsrc[h], 2, 256)
                    for ec in range(2):
                        b = bankM()
                        mm(b, P(b, T), [(wv[:, kc, ec * 128:(ec + 1) * 128], ch[:, kc, :]) for kc in range(2)], ['ch', wk])
                        act(dst[:, ec, :], P(b, T), AF.Copy, ['ps%d' % b], [dk_], scale=scl)
                    yield
                b = bankM()
                mm(b, P(b, T), [(SEL[:, h, :], G['inter'][0:4, 0:T])], ['SEL', 'g_inter'])
                for ec in range(2):
                    tt(qiT[:, ec, :], P(b, T), qT[:, ec, :], ALU.mult, ['ps%d' % b, 'qT'], ['qiT'])
                yield
                if h == 0:
                    if kind == 'P':
                        dump('u' + kind, u, ['u'])
                    dump('ch' + kind, ch, ['ch']); dump('qT' + kind, qT, ['qT']); dump('kT' + kind, kT, ['kT'])
                    dump('vm' + kind, vm, ['vm']); dump('ogm' + kind, ogm, ['ogm'])
                def finishA(i, bP):
                    pk = 'ps%d' % bP
                    sm_ = smL[i % 2]; sq_ = 'sm%d_' % (i % 2); hg_ = hgL[i]; hk_ = 'hg%d' % i
                    act(sm_[:, 0:1], PS[bP][:, 256:257], AF.Square, [pk], [sq_ + '0'])
                    act(junkb[:], PS[bP][:, 0:256], AF.Square, [pk], ['junkb', sq_ + '2'], accum=sm_[:, 2:3])
                    ts(sm_[:, 1:2], sm_[:, 0:1], colsTM[:, i, 16 + h:17 + h], ALU.max, [sq_ + '0', 'colsTM'], [sq_ + '1'], s2=EPS, op1=ALU.mult)
                    act(sm_[:, 3:4], sm_[:, 2:3], AF.Ln, [sq_ + '2', sq_ + '1'], [sq_ + '3'], bias=sm_[:, 1:2], scale=1.0 / 256)
                    act(sm_[:, 4:5], sm_[:, 3:4], AF.Exp, [sq_ + '3'], [sq_ + '4'], scale=-0.5)
                    stt(hg_[:], PS[bP][:, 0:256], sm_[:, 4:5], ogm[:, i, :], ALU.mult, ALU.mult, [pk, sq_ + '4', 'ogm'], [hk_])
                    if h == 0 and i == 0:
                        dump('DTs' + kind, DTsL[0][:], ['DTs0']); dump('sDb' + kind, sDbL[0][:], ['sDb0'])
                        dump('hg' + kind, hg_[:], [hk_])

                def finishB(i):
                    tsl = slice(i * 128, (i + 1) * 128)
                    hg_ = hgL[i]; hk_ = 'hg%d' % i
                    bt = bankM()
                    tr(bt, [(PS[bt][:, :].bitcast(BF16)[:, ec * 128:(ec + 1) * 128], hg_[:, ec * 128:(ec + 1) * 128]) for ec in range(2)],
                       identb[:], [hk_, 'identb'])
                    cp(hmT[:, 2 * h:2 * h + 2, tsl], PS[bt][:, :].bitcast(BF16)[:, 0:256].rearrange("p (e c) -> p e c", c=128),
                       ['ps%d' % bt], ['hmT%d' % h], eng='act')

                for i in range(nt):
                    tsl = slice(i * 128, (i + 1) * 128)
                    b1 = bankM()
                    mm(b1, PS[b1][:, 0:128], [(kT[:, ec, tsl], qT[:, ec, tsl]) for ec in range(2)], ['kT', 'qT'])
                    b2 = bankM()
                    mm(b2, PS[b2][:, 0:128], [(SELb[:, h, :], Ghi[0:4, tsl]), (SELb[:, h, :], Gmid[0:4, tsl]), (SELb[:, h, :], Glo[0:4, tsl]),
                                              (identb[:], mbigb)], ['SELb', 'g_Ghi', 'g_Gmid', 'g_Glo', 'identb', 'maskb'])
                    D_ = DTsL[i % 2]; dk_ = 'DTs%d' % (i % 2)
                    act(D_[:], PS[b2][:, 0:128], AF.Exp, ['ps%d' % b2, 'colsTM'], [dk_], bias=colsTM[:, i, h:h + 1], scale=-1.0)
                    tt(sDbL[i][:], PS[b1][:, 0:128], D_[:], ALU.mult, ['ps%d' % b1, dk_], ['sDb%d' % i])
                    yield
                if kind == 'P':
                    snaps = [CTb[:, h, :, :]] + [CTs_[j][:, :, 0:257] for j in range(nt - 1)]
                    snapk = ['CTb'] + ['CTsnap%d' % j for j in range(nt - 1)]
                    for i in range(nt):
                        tsl = slice(i * 128, (i + 1) * 128)
                        bk = bankM()
                        tr(bk, [(PS[bk][:, :].bitcast(BF16)[:, ec * 128:(ec + 1) * 128], kT[:, ec, tsl]) for ec in range(2)], identb[:], ['kT', 'identb'])
                        ts(kwL[i][:], PS[bk][:, :].bitcast(BF16)[:, 0:256], colsTM[:, i, 8 + h:9 + h], ALU.mult, ['ps%d' % bk, 'colsTM'], ['kw%d' % i])
                        yield
                    for i in range(nt):
                        kw_ = kwL[i]; kk_ = 'kw%d' % i
                        for dc in range(2):
                            bu = bankM()
                            mm(bu, PS[bu][:, 0:257], [(kw_[:, dc * 128:(dc + 1) * 128], vm[:, i, :])], [kk_, 'vm'])
                            stt(CT32[:, h, dc, :], CT32[:, h, dc, :], decbc[:, h, i:i + 1], PS[bu][:, 0:257], ALU.mult, ALU.add,
                                ['CT32', 'decbc', 'ps%d' % bu], ['CT32'])
                        if i < nt - 1:
                            cp(snaps[i + 1], CT32[:, h, :, :], ['CT32'], [snapk[i + 1]], eng='act')
                        if h == 0 and i == 0:
                            dump('CT0' + kind, CT32[:, 0, :, :], ['CT32']); dump('kw' + kind, kw_[:], [kk_])
                        yield
                    for i in range(nt):
                        tsl = slice(i * 128, (i + 1) * 128)
                        bP = bankM()
                        mm(bP, PS[bP][:, 0:257], [(sDbL[i][:], vm[:, i, :])] + [(qiT[:, ec, tsl], snaps[i][:, ec, :]) for ec in range(2)],
                           ['sDb%d' % i, 'vm', 'qiT', snapk[i]])
                        finishA(i, bP)
                        yield
                    for i in range(nt):
                        finishB(i)
                        yield
                    cp(CTb[:, h, :, :], CT32[:, h, :, :], ['CT32'], ['CTb'], eng='act')
                    yield
                else:
                    i = 0
                    tsl = slice(0, 128)
                    kw = kwL[0]
                    sDb = sDbL[0]
                    bP = 6
                    for ec in range(2):
                        cp(qz[:, ec, :].rearrange("p (j c) -> p j c", c=136)[:, :, 0:8],
                           qiT[:, ec, :].rearrange("p (j c) -> p j c", c=8), ['qiT'], ['qz'])
                    bk = bankM()
                    tr(bk, [(PS[bk][:, :].bitcast(BF16)[:, ec * 128:(ec + 1) * 128], kT[:, ec, tsl]) for ec in range(2)], identb[:], ['kT', 'identb'])
                    cp(kw[:], PS[bk][:, :].bitcast(BF16)[:, 0:256], ['ps%d' % bk], ['kw0'])
                    ts(wz, segm, colsTM[:, 0, 8 + h:9 + h], ALU.mult, ['CST', 'colsTM'], ['wz'])

                    def loadC(s_):
                        k_ = s_ % NSL
                        sdma(C0[k_], sC[s_, h].rearrange("(e p) d -> p e d", p=128), [], ['C0_%d' % k_], 'C0_%d' % k_)
                    for s in range(NSL):
                        loadC(s)
                    yield

                    def st1(s):
                        sl_ = s % NSL
                        for dc in range(2):
                            bt = bankM()
                            tr(bt, [(PS[bt][:, e2 * 128:(e2 + 1) * 128], C0[sl_][:, e2, dc * 128:(dc + 1) * 128]) for e2 in range(2)],
                               ident, ['C0_%d' % sl_, 'CST'])
                            cp(C0Tb[sl_][:, dc, 0:256], PS[bt][:, 0:256], ['ps%d' % bt], ['C0Tb%d' % sl_], eng='act')
                            cp(C0Tb[sl_][:, dc, 256:257], n0T[:, 2 * h + dc, s:s + 1], ['n0T'], ['C0Tb%d' % sl_])

                    def st2(s):
                        sl_ = s % NSL
                        S.op('pe', (lambda s_, sl__: (lambda pe: [pe.matmul(PS[6][:, 0:257], qz[:, 0, s_ * 128:(s_ + 1) * 128], C0Tb[sl__][:, 0, :], start=(s_ == 0), stop=False),
                                                                 pe.matmul(PS[6][:, 0:257], qz[:, 1, s_ * 128:(s_ + 1) * 128], C0Tb[sl__][:, 1, :], start=False, stop=False)][-1]))(s, sl_),
                             ['qz', 'C0Tb%d' % sl_], ['ps6'])
                        ts(kwz[s % 2], kw[:], wz[:, s:s + 1], ALU.mult, ['kw0', 'wz'], ['kwz%d' % (s % 2)])

                    def st3(s):
                        sl_ = s % NSL
                        kz_ = kwz[s % 2]
                        for e2 in range(2):
                            bu = bankM()
                            mm(bu, PS[bu][:, 0:256], [(vm[:, 0, e2 * 128:(e2 + 1) * 128], kz_)], ['vm', 'kwz%d' % (s % 2)])
                            stt(Cout[s % 2][:, e2, :], C0[sl_][:, e2, :], decbc[:, h, s:s + 1], PS[bu][:, 0:256], ALU.mult, ALU.add,
                                ['C0_%d' % sl_, 'decbc', 'ps%d' % bu], ['Cout%d' % (s % 2)])
                        if s + NSL < NS:
                            loadC(s + NSL)
                        sdma(o_Cs[s, h].rearrange("(e p) d -> p e d", p=128), Cout[s % 2], ['Cout%d' % (s % 2)], [], 'Cout%d' % (s % 2))

                    for k_it in range(NS + 2):
                        if k_it < NS:
                            st1(k_it)
                        if 0 <= k_it - 1 < NS:
                            st2(k_it - 1)
                        if 0 <= k_it - 2 < NS:
                            st3(k_it - 2)
                        yield
                    bn = bankM()
                    mm(bn, PS[bn][0:NS, 0:256], [(wz, kw[:])], ['wz', 'kw0'])
                    stt(nnew[:, h * 256:(h + 1) * 256], n0[:, h * 256:(h + 1) * 256], dectm[:, h:h + 1], PS[bn][0:NS, 0:256],
                        ALU.mult, ALU.add, ['n0', 'dectm', 'ps%d' % bn], ['nnew'])
                    S.op('pe', lambda pe: pe.matmul(PS[6][:, 0:257], sDbL[0][:], vm[:, 0, :], start=False, stop=True), ['sDb0', 'vm'], ['ps6'])
                    finishA(0, 6)
                    yield
                    finishB(0)
                    yield

            def gla_head(h):
                def decay_chain():
                    b = bankG()
                    mm(b, P(b, T), [(wa2[:, h * 128:(h + 1) * 128], agT[0:16, :])], ['wa2', 'agT'])
                    act(t1[:, :], P(b, T), AF.Exp, ['ps%d' % b, 'negba'], ['gt1'], bias=negba[:, h:h + 1], scale=-1.0)
                    yield
                    act(t2[:, :], t1[:, :], AF.Ln, ['gt1'], ['gt2'], bias=1.0)
                    yield
                    for s in range(nseg):
                        sl = slice(s * L, (s + 1) * L)
                        scan(Cs[:, sl], onesf[:, 0:L], t2[:, sl], 0.0, ALU.mult, ALU.add, ['CST', 'gt2'], ['Cs'])
                        if kind == 'P' or s % 4 == 3:
                            yield
                    act(eal[:, 0:nseg], Cs[:, :].rearrange("p (n c) -> p n c", c=L)[:, :, L - 1], AF.Exp, ['Cs'], ['eal'], scale=-1.0 / 16)
                    yield
                    act(t1[:, :], Cs[:, :], AF.Exp, ['Cs'], ['gt1'], scale=-1.0 / 16)
                    yield
                    act(t2[:, :], Cs[:, :], AF.Exp, ['Cs'], ['gt2'], scale=1.0 / 16)
                    yield

                def vr_proj():
                    wv, wk = wload(w_in[:, C_VG + h * 256:C_VG + (h + 1) * 256], 8, 256)
                    for i in range(nt):
                        b = bankG()
                        mm(b, P(b, 256), [(hT[:, k, i * 128:(i + 1) * 128], wv[:, k, :]) for k in range(8)], HK + [wk])
                        cp(vg[:, i, :], P(b, 256), ['ps%d' % b], ['vg'], eng='act')
                        yield
                    wv, wk = wload(w_in[:, C_RG + h * 256:C_RG + (h + 1) * 256], 8, 256)
                    for i in range(nt):
                        b = bankG()
                        mm(b, P(b, 256), [(hT[:, k, i * 128:(i + 1) * 128], wv[:, k, :]) for k in range(8)], HK + [wk])
                        sig3(sgt2[:, :], P(b, 256), ['ps%d' % b], 'sgt2')
                        tt(sgt2[:, :], sgt2[:, :], gbc_g[:, h * 256:(h + 1) * 256], ALU.mult, ['sgt2', 'gbc_g'], ['sgt2'])
                        tt(rg[:, i, :], P(b, 256), sgt2[:, :], ALU.mult, ['ps%d' % b, 'sgt2'], ['rg'])
                        yield

                yield from interleave([decay_chain(), vr_proj()], [1, 1])
                wv, wk = wload(w_in[:, C_QG + h * 128:C_QG + (h + 1) * 128], 8, 128)
                b = bankG()
                mm(b, P(b, T), [(wv[:, k, :], hT[:, k, 0:T]) for k in range(8)], HK + [wk])
                stt(qgT[:, :], P(b, T), 128.0 ** -0.5, t1[:, :], ALU.mult, ALU.mult, ['ps%d' % b, 'gt1'], ['qgT'])
                yield
                wv, wk = wload(w_in[:, C_KG + h * 128:C_KG + (h + 1) * 128], 8, 128)
                b = bankG()
                mm(b, P(b, T), [(wv[:, k, :], hT[:, k, 0:T]) for k in range(8)], HK + [wk])
                tt(kgT[:, :], P(b, T), t2[:, :], ALU.mult, ['ps%d' % b, 'gt2'], ['kgT'])
                yield
                def gfinishA(i, b2):
                    sm_ = smgL[i % 2]; sq_ = 'sg%d_' % (i % 2); og_ = ogbL[i]; ok_ = 'ogb%d' % i
                    act(junkb[:], PS[b2][:, 0:256], AF.Square, ['ps%d' % b2], ['junkb', sq_ + '0'], accum=sm_[:, 0:1])
                    act(sm_[:, 1:2], sm_[:, 0:1], AF.Ln, [sq_ + '0', 'epsc'], [sq_ + '1'], bias=epsc[:], scale=1.0 / 256)
                    act(sm_[:, 2:3], sm_[:, 1:2], AF.Exp, [sq_ + '1'], [sq_ + '2'], scale=-0.5)
                    stt(og_[:], PS[b2][:, 0:256], sm_[:, 2:3], rg[:, i, :], ALU.mult, ALU.mult, ['ps%d' % b2, sq_ + '2', 'rg'], [ok_])

                def gfinishB(i):
                    tsl = slice(i * 128, (i + 1) * 128)
                    og_ = ogbL[i]; ok_ = 'ogb%d' % i
                    bt = bankG()
                    tr(bt, [(PS[bt][:, :].bitcast(BF16)[:, ec * 128:(ec + 1) * 128], og_[:, ec * 128:(ec + 1) * 128]) for ec in range(2)],
                       identb[:], [ok_, 'identb'])
                    cp(ogT[:, 2 * h:2 * h + 2, tsl], PS[bt][:, :].bitcast(BF16)[:, 0:256].rearrange("p (e c) -> p e c", c=128),
                       ['ps%d' % bt], ['ogT%d' % h], eng='act')

                for i in range(nt):
                    tsl = slice(i * 128, (i + 1) * 128)
                    b1 = bankG()
                    mm(b1, PS[b1][:, 0:128], [(kgT[:, tsl], qgT[:, tsl])], ['kgT', 'qgT'])
                    tt(attbL[i][:], PS[b1][:, 0:128], m01, ALU.mult, ['ps%d' % b1, 'CST'], ['attb%d' % i])
                    bk = bankG()
                    tr(bk, [(PS[bk][:, :].bitcast(BF16)[:, 0:128], kgT[:, tsl])], identb[:], ['kgT', 'identb'])
                    cp(kgtokL[i][:], PS[bk][:, :].bitcast(BF16)[:, 0:128], ['ps%d' % bk], ['kgtok%d' % i], eng='act')
                    yield
                if kind == 'P':
                    ssn = [Sb[:, h, :]] + [Ssnap[j] for j in range(nt - 1)]
                    ssk = ['Sb'] + ['Ssnap%d' % j for j in range(nt - 1)]
                    for i in range(nt):
                        bu = bankG()
                        mm(bu, PS[bu][:, 0:256], [(kgtokL[i][:], vg[:, i, :])], ['kgtok%d' % i, 'vg'])
                        tt(stmp[:], S32[:, h, :], PS[bu][:, 0:256], ALU.add, ['S32', 'ps%d' % bu], ['stmp'])
                        ts(S32[:, h, :], stmp[:], eal[:, i:i + 1], ALU.mult, ['stmp', 'eal'], ['S32'])
                        if i < nt - 1:
                            cp(ssn[i + 1], S32[:, h, :], ['S32'], [ssk[i + 1]], eng='act')
                        yield
                    for i in range(nt):
                        tsl = slice(i * 128, (i + 1) * 128)
                        b2 = bankG()
                        mm(b2, PS[b2][:, 0:256], [(attbL[i][:], vg[:, i, :]), (qgT[:, tsl], ssn[i])], ['attb%d' % i, 'vg', 'qgT', ssk[i]])
                        gfinishA(i, b2)
                        yield
                    for i in range(nt):
                        gfinishB(i)
                        yield
                    cp(Sb[:, h, :], S32[:, h, :], ['S32'], ['Sb'], eng='act')
                    yield
                else:
                    attb = attbL[0]; kgtok = kgtokL[0]
                    cp(qgz[:, :].rearrange("p (j c) -> p j c", c=136)[:, :, 0:8], qgT[:, :].rearrange("p (j c) -> p j c", c=8), ['qgT'], ['qgz'])
                    S.op('pe', lambda pe: pe.matmul(PS[7][:, 0:256], attbL[0][:], vg[:, 0, :], start=True, stop=False), ['attb0', 'vg'], ['ps7'])

                    def loadS(s_):
                        k_ = s_ % NSL
                        sdma(S0[k_], sS[s_, h], [], ['S0_%d' % k_], 'S0_%d' % k_)
                    for s in range(NSL):
                        loadS(s)
                    yield

                    def gs1(s):
                        sl_ = s % NSL
                        cp(S0b[sl_], S0[sl_], ['S0_%d' % sl_], ['S0b%d' % sl_], eng='act')
                        ts(kgz[s % 2], kgtok[:], segm[:, s:s + 1], ALU.mult, ['kgtok0', 'CST'], ['kgz%d' % (s % 2)])

                    def gs2(s):
                        sl_ = s % NSL
                        S.op('pe', (lambda s_, sl__: (lambda pe: pe.matmul(PS[7][:, 0:256], qgz[:, s_ * 128:(s_ + 1) * 128], S0b[sl__][:], start=False, stop=(s_ == NS - 1))))(s, sl_),
                             ['qgz', 'S0b%d' % sl_], ['ps7'])
                        bu = bankG()
                        mm(bu, PS[bu][:, 0:256], [(kgz[s % 2], vg[:, 0, :])], ['kgz%d' % (s % 2), 'vg'])
                        tt(stmp[:], S0[sl_], PS[bu][:, 0:256], ALU.add, ['S0_%d' % sl_, 'ps%d' % bu], ['stmp'])
                        ts(Sout[s % 2], stmp[:], eal[:, s:s + 1], ALU.mult, ['stmp', 'eal'], ['Sout%d' % (s % 2)])
                        if s + NSL < NS:
                            loadS(s + NSL)
                        sdma(o_Ss[s, h], Sout[s % 2], ['Sout%d' % (s % 2)], [], 'Sout%d' % (s % 2))

                    for k_it in range(NS + 1):
                        if k_it < NS:
                            gs1(k_it)
                        if 0 <= k_it - 1 < NS:
                            gs2(k_it - 1)
                        yield
                    gfinishA(0, 7)
                    yield
                    gfinishB(0)
                    yield

            wv, wk = wload(w_in[:, C_AG:C_AG + 16], 8, 16)
            b = bank()
            mm(b, PS[b][0:16, 0:T], [(wv[:, k, 0:16], hT[:, k, 0:T]) for k in range(8)], HK + [wk])
            cp(agT[0:16, :], PS[b][0:16, 0:T], ['ps%d' % b], ['agT'])
            for h in range(4):
                gens = [ml_head(h), gla_head(h)]
                while gens:
                    for g_ in list(gens):
                        try:
                            next(g_)
                        except StopIteration:
                            gens.remove(g_)

            dump('hmT' + kind, hmT[:, :, 0:T], ['hmT%d' % h_ for h_ in range(4)])
            dump('ogT' + kind, ogT[:, :, 0:T], ['ogT%d' % h_ for h_ in range(4)])
            A.off = a_mark
            sgA = A.get('sgA', 4 * T).rearrange("p (f t) -> p f t", t=T)
            yA = A.get('yA', 4 * T).rearrange("p (f t) -> p f t", t=T)
            ytmp = A.get('ytmp', T)
            yT = A.get('yT', 8 * T, BF16).rearrange("p (f t) -> p f t", t=T)
            HM = ['hmT%d' % h for h in range(4)]
            OG = ['ogT%d' % h for h in range(4)]
            for g in range(2):
                for (gcol0, wp, src, sk_, first) in ((C_GA, w_pa, hmT, HM, True), (C_GB, w_pb, ogT, OG, False)):
                    wv, wk = wload(w_in[:, gcol0 + g * 512:gcol0 + (g + 1) * 512], 8, 512)
                    for fc in range(4):
                        b = bank()
                        mm(b, P(b, T), [(wv[:, k, fc * 128:(fc + 1) * 128], hT[:, k, 0:T]) for k in range(8)], HK + [wk])
                        act(sgA[:, fc, :], P(b, T), AF.Sigmoid, ['ps%d' % b], ['sgA'])
                    wv, wk = wload(wp[:, g * 512:(g + 1) * 512], 8, 512)
                    for fc in range(4):
                        b = bank()
                        mm(b, P(b, T), [(wv[:, k, fc * 128:(fc + 1) * 128], src[:, k, 0:T]) for k in range(8)], sk_ + [wk])
                        if first:
                            tt(yA[:, fc, :], P(b, T), sgA[:, fc, :], ALU.mult, ['ps%d' % b, 'sgA'], ['yA'])
                        else:
                            tt(ytmp[:, :], P(b, T), sgA[:, fc, :], ALU.mult, ['ps%d' % b, 'sgA'], ['ytmp'])
                            tt(yT[:, g * 4 + fc, :], ytmp[:, :], yA[:, fc, :], ALU.add, ['ytmp', 'yA'], ['yT'])
            dump('yT' + kind, yT, ['yT'])
            for g in range(2):
                wv, wk = wload(w_o[:, g * 512:(g + 1) * 512], 8, 512)
                for fc in range(4):
                    f = g * 4 + fc
                    b = bank()
                    mm(b, P(b, T), [(wv[:, k, fc * 128:(fc + 1) * 128], yT[:, k, :]) for k in range(8)], ['yT', wk])
                    tt(xT[:, f, 0:T], P(b, T), xT[:, f, 0:T], ALU.add, ['ps%d' % b, 'xT%d' % f], ['xT%d' % f])
                    norm_partial(f, T)

            if kind == 'S':
                sdma(o_ns, nnew[:], ['nnew'], [], 'st_nn')
                for k in range(8):
                    b = bank()
                    tr(b, [(PS[b][0:48, 0:128], uhS2[:, k, :, :].rearrange("p s j -> p (s j)"))], ident, ['uhS2', 'CST'])
                    cp(sc48[:, k * 128:(k + 1) * 128], PS[b][0:48, 0:128], ['ps%d' % b], ['yout'])
                sdma(o_convs, sc48[:], ['yout'], [], 'st_c')
            elif last:
                for k in range(8):
                    b = bank()
                    tr(b, [(PS[b][0:3, 0:128], uhist[:, k, :])], ident, ['uhist', 'CST'])
                    cp(sc48[0:3, k * 128:(k + 1) * 128], PS[b][0:3, 0:128], ['ps%d' % b], ['yout'])
                sdma(o_convp, sc48[0:3, :], ['yout'], [], 'st_c')
                for h in range(4):
                    for dc in range(2):
                        for e2 in range(2):
                            b = bank()
                            tr(b, [(PS[b][:, 0:128], CT32[:, h, dc, e2 * 128:(e2 + 1) * 128])], ident, ['CT32', 'CST'])
                            cp(Cn[0][:, e2, dc * 128:(dc + 1) * 128], PS[b][:, 0:128], ['ps%d' % b], ['Cn0'])
                        b = bank()
                        tr(b, [(PS[b][0:1, 0:128], CT32[:, h, dc, 256:257])], ident, ['CT32', 'CST'])
                        cp(nnew[0:1, h * 256 + dc * 128:h * 256 + (dc + 1) * 128], PS[b][0:1, 0:128], ['ps%d' % b], ['nnew'])
                    sdma(o_Cp[h].rearrange("(e p) d -> p e d", p=128), Cn[0][:], ['Cn0'], [], 'Cn0')
                    sdma(o_Sp[h], S32[:, h, :], ['S32'], [], 'st_sp')
                sdma(o_np.rearrange("(o h) d -> o (h d)", o=1), nnew[0:1, :], ['nnew'], [], 'st_nn')

        blocks = [('S', xs, ys, 128, False)] + [('P', xp[i * 512:(i + 1) * 512, :], yp[i * 512:(i + 1) * 512, :], 512, i == 3) for i in range(4)]
        for bi, (kind, src, dst, T, last) in enumerate(blocks):
            S.curT = T
            A = Arena(T)
            load_x(src, T, prefetched=(bi > 0))
            ffn(w1u, w1d, gc1, 'gc1', T, A)
            dump('x1' + kind, xT[:, :, 0:T], XK)
            A = Arena(T)
            mixer(T, kind, last, A)
            dump('x2' + kind, xT[:, :, 0:T], XK)
            if bi + 1 < len(blocks):
                prefetch_x(blocks[bi + 1][1], blocks[bi + 1][3])
                S.curT = T
            A = Arena(T)
            ffn(w2u, w2d, gc2, 'gc2', T, A, partial_done=True, next_norm=False)
            final_out(dst, T)
        S.finish()

        with nc.Block() as block:
            @block.tensor
            def _(e):
                for f in S.stream['pe']:
                    f(e)

            @block.scalar
            def _(e):
                for f in S.stream['act']:
                    f(e)

            @block.vector
            def _(e):
                for f in S.stream['dve']:
                    f(e)

            @block.gpsimd
            def _(e):
                for f in S.stream['pool']:
                    f(e)

            @block.sync
            def _(e):
                for f in S.stream['sp']:
                    f(e)
    return nc


def make_consts():
    c = np.zeros((128, 1792), np.float32)
    idx = np.arange(128)
    c[:, 0:128] = np.eye(128, dtype=np.float32)
    s, t = idx[:, None], idx[None, :]
    causal = s <= t
    same = (s // DS) == (t // DS)
    c[:, 128:256] = np.where(causal, 0.0, BIG)
    c[:, 256:384] = np.where(causal & same, 0.0, BIG)
    c[:, 384:512] = causal.astype(np.float32)
    c[:, 512:640] = (causal & same).astype(np.float32)
    c[:, 640:656] = ((idx[:, None] // DS) == np.arange(NS)[None, :]).astype(np.float32)
    for k in range(4):
        c[k, 656 + k * 128:656 + (k + 1) * 128] = 1.0
    c[:, 1280:1792] = 1.0
    return c


_NC_CACHE = {}


def kernel(x_prompt, x_sample, state_conv, state_mlstm_C, state_mlstm_n, state_mlstm_m, state_gla_S,
           g_ffn1, w_ffn1_up, w_ffn1_down, g_mix, w_in, conv_w, conv_b, w_mq, w_mk, b_if, g_mhead,
           w_a2, b_a, g_ghead, w_pa, w_pb, w_o, g_ffn2, w_ffn2_up, w_ffn2_down, g_final):
    f = lambda a: np.ascontiguousarray(np.asarray(a, dtype=np.float32))
    shared = {
        "g_ffn1": f(g_ffn1).reshape(D), "w_ffn1_up": f(w_ffn1_up).reshape(D, 2 * DFF), "w_ffn1_down": f(w_ffn1_down).reshape(DFF, D),
        "g_mix": f(g_mix).reshape(D), "w_in": f(w_in).reshape(D, 8216), "conv_w": f(conv_w).reshape(4, D), "conv_b": f(conv_b).reshape(D),
        "w_mq": f(w_mq).reshape(4, 256, 256), "w_mk": f(w_mk).reshape(4, 256, 256), "b_if": f(b_if).reshape(2, 4),
        "g_mhead": f(g_mhead).reshape(D), "w_a2": f(w_a2).reshape(16, 512), "b_a": f(b_a).reshape(512), "g_ghead": f(g_ghead).reshape(D),
        "w_pa": f(w_pa).reshape(D, D), "w_pb": f(w_pb).reshape(D, D), "w_o": f(w_o).reshape(D, D),
        "g_ffn2": f(g_ffn2).reshape(D), "w_ffn2_up": f(w_ffn2_up).reshape(D, 2 * DFF), "w_ffn2_down": f(w_ffn2_down).reshape(DFF, D),
        "g_final": f(g_final).reshape(D), "consts": make_consts(),
    }
    xp_, xs_ = f(x_prompt), f(x_sample)
    sc_, sC_, sn_, sm_, sS_ = f(state_conv)[0], f(state_mlstm_C)[0], f(state_mlstm_n)[0], f(state_mlstm_m)[0], f(state_gla_S)[0]
    in_maps = []
    for c in range(8):
        sl = slice(c * NS, (c + 1) * NS)
        m = dict(shared)
        m.update({
            "xp": xp_[c], "xs": xs_[sl].reshape(128, D), "sconv": sc_[sl].reshape(48, D), "sC": sC_[sl],
            "sn": sn_[sl].reshape(NS, D), "sm": sm_[sl], "sS": sS_[sl],
        })
        in_maps.append({k: np.ascontiguousarray(v) for k, v in m.items()})
    dbgm = bool(_NC_CACHE.get('debug'))
    key = 'nc_dbg' if dbgm else 'nc'
    if key not in _NC_CACHE:
        _NC_CACHE[key] = build_program(debug=dbgm)
    res = run_bass_kernel_spmd(_NC_CACHE[key], in_maps, core_ids=list(range(8)))
    R = res.results
    _NC_CACHE['raw'] = R if dbgm else None
    cat = lambda k: np.stack([np.asarray(r[k]) for r in R], 0)
    y_prompt = cat("yp").reshape(8, SEQ, D)
    y_sample = cat("ys").reshape(128, DS, D)
    conv_p = cat("conv_p").reshape(1, 8, 3, D)
    C_p = cat("C_p").reshape(1, 8, 4, 256, 256)
    n_p = cat("n_p").reshape(1, 8, 4, 256)
    m_p = cat("m_p").reshape(1, 8, 4)
    S_p = cat("S_p").reshape(1, 8, 4, 128, 256)
    conv_s = cat("conv_s").reshape(1, 128, 3, D)
    C_s = cat("C_s").reshape(1, 128, 4, 256, 256)
    n_s = cat("n_s").reshape(1, 128, 4, 256)
    m_s = cat("m_s").reshape(1, 128, 4)
    S_s = cat("S_s").reshape(1, 128, 4, 128, 256)
    outs = (y_prompt, y_sample, conv_p, C_p, n_p, m_p, S_p, conv_s, C_s, n_s, m_s, S_s)
    return tuple(np.ascontiguousarray(o, dtype=np.float32) for o in outs)
```

```python
import numpy as np
from contextlib import ExitStack
import concourse.bass as bass
import concourse.mybir as mybir
from concourse.bass_utils import run_bass_kernel_spmd

F32 = mybir.dt.float32
BF16 = mybir.dt.bfloat16
AF = mybir.ActivationFunctionType
ALU = mybir.AluOpType

D = 1024
SEQ = 2048
NS = 16
DS = 8
DFF = 2816
NFF = DFF // 128
EPS = 1e-6
BIG = 30000.0
TMAX = 512
WSLOT = 4096
NWS = 3
ENG = ['pe', 'act', 'dve', 'pool', 'sp']

C_U, C_V, C_O, C_IF, C_QG, C_KG, C_VG, C_RG, C_AG, C_GA, C_GB = 0, 1024, 2048, 3072, 3080, 3592, 4104, 5128, 6152, 6168, 7192


class Sched:
    def __init__(self, nc, es):
        self.nc, self.es = nc, es
        self.stream = {e: [] for e in ENG}
        self.semobj = {}
        self.cnt = {}
        for e in ['pe', 'act', 'dve', 'pool']:
            self.semobj['E' + e] = es.enter_context(nc.semaphore('sem_' + e))
            self.cnt['E' + e] = 0
        self.waited = {e: {} for e in ENG}
        self.lastw = {}
        self.readers = {}
        self.ranges = {}
        self.rkeys = []
        self.curT = 0

    def register(self, key, lo, hi):
        if key not in self.ranges:
            self.ranges[key] = (lo, hi)
            self.rkeys.append(key)

    def _expand(self, writes):
        out = []
        for k in writes:
            out.append(k)
            if k in self.ranges:
                lo, hi = self.ranges[k]
                for k2 in self.rkeys:
                    if k2 != k:
                        l2, h2 = self.ranges[k2]
                        if l2 < hi and lo < h2:
                            out.append(k2)
        return out

    def _deps(self, e, reads, writes):
        need = {}

        def add(t):
            if t is not None and need.get(t[0], 0) < t[1]:
                need[t[0]] = t[1]
        for b in reads:
            add(self.lastw.get(b))
        for b in writes:
            add(self.lastw.get(b))
            for sk, v in self.readers.get(b, {}).items():
                add((sk, v))
        out = []
        for sk, v in need.items():
            if self.waited[e].get(sk, 0) < v:
                self.waited[e][sk] = v
                out.append((sk, v))
        return out

    def _commit(self, tok, reads, writes):
        for b in writes:
            self.lastw[b] = tok
            self.readers[b] = {}
        for b in reads:
            r = self.readers.setdefault(b, {})
            if r.get(tok[0], 0) < tok[1]:
                r[tok[0]] = tok[1]

    def _map(self, keys):
        sfx = '_%d' % self.curT
        return [(k + sfx) if (k + sfx) in self.ranges else k for k in keys]

    def op(self, e, fn, reads=(), writes=()):
        writes = self._expand(self._map(writes))
        reads = self._map(reads)
        waits = self._deps(e, reads, writes)
        sk = 'E' + e
        self.cnt[sk] += 1
        tok = (sk, self.cnt[sk])
        sem = self.semobj[sk]
        so = self.semobj

        def run(eng):
            for (k, v) in waits:
                eng.wait_ge(so[k], v)
            fn(eng).then_inc(sem, 1)
        self.stream[e].append(run)
        self._commit(tok, reads, writes)

    def dma(self, q, out, in_, reads, writes, skey, **kw):
        writes = self._expand(self._map(writes))
        reads = self._map(reads)
        waits = self._deps(q, reads, writes)
        sk = 'D' + skey
        if sk not in self.semobj:
            self.semobj[sk] = self.es.enter_context(self.nc.semaphore('dsem_' + skey))
            self.cnt[sk] = 0
        self.cnt[sk] += 16
        tok = (sk, self.cnt[sk])
        sem = self.semobj[sk]
        so = self.semobj

        def run(eng):
            for (k, v) in waits:
                eng.wait_ge(so[k], v)
            eng.dma_start(out=out, in_=in_, **kw).then_inc(sem, 16)
        self.stream[q].append(run)
        self._commit(tok, reads, writes)

    def finish(self):
        so, cnt = self.semobj, dict(self.cnt)

        def run(eng):
            for k, v in cnt.items():
                if v > 0:
                    eng.wait_ge(so[k], v)
        self.stream['sp'].append(run)


def build_program(debug=False):
    nc = bass.Bass("TRN2", target_bir_lowering=False)
    es = ExitStack()
    dbg = {}

    def din(name, shape):
        return nc.dram_tensor(name, list(shape), F32, kind="ExternalInput").ap()

    def dout(name, shape):
        return nc.dram_tensor(name, list(shape), F32, kind="ExternalOutput").ap()

    xp = din("xp", [SEQ, D]); xs = din("xs", [128, D])
    sconv = din("sconv", [48, D]); sC = din("sC", [NS, 4, 256, 256]); sn = din("sn", [NS, D])
    sm = din("sm", [NS, 4]); sS = din("sS", [NS, 4, 128, 256])
    g_ffn1 = din("g_ffn1", [D]); w1u = din("w_ffn1_up", [D, 2 * DFF]); w1d = din("w_ffn1_down", [DFF, D])
    g_mix = din("g_mix", [D]); w_in = din("w_in", [D, 8216]); conv_w = din("conv_w", [4, D]); conv_b = din("conv_b", [D])
    w_mq = din("w_mq", [4, 256, 256]); w_mk = din("w_mk", [4, 256, 256]); b_if = din("b_if", [2, 4])
    g_mhead = din("g_mhead", [D]); w_a2 = din("w_a2", [16, 512]); b_a = din("b_a", [512]); g_ghead = din("g_ghead", [D])
    w_pa = din("w_pa", [D, D]); w_pb = din("w_pb", [D, D]); w_o = din("w_o", [D, D])
    g_ffn2 = din("g_ffn2", [D]); w2u = din("w_ffn2_up", [D, 2 * DFF]); w2d = din("w_ffn2_down", [DFF, D])
    g_final = din("g_final", [D]); cst = din("consts", [128, 1792])

    yp = dout("yp", [SEQ, D]); ys = dout("ys", [128, D])
    o_convp = dout("conv_p", [3, D]); o_Cp = dout("C_p", [4, 256, 256]); o_np = dout("n_p", [4, 256])
    o_mp = dout("m_p", [1, 4]); o_Sp = dout("S_p", [4, 128, 256])
    o_convs = dout("conv_s", [48, D]); o_Cs = dout("C_s", [NS, 4, 256, 256]); o_ns = dout("n_s", [NS, D])
    o_ms = dout("m_s", [NS, 4]); o_Ss = dout("S_s", [NS, 4, 128, 256])

    with es:
        S = Sched(nc, es)

        def sb(name, shape, dt=F32):
            return es.enter_context(nc.sbuf_tensor(name, list(shape), dt))

        xT = sb("xT", [128, 8, TMAX])
        hT = sb("hT", [128, 8, TMAX], BF16)
        wsl = [sb("wsl%d" % i, [128, WSLOT], BF16) for i in range(NWS)]
        CST = sb("CST", [128, 1792])
        ident = CST[:, 0:128]; maskbigP = CST[:, 128:256]; maskbigS = CST[:, 256:384]
        mask01P = CST[:, 384:512]; mask01S = CST[:, 512:640]
        onesf = CST[:, 1280:1792]
        segm = CST[:, 640:656]
        SEL = sb("SEL", [4, 4, 128])
        identb = sb("identb", [128, 128], BF16); SELb = sb("SELb", [4, 4, 128], BF16)
        maskbP = sb("maskbP", [128, 128], BF16); maskbS = sb("maskbS", [128, 128], BF16)
        onesb = sb("onesb", [128, 128], BF16)
        gbc_m = sb("gbc_m", [128, D]); gbc_g = sb("gbc_g", [128, D]); gbc_f = sb("gbc_f", [128, D])
        gc1 = sb("gc1", [128, 8]); gcm = sb("gcm", [128, 8]); gc2 = sb("gc2", [128, 8])
        cw = sb("cw", [128, 4, 8]); cb = sb("cb", [128, 8])
        negba = sb("negba", [128, 4]); wa2 = sb("wa2", [16, 512])
        bif = sb("bif", [4, 2]); negbf = sb("negbf", [4, 1])
        epsc = sb("epsc", [128, 1])
        yout = sb("yout", [128, D])
        sqb = [sb("sqb%d" % i, [128, TMAX], BF16) for i in range(2)]
        srt = sb("srt", [128, TMAX]); rstd = sb("rstd", [128, TMAX])
        hmT = sb("hmT", [128, 8, TMAX], BF16); ogT = sb("ogT", [128, 8, TMAX], BF16)
        CT32 = sb("CT32", [128, 4, 2, 257]); CTb = sb("CTb", [128, 4, 2, 257], BF16)
        S32 = sb("S32", [128, 4, 256]); Sb = sb("Sb", [128, 4, 256], BF16)
        uhist = sb("uhist", [128, 8, 3]); uhS = sb("uhS", [128, 8, NS, 3]); uhS2 = sb("uhS2", [128, 8, NS, 3])
        carF = sb("carF", [4, 1]); carM = sb("carM", [4, 1]); carG = sb("carG", [4, 1])
        m0T = sb("m0T", [4, NS])
        colsTM = sb("colsTM", [128, 4, 20]); decbc = sb("decbc", [128, 4, 16]); dectm = sb("dectm", [NS, 4])
        DTsL = [sb("DTs%d" % i, [128, 128]) for i in range(2)]; sDbL = [sb("sDb%d" % i, [128, 128], BF16) for i in range(4)]
        junkb = sb("junkb", [128, 256], BF16)
        smL = [sb("smL%d" % i, [128, 8]) for i in range(2)]; smgL = [sb("smg%d" % i, [128, 8]) for i in range(2)]
        smF = [sb("smF%d" % i, [128, 4]) for i in range(2)]
        hgL = [sb("hg%d" % i, [128, 256], BF16) for i in range(4)]; kwL = [sb("kw%d" % i, [128, 256], BF16) for i in range(4)]
        attbL = [sb("attb%d" % i, [128, 128], BF16) for i in range(4)]; ogbL = [sb("ogb%d" % i, [128, 256], BF16) for i in range(4)]
        kgtokL = [sb("kgtok%d" % i, [128, 128], BF16) for i in range(4)]
        stmp = sb("stmp", [128, 256])
        Cn = [sb("Cn%d" % i, [128, 2, 256]) for i in range(1)]
        nnew = sb("nnew", [NS, D])
        sc48 = yout[0:48, :]; msout = sb("msout", [NS, 4])
        ARENA = 34816
        AR = sb("AR", [128, ARENA], BF16)

        PS = [es.enter_context(nc.psum_tensor("ps%d" % i, [128, 512], F32)) for i in range(8)]
        bankctr = [0]

        def bank():
            b = bankctr[0] % 6
            bankctr[0] += 1
            return b
        bmc = [0]; bgc = [0]

        def bankM():
            b = bmc[0] % 4
            bmc[0] += 1
            return b

        def bankG():
            b = 4 + bgc[0] % 2
            bgc[0] += 1
            return b

        class Arena:
            def __init__(self, T):
                self.off = 0
                self.T = T

            def get(self, name, n, dt=F32):
                name = '%s_%d' % (name, self.T)
                ne = n * (2 if dt == F32 else 1)
                o = self.off
                self.off += ne + (ne % 2)
                assert self.off <= ARENA, (name, self.off)
                ap = AR[:, o:o + ne]
                if dt == F32:
                    ap = ap.bitcast(F32)
                S.register(name, o, o + ne)
                return ap

        def P(b, n=512):
            return PS[b][:, 0:n]

        def mm(b, out, pairs, reads):
            def fn(pe):
                n = len(pairs)
                for i, (l, r) in enumerate(pairs):
                    ins = pe.matmul(out, l, r, start=(i == 0), stop=(i == n - 1))
                return ins
            S.op('pe', fn, reads=reads, writes=['ps%d' % b])

        def tr(b, outs_ins, idn, reads):
            def fn(pe):
                for (o, i) in outs_ins:
                    ins = pe.transpose(o, i, idn)
                return ins
            S.op('pe', fn, reads=reads, writes=['ps%d' % b])

        def act(out, in_, func, reads, writes, bias=None, scale=None, accum=None):
            kw_ = {}
            if bias is not None:
                kw_['bias'] = bias
            if scale is not None:
                kw_['scale'] = scale
            if accum is not None:
                kw_['accum_out'] = accum
            S.op('act', lambda e: e.activation(out=out, in_=in_, func=func, **kw_), reads, writes)

        def tt(out, a, b_, op, reads, writes, eng='dve'):
            S.op(eng, lambda e: e.tensor_tensor(out=out, in0=a, in1=b_, op=op), reads, writes)

        def stt(out, a, sc, b_, op0, op1, reads, writes):
            S.op('dve', lambda e: e.scalar_tensor_tensor(out=out, in0=a, scalar=sc, in1=b_, op0=op0, op1=op1), reads, writes)

        def ts(out, a, s1, op0, reads, writes, s2=None, op1=None, eng='dve'):
            if op1 is None:
                S.op(eng, lambda e: e.tensor_scalar(out=out, in0=a, scalar1=s1, scalar2=None, op0=op0), reads, writes)
            else:
                S.op(eng, lambda e: e.tensor_scalar(out=out, in0=a, scalar1=s1, scalar2=s2, op0=op0, op1=op1), reads, writes)

        def cp(out, in_, reads, writes, eng='dve'):
            if eng == 'act':
                S.op(eng, lambda e: e.activation(out=out, in_=in_, func=AF.Copy), reads, writes)
            else:
                S.op(eng, lambda e: e.tensor_copy(out=out, in_=in_), reads, writes)

        def recip(out, in_, reads, writes):
            S.op('dve', lambda e: e.reciprocal(out=out, in_=in_), reads, writes)

        def sig3(out, in_, reads, key):
            act(out, in_, AF.Exp, reads, [key], scale=-1.0)
            act(out, out, AF.Ln, [key], [key], bias=1.0)
            act(out, out, AF.Exp, [key], [key], scale=-1.0)

        def scan(out, d0, d1, init, op0, op1, reads, writes):
            S.op('dve', lambda e: e.tensor_tensor_scan(out=out, data0=d0, data1=d1, initial=init, op0=op0, op1=op1), reads, writes)

        def mset(ap, v, writes, eng='dve'):
            S.op(eng, lambda e: e.memset(ap, v), [], writes)

        uq = [0]

        def sdma(out, in_, reads, writes, skey, q='sp'):
            if skey in ('c2', 'c3'):
                uq[0] += 1
                skey = 'k%d' % uq[0]
            S.dma(q, out, in_, reads, writes, skey, allow_slow_non_contiguous=True)

        def dump(name, ap, reads):
            if not debug or name in dbg:
                return
            dbg[name] = nc.dram_tensor("dbg_" + name, list(ap.shape), ap.dtype, kind="ExternalOutput").ap()
            S.dma('sp', dbg[name], ap, reads, [], 'dbg', allow_slow_non_contiguous=True)

        wctr = [0]

        def wload(src, K, ncol):
            i = wctr[0] % NWS
            wctr[0] += 1
            view = wsl[i][:, 0:K * ncol].rearrange("p (k c) -> p k c", c=ncol)
            S.dma('pool', view, src.rearrange("(k p) c -> p k c", p=128), [], ['wsl%d' % i], 'w%d' % i)
            return view, 'wsl%d' % i

        sdma(CST[:], cst, [], ['CST'], 'c0')
        for h in range(4):
            pass
        sdma(SEL[:].rearrange("k h m -> k (h m)"), cst[0:4, 656:656 + 512], [], ['SEL'], 'c1')
        cp(identb[:], ident, ['CST'], ['identb'])
        cp(SELb[:], SEL[:], ['SEL'], ['SELb'])
        cp(maskbP[:], maskbigP, ['CST'], ['maskb'])
        cp(maskbS[:], maskbigS, ['CST'], ['maskb'])
        mset(onesb[:], 1.0, ['onesb'])
        mset(epsc[:], EPS, ['epsc'])
        for (t_, g_) in ((gbc_m, g_mhead), (gbc_g, g_ghead), (gbc_f, g_final)):
            sdma(t_[:], g_.rearrange("(o d) -> o d", o=1).broadcast_to([128, D]), [], [t_.name], 'c2')
        for (t_, g_) in ((gc1, g_ffn1), (gcm, g_mix), (gc2, g_ffn2), (cb, conv_b)):
            sdma(t_[:], g_.rearrange("(k p) -> p k", p=128), [], [t_.name], 'c3')
        sdma(cw[:], conv_w.rearrange("j (k p) -> p j k", p=128), [], ['cw'], 'c3')
        sdma(negba[:], b_a.rearrange("(h p) -> p h", p=128), [], ['negba'], 'c3')
        ts(negba[:], negba[:], -1.0, ALU.mult, ['negba'], ['negba'])
        sdma(wa2[:], w_a2, [], ['wa2'], 'c3')
        sdma(bif[:], b_if.rearrange("t h -> h t"), [], ['bif'], 'c3')
        ts(negbf[:], bif[:, 1:2], -1.0, ALU.mult, ['bif'], ['negbf'])
        mset(uhist[:], 0.0, ['uhist'])
        mset(carF[:], 0.0, ['carF']); mset(carM[:], 0.0, ['carM']); mset(carG[:], 0.0, ['carG'])
        mset(CT32[:], 0.0, ['CT32']); mset(CTb[:], 0.0, ['CTb'])
        mset(S32[:], 0.0, ['S32']); mset(Sb[:], 0.0, ['Sb'])

        XIN_OFF = 12288

        def xin_bufs(T):
            A_ = Arena(T)
            A_.off = XIN_OFF
            return [A_.get('xinA%d' % i, D) for i in range(T // 128)]

        def prefetch_x(src, T):
            S.curT = T
            bufs = xin_bufs(T)
            for i in range(T // 128):
                sdma(bufs[i], src[i * 128:(i + 1) * 128, :], [], ['xinA%d' % i], 'xinA%d' % i)

        def load_x(src, T, prefetched):
            nt = T // 128
            bufs = xin_bufs(T)
            if not prefetched:
                for i in range(nt):
                    sdma(bufs[i], src[i * 128:(i + 1) * 128, :], [], ['xinA%d' % i], 'xinA%d' % i)
            for i in range(nt):
                for half in range(2):
                    b = bank()
                    tr(b, [(PS[b][:, kk * 128:(kk + 1) * 128], bufs[i][:, (half * 4 + kk) * 128:(half * 4 + kk + 1) * 128]) for kk in range(4)],
                       ident, ['xinA%d' % i, 'CST'])
                    cp(xT[:, half * 4:half * 4 + 4, i * 128:(i + 1) * 128],
                       PS[b][:, :].rearrange("p (k c) -> p k c", c=128), ['ps%d' % b], ['xT%d' % (half * 4 + k) for k in range(4)],
                       eng='act' if half else 'dve')

        XK = ['xT%d' % k for k in range(8)]
        HK = ['hT%d' % k for k in range(8)]

        ncnt = [0]

        def norm_partial(k, T):
            n = ncnt[0] % 8
            ncnt[0] += 1
            q = sqb[n % 2]
            act(q[:, 0:T], xT[:, k, 0:T], AF.Square, ['xT%d' % k], ['sqb%d' % (n % 2)])
            S.op('pe', (lambda nn, qq, TT: (lambda pe: pe.matmul(PS[7][:, 0:TT], onesb[:], qq[:, 0:TT], start=(nn == 0), stop=(nn == 7))))(n, q, T),
                 ['sqb%d' % (n % 2), 'onesb'], ['ps7'])

        def norm(gc, gname, T, partial_done=False):
            b = 7
            if not partial_done:
                for k in range(8):
                    norm_partial(k, T)
            act(srt[:, 0:T], P(b, T), AF.Ln, ['ps%d' % b, 'epsc'], ['srt'], bias=epsc[:], scale=1.0 / D)
            act(rstd[:, 0:T], srt[:, 0:T], AF.Exp, ['srt'], ['rstd'], scale=-0.5)
            for k in range(8):
                stt(hT[:, k, 0:T], xT[:, k, 0:T], gc[:, k:k + 1], rstd[:, 0:T], ALU.mult, ALU.mult,
                    ['xT%d' % k, 'rstd', gname], ['hT%d' % k])

        def ffn(wu, wd, gc, gname, T, A, partial_done=False, next_norm=True):
            norm(gc, gname, T, partial_done)
            actb = A.get('act', NFF * T, BF16).rearrange("p (j t) -> p j t", t=T)
            for part in (1, 0):
                c0 = part * DFF
                for (uo, nc_) in [(i * 512, 512) for i in range(5)] + [(2560, 256)]:
                    wv, wk = wload(wu[:, c0 + uo:c0 + uo + nc_], 8, nc_)
                    for jj in range(nc_ // 128):
                        j = uo // 128 + jj
                        b = bank()
                        mm(b, P(b, T), [(wv[:, k, jj * 128:(jj + 1) * 128], hT[:, k, 0:T]) for k in range(8)], HK + [wk])
                        if part == 1:
                            act(actb[:, j, :], P(b, T), AF.Silu, ['ps%d' % b], ['act'])
                        else:
                            tt(actb[:, j, :], P(b, T), actb[:, j, :], ALU.mult, ['ps%d' % b, 'act'], ['act'])
            for f in range(8):
                wv, wk = wload(wd[:, f * 128:(f + 1) * 128], NFF, 128)
                b = bank()
                mm(b, P(b, T), [(wv[:, k, :], actb[:, k, :]) for k in range(NFF)], ['act', wk])
                stt(xT[:, f, 0:T], P(b, T), 0.5, xT[:, f, 0:T], ALU.mult, ALU.add, ['ps%d' % b, 'xT%d' % f], ['xT%d' % f])
                if next_norm:
                    norm_partial(f, T)

        def final_out(dst, T):
            nt = T // 128
            A_ = Arena(T)
            xtk = [A_.get('xtokA%d' % i, D) for i in range(nt)]
            yo = [A_.get('youtA%d' % i, D) for i in range(2)]
            for i in range(nt):
                for half in range(2):
                    b = bank()
                    tr(b, [(PS[b][:, kk * 128:(kk + 1) * 128], xT[:, half * 4 + kk, i * 128:(i + 1) * 128]) for kk in range(4)],
                       ident, XK + ['CST'])
                    cp(xtk[i][:, half * 512:(half + 1) * 512], PS[b][:, :], ['ps%d' % b], ['xtokA%d' % i], eng='act' if half else 'dve')
            for i in range(nt):
                y_ = yo[i % 2]; yk = 'youtA%d' % (i % 2); sm_ = smF[i % 2]; sk = 'smF%d_' % (i % 2)
                act(y_, xtk[i], AF.Square, ['xtokA%d' % i], [yk, sk + '0'], accum=sm_[:, 0:1])
                act(sm_[:, 1:2], sm_[:, 0:1], AF.Ln, [sk + '0', 'epsc'], [sk + '1'], bias=epsc[:], scale=1.0 / D)
                act(sm_[:, 2:3], sm_[:, 1:2], AF.Exp, [sk + '1'], [sk + '2'], scale=-0.5)
                stt(y_, xtk[i], sm_[:, 2:3], gbc_f[:], ALU.mult, ALU.mult, ['xtokA%d' % i, sk + '2', 'gbc_f'], [yk])
                sdma(dst[i * 128:(i + 1) * 128, :], y_, [yk], [], 'youtA%d' % (i % 2))

        def mixer(T, kind, last, A):
            nt = T // 128
            L = 128 if kind == 'P' else DS
            nseg = T // L
            mbigb = maskbP[:] if kind == 'P' else maskbS[:]
            m01 = mask01P if kind == 'P' else mask01S
            norm(gcm, 'gcm', T, True)
            G = {n: A.get('g_' + n, T) for n in ('gi', 'lf', 'F', 'm', 'G', 'a', 'inter', 'w', 'em', 'em2', 't1', 't2')}

            def g4(n):
                return G[n][0:4, :]

            def g3(n):
                return G[n][0:4, :].rearrange("p (n c) -> p n c", c=L)
            wv, wk = wload(w_in[:, C_IF:C_IF + 8], 8, 8)
            bi = bank(); bf_ = bank()
            mm(bi, PS[bi][0:4, 0:T], [(wv[:, k, 0:4], hT[:, k, 0:T]) for k in range(8)], HK + [wk])
            mm(bf_, PS[bf_][0:4, 0:T], [(wv[:, k, 4:8], hT[:, k, 0:T]) for k in range(8)], HK + [wk])
            act(g4('gi'), PS[bi][0:4, 0:T], AF.Identity, ['ps%d' % bi, 'bif'], ['g_gi'], bias=bif[:, 0:1])
            act(g4('t1'), PS[bf_][0:4, 0:T], AF.Exp, ['ps%d' % bf_, 'negbf'], ['g_t1'], bias=negbf[:], scale=-1.0)
            act(g4('t2'), g4('t1'), AF.Ln, ['g_t1'], ['g_t2'], bias=1.0)
            ts(g4('lf'), g4('t2'), -1.0, ALU.mult, ['g_t2'], ['g_lf'])
            gst = A.get('g_gst', 16); gl = A.get('g_gl', 16); dec = A.get('g_dec', 16)
            if kind == 'S':
                qz = A.get('qz', 2 * 2176, BF16).rearrange("p (e c) -> p e c", c=2176)
                qgz = A.get('qgz', 2176, BF16)
                C0 = [A.get('C0_%d' % i, 512).rearrange("p (e d) -> p e d", d=256) for i in range(6)]
                C0Tb = [A.get('C0Tb%d' % i, 2 * 258, BF16).rearrange("p (e d) -> p e d", d=258)[:, :, 0:257] for i in range(4)]
                S0 = [A.get('S0_%d' % i, 256) for i in range(4)]
                S0b = [A.get('S0b%d' % i, 256, BF16) for i in range(4)]
                Cout = [A.get('Cout%d' % i, 512).rearrange("p (e d) -> p e d", d=256) for i in range(2)]
                Sout = [A.get('Sout%d' % i, 256) for i in range(2)]
                n0 = A.get('n0', D)[0:NS, :]
                n0T = A.get('n0T', 8 * NS).rearrange("p (k s) -> p k s", s=NS)
                wz = A.get('wz', NS, BF16)
                kwz = [A.get('kwz%d' % i, 256, BF16) for i in range(2)]
                kgz = [A.get('kgz%d' % i, 128, BF16) for i in range(2)]
                mset(qz, 0.0, ['qz'], eng='pool')
                mset(qgz, 0.0, ['qgz'], eng='pool')
            Ghi = A.get('g_Ghi', T, BF16); Gmid = A.get('g_Gmid', T, BF16); Glo = A.get('g_Glo', T, BF16)
            if kind == 'S':
                sdma(sc48[:], sconv, [], ['yout'], 'st_c')
                for k in range(8):
                    b = bank()
                    tr(b, [(PS[b][:, 0:48], sc48[:, k * 128:(k + 1) * 128])], ident[0:48, 0:48], ['yout', 'CST'])
                    cp(uhS[:, k, :, :], PS[b][:, 0:48].rearrange("p (s j) -> p s j", j=3), ['ps%d' % b], ['uhS'])
                sdma(n0, sn, [], ['n0'], 'st_n')
                for k in range(8):
                    b = bank()
                    tr(b, [(PS[b][:, 0:NS], n0[:, k * 128:(k + 1) * 128])], ident[0:NS, 0:NS], ['n0', 'CST'])
                    cp(n0T[:, k, :], PS[b][:, 0:NS], ['ps%d' % b], ['n0T'])

            def gates_gen():
                if kind == 'P':
                    scan(g4('F'), onesf[0:4, 0:T], g4('lf'), carF[:], ALU.mult, ALU.add, ['CST', 'g_lf', 'carF'], ['g_F'])
                    scan(g4('m'), g4('lf'), g4('gi'), carM[:], ALU.add, ALU.max, ['g_lf', 'g_gi', 'carM'], ['g_m'])
                    yield
                else:
                    sdma(m0T[:], sm.rearrange("s h -> h s"), [], ['m0T'], 'st_m')
                    for s in range(NS):
                        sl = slice(s * DS, (s + 1) * DS)
                        scan(G['F'][0:4, sl], onesf[0:4, 0:DS], G['lf'][0:4, sl], 0.0, ALU.mult, ALU.add, ['CST', 'g_lf'], ['g_F'])
                        scan(G['m'][0:4, sl], G['lf'][0:4, sl], G['gi'][0:4, sl], m0T[:, s:s + 1], ALU.add, ALU.max,
                             ['g_lf', 'g_gi', 'm0T'], ['g_m'])
                        if s % 4 == 3:
                            yield
                tt(g4('G'), g4('m'), g4('F'), ALU.subtract, ['g_m', 'g_F'], ['g_G'])
                tt(g4('a'), g4('gi'), g4('F'), ALU.subtract, ['g_gi', 'g_F'], ['g_a'])
                yield
                act(g4('em'), g4('m'), AF.Exp, ['g_m'], ['g_em'], scale=-1.0)
                act(g4('em2'), g4('m'), AF.Exp, ['g_m'], ['g_em2'], scale=-2.0)
                yield
                cp(gl[0:4, 0:nseg], g3('G')[:, :, L - 1], ['g_G'], ['g_gl'])
                if kind == 'P':
                    cp(gst[0:4, 0:1], carG[:], ['carG'], ['g_gst'])
                    if nseg > 1:
                        cp(gst[0:4, 1:nseg], gl[0:4, 0:nseg - 1], ['g_gl', 'g_gst'], ['g_gst'])
                else:
                    cp(gst[0:4, 0:nseg], m0T[:], ['m0T'], ['g_gst'])
                tt(g3('t1'), gst[0:4, 0:nseg].unsqueeze(2).broadcast_to([4, nseg, L]), g3('G'), ALU.subtract, ['g_gst', 'g_G'], ['g_t1'])
                act(g4('inter'), g4('t1'), AF.Exp, ['g_t1'], ['g_inter'])
                yield
                tt(g3('t2'), g3('a'), gl[0:4, 0:nseg].unsqueeze(2).broadcast_to([4, nseg, L]), ALU.subtract, ['g_a', 'g_gl'], ['g_t2'])
                act(g4('w'), g4('t2'), AF.Exp, ['g_t2'], ['g_w'])
                yield
                tt(dec[0:4, 0:nseg], gst[0:4, 0:nseg], gl[0:4, 0:nseg], ALU.subtract, ['g_gst', 'g_gl'], ['g_dec'])
                act(dec[0:4, 0:nseg], dec[0:4, 0:nseg], AF.Exp, ['g_dec'], ['g_dec'])
                yield
                cp(Ghi[0:4, :], g4('G'), ['g_G'], ['g_Ghi'])
                tt(g4('t1'), g4('G'), Ghi[0:4, :], ALU.subtract, ['g_G', 'g_Ghi', 'g_inter'], ['g_t1'])
                cp(Gmid[0:4, :], g4('t1'), ['g_t1'], ['g_Gmid'])
                yield
                tt(g4('t2'), g4('t1'), Gmid[0:4, :], ALU.subtract, ['g_t1', 'g_Gmid', 'g_w'], ['g_t2'])
                cp(Glo[0:4, :], g4('t2'), ['g_t2'], ['g_Glo'])
                yield
                if kind == 'P':
                    cp(carF[:], G['F'][0:4, T - 1:T], ['g_F', 'carF'], ['carF'])
                    cp(carM[:], G['m'][0:4, T - 1:T], ['g_m', 'carM'], ['carM'])
                    cp(carG[:], G['G'][0:4, T - 1:T], ['g_G', 'carG'], ['carG'])
                b = bank()
                pairs = []
                for i in range(nt):
                    for qi, qn in enumerate(('a', 'inter', 'w', 'em', 'em2')):
                        pairs.append((PS[b][:, i * 20 + qi * 4:i * 20 + qi * 4 + 4], G[qn][0:4, i * 128:(i + 1) * 128]))

                def fnc(pe, pairs=pairs):
                    for (o, l) in pairs:
                        ins = pe.matmul(o, l, ident[0:4, 0:4], start=True, stop=True)
                    return ins
                S.op('pe', fnc, ['g_a', 'g_inter', 'g_w', 'g_em', 'g_em2', 'CST'], ['ps%d' % b])
                cp(colsTM[:, 0:nt, :], PS[b][:, 0:nt * 20].rearrange("p (i c) -> p i c", c=20), ['ps%d' % b], ['colsTM'])
                yield
                b = bank()

                def fnd(pe, b=b, nseg=nseg, dec=dec):
                    for h in range(4):
                        ins = pe.matmul(PS[b][:, h * 16:h * 16 + nseg], SEL[:, h, :], dec[0:4, 0:nseg], start=True, stop=True)
                    return ins
                S.op('pe', fnd, ['g_dec', 'SEL'], ['ps%d' % b])
                cp(decbc[:, :, 0:nseg], PS[b][:, 0:64].rearrange("p (h c) -> p h c", c=16)[:, :, 0:nseg], ['ps%d' % b], ['decbc'])
                yield
                dump('h' + kind, hT[:, :, 0:T], HK)
                for nme in ('gi', 'lf', 'F', 'm', 'G', 'a', 'inter', 'w', 'em'):
                    dump(nme + kind, g4(nme), ['g_' + nme])
                dump('colsTM' + kind, colsTM[:, 0:nt, :], ['colsTM'])
                dump('decbc' + kind, decbc[:, :, 0:nseg], ['decbc'])
                if kind == 'S':
                    b = bank()
                    mm(b, PS[b][0:NS, 0:4], [(dec[0:4, 0:NS], ident[0:4, 0:4])], ['g_dec', 'CST'])
                    cp(dectm[:], PS[b][0:NS, 0:4], ['ps%d' % b], ['dectm'])
                    b = bank()
                    mm(b, PS[b][0:NS, 0:4], [(g3('m')[:, :, DS - 1], ident[0:4, 0:4])], ['g_m', 'CST'])
                    cp(msout[:], PS[b][0:NS, 0:4], ['ps%d' % b], ['msout'])
                    sdma(o_ms, msout[:], ['msout'], [], 'st_ms')
                elif last:
                    b = bank()
                    mm(b, PS[b][0:1, 0:4], [(G['m'][0:4, T - 1:T], ident[0:4, 0:4])], ['g_m', 'CST'])
                    cp(msout[0:1, :], PS[b][0:1, 0:4], ['ps%d' % b], ['msout'])
                    sdma(o_mp, msout[0:1, :], ['msout'], [], 'st_ms')

                yield

            gates_g = gates_gen()
            a_mark = A.off
            if kind == 'P':
                u = A.get('u', 2 * (T + 3)).rearrange("p (e t) -> p e t", t=T + 3)
            else:
                u4 = A.get('u', 2 * NS * 11).rearrange("p (e s t) -> p e s t", s=NS, t=11)
            cbuf = A.get('cbuf', T); ctmp = A.get('ctmp', T)
            ch = A.get('ch', 2 * T, BF16).rearrange("p (e t) -> p e t", t=T)
            qT = A.get('qT', 2 * T, BF16).rearrange("p (e t) -> p e t", t=T)
            kT = A.get('kT', 2 * T, BF16).rearrange("p (e t) -> p e t", t=T)
            qiT = A.get('qiT', 2 * T, BF16).rearrange("p (e t) -> p e t", t=T)
            vm = A.get('vm', nt * 257, BF16).rearrange("p (i c) -> p i c", c=257)
            ogm = A.get('ogm', nt * 256, BF16).rearrange("p (i c) -> p i c", c=256)
            sgt = A.get('sgt', 256)
            agT = A.get('agT', T)
            t1 = A.get('gt1', T); t2 = A.get('gt2', T); Cs = A.get('Cs', T); eal = A.get('eal', 16)
            qgT = A.get('qgT', T, BF16); kgT = A.get('kgT', T, BF16)
            vg = A.get('vg', nt * 256, BF16).rearrange("p (i c) -> p i c", c=256)
            rg = A.get('rg', nt * 256, BF16).rearrange("p (i c) -> p i c", c=256)
            sgt2 = A.get('sgt2', 256)
            if kind == 'P':
                CTs_ = [A.get('CTsnap%d' % j, 2 * 258, BF16).rearrange("p (e c) -> p e c", c=258) for j in range(nt - 1)]
                Ssnap = [A.get('Ssnap%d' % j, 256, BF16) for j in range(nt - 1)]
            NSL = 4

            def interleave(gens, ratios):
                gens = list(gens); ratios = list(ratios)
                while gens:
                    for gi in range(len(gens) - 1, -1, -1):
                        pass
                    alive = []
                    for g_, r_ in zip(gens, ratios):
                        ok = True
                        for _ in range(r_):
                            try:
                                next(g_)
                            except StopIteration:
                                ok = False
                                break
                        if ok:
                            alive.append((g_, r_))
                    gens = [a for a, _ in alive]; ratios = [b_ for _, b_ in alive]
                    yield

            def ml_head(h):
                wvu, wku = wload(w_in[:, C_U + h * 256:C_U + (h + 1) * 256], 8, 256)
                wins = []
                for ec in range(2):
                    c = 2 * h + ec
                    b = bankM()
                    mm(b, P(b, T), [(wvu[:, k, ec * 128:(ec + 1) * 128], hT[:, k, 0:T]) for k in range(8)], HK + [wku])
                    if kind == 'P':
                        cp(u[:, ec, 0:3], uhist[:, c, :], ['uhist'], ['u'])
                        cp(u[:, ec, 3:3 + T], P(b, T), ['ps%d' % b], ['u'], eng='act')
                        cp(uhist[:, c, :], u[:, ec, T:T + 3], ['u'], ['uhist'])
                        wins.append(([u[:, ec, 3 - j:3 - j + T] for j in range(4)], cbuf[:, 0:T]))
                    else:
                        cp(u4[:, ec, :, 0:3], uhS[:, c, :, :], ['uhS'], ['u'])
                        cp(u4[:, ec, :, 3:11], P(b, T).rearrange("p (s t) -> p s t", t=DS), ['ps%d' % b], ['u'], eng='act')
                        cp(uhS2[:, c, :, :], u4[:, ec, :, 8:11], ['u'], ['uhS2'])
                        wins.append(([u4[:, ec, :, 3 - j:11 - j] for j in range(4)], cbuf[:, 0:T].rearrange("p (s t) -> p s t", t=DS)))
                    yield

                def conv_chain():
                    for ec in range(2):
                        c = 2 * h + ec
                        win, cv = wins[ec]
                        act(cv, win[0], AF.Identity, ['u', 'cw', 'cb'], ['cbuf'], bias=cb[:, c:c + 1], scale=cw[:, 3, c:c + 1])
                        yield
                        for j in (1, 2, 3):
                            stt(cv, win[j], cw[:, 3 - j, c:c + 1], cv, ALU.mult, ALU.add, ['u', 'cw', 'cbuf'], ['cbuf'])
                            yield
                        act(ctmp[:, 0:T], cbuf[:, 0:T], AF.Exp, ['cbuf'], ['ctmp'], scale=-1.0)
                        yield
                        act(ctmp[:, 0:T], ctmp[:, 0:T], AF.Ln, ['ctmp'], ['ctmp'], bias=1.0)
                        yield
                        act(ctmp[:, 0:T], ctmp[:, 0:T], AF.Exp, ['ctmp'], ['ctmp'], scale=-1.0)
                        yield
                        tt(ch[:, ec, :], cbuf[:, 0:T], ctmp[:, 0:T], ALU.mult, ['cbuf', 'ctmp'], ['ch'])
                        yield

                def vo_proj():
                    wv, wk = wload(w_in[:, C_V + h * 256:C_V + (h + 1) * 256], 8, 256)
                    mset(vm[:, :, 256:257], 1.0, ['vm'])
                    for i in range(nt):
                        b = bankM()
                        mm(b, P(b, 256), [(hT[:, k, i * 128:(i + 1) * 128], wv[:, k, :]) for k in range(8)], HK + [wk])
                        cp(vm[:, i, 0:256], P(b, 256), ['ps%d' % b], ['vm'], eng='act')
                        yield
                    wv, wk = wload(w_in[:, C_O + h * 256:C_O + (h + 1) * 256], 8, 256)
                    for i in range(nt):
                        b = bankM()
                        mm(b, P(b, 256), [(hT[:, k, i * 128:(i + 1) * 128], wv[:, k, :]) for k in range(8)], HK + [wk])
                        sig3(sgt[:, :], P(b, 256), ['ps%d' % b], 'sgt')
                        tt(ogm[:, i, :], sgt[:, :], gbc_m[:, h * 256:(h + 1) * 256], ALU.mult, ['sgt', 'gbc_m'], ['ogm'])
                        yield

                yield from interleave([conv_chain(), vo_proj()], [2, 1])
                for _ in gates_g:
                    pass
                for (wsrc, dst, dk_, scl) in ((w_mq, qT, 'qT', 1.0), (w_mk, kT, 'kT', 1.0 / 16.0)):
                    wv, wk = wload(wsrc[h], 2, 256)
                    for ec in range(2):
                        b = bankM()
                        mm(b, P(b, T), [(wv[:, kc, ec * 128:(ec + 1) * 128], ch[:, kc, :]) for kc in range(2)], ['ch', wk])
                        act(dst[:, ec, :], P(b, T), AF.Copy, ['ps%d' % b], [dk_], scale=scl)
                    yield
                b = bankM()
                mm(b, P(b, T), [(SEL[:, h, :], G['inter'][0:4, 0:T])], ['SEL', 'g_inter'])
                for ec in range(2):
                    tt(qiT[:, ec, :], P(b, T), qT[:, ec, :], ALU.mult, ['ps%d' % b, 'qT'], ['qiT'])
                yield
                if h == 0:
                    if kind == 'P':
                        dump('u' + kind, u, ['u'])
                    dump('ch' + kind, ch, ['ch']); dump('qT' + kind, qT, ['qT']); dump('kT' + kind, kT, ['kT'])
                    dump('vm' + kind, vm, ['vm']); dump('ogm' + kind, ogm, ['ogm'])
                def finishA(i, bP):
                    pk = 'ps%d' % bP
                    sm_ = smL[i % 2]; sq_ = 'sm%d_' % (i % 2); hg_ = hgL[i]; hk_ = 'hg%d' % i
                    act(sm_[:, 0:1], PS[bP][:, 256:257], AF.Square, [pk], [sq_ + '0'])
                    act(junkb[:], PS[bP][:, 0:256], AF.Square, [pk], ['junkb', sq_ + '2'], accum=sm_[:, 2:3])
                    ts(sm_[:, 1:2], sm_[:, 0:1], colsTM[:, i, 16 + h:17 + h], ALU.max, [sq_ + '0', 'colsTM'], [sq_ + '1'], s2=EPS, op1=ALU.mult)
                    act(sm_[:, 3:4], sm_[:, 2:3], AF.Ln, [sq_ + '2', sq_ + '1'], [sq_ + '3'], bias=sm_[:, 1:2], scale=1.0 / 256)
                    act(sm_[:, 4:5], sm_[:, 3:4], AF.Exp, [sq_ + '3'], [sq_ + '4'], scale=-0.5)
                    stt(hg_[:], PS[bP][:, 0:256], sm_[:, 4:5], ogm[:, i, :], ALU.mult, ALU.mult, [pk, sq_ + '4', 'ogm'], [hk_])
                    if h == 0 and i == 0:
                        dump('DTs' + kind, DTsL[0][:], ['DTs0']); dump('sDb' + kind, sDbL[0][:], ['sDb0'])
                        dump('hg' + kind, hg_[:], [hk_])

                def finishB(i):
                    tsl = slice(i * 128, (i + 1) * 128)
                    hg_ = hgL[i]; hk_ = 'hg%d' % i
                    bt = bankM()
                    tr(bt, [(PS[bt][:, :].bitcast(BF16)[:, ec * 128:(ec + 1) * 128], hg_[:, ec * 128:(ec + 1) * 128]) for ec in range(2)],
                       identb[:], [hk_, 'identb'])
                    cp(hmT[:, 2 * h:2 * h + 2, tsl], PS[bt][:, :].bitcast(BF16)[:, 0:256].rearrange("p (e c) -> p e c", c=128),
                       ['ps%d' % bt], ['hmT%d' % h], eng='act')

                for i in range(nt):
                    tsl = slice(i * 128, (i + 1) * 128)
                    b1 = bankM()
                    mm(b1, PS[b1][:, 0:128], [(kT[:, ec, tsl], qT[:, ec, tsl]) for ec in range(2)], ['kT', 'qT'])
                    b2 = bankM()
                    mm(b2, PS[b2][:, 0:128], [(SELb[:, h, :], Ghi[0:4, tsl]), (SELb[:, h, :], Gmid[0:4, tsl]), (SELb[:, h, :], Glo[0:4, tsl]),
                                              (identb[:], mbigb)], ['SELb', 'g_Ghi', 'g_Gmid', 'g_Glo', 'identb', 'maskb'])
                    D_ = DTsL[i % 2]; dk_ = 'DTs%d' % (i % 2)
                    act(D_[:], PS[b2][:, 0:128], AF.Exp, ['ps%d' % b2, 'colsTM'], [dk_], bias=colsTM[:, i, h:h + 1], scale=-1.0)
                    tt(sDbL[i][:], PS[b1][:, 0:128], D_[:], ALU.mult, ['ps%d' % b1, dk_], ['sDb%d' % i])
                    yield
                if kind == 'P':
                    snaps = [CTb[:, h, :, :]] + [CTs_[j][:, :, 0:257] for j in range(nt - 1)]
                    snapk = ['CTb'] + ['CTsnap%d' % j for j in range(nt - 1)]
                    for i in range(nt):
                        tsl = slice(i * 128, (i + 1) * 128)
                        bk = bankM()
                        tr(bk, [(PS[bk][:, :].bitcast(BF16)[:, ec * 128:(ec + 1) * 128], kT[:, ec, tsl]) for ec in range(2)], identb[:], ['kT', 'identb'])
                        ts(kwL[i][:], PS[bk][:, :].bitcast(BF16)[:, 0:256], colsTM[:, i, 8 + h:9 + h], ALU.mult, ['ps%d' % bk, 'colsTM'], ['kw%d' % i])
                        yield
                    for i in range(nt):
                        kw_ = kwL[i]; kk_ = 'kw%d' % i
                        for dc in range(2):
                            bu = bankM()
                            mm(bu, PS[bu][:, 0:257], [(kw_[:, dc * 128:(dc + 1) * 128], vm[:, i, :])], [kk_, 'vm'])
                            stt(CT32[:, h, dc, :], CT32[:, h, dc, :], decbc[:, h, i:i + 1], PS[bu][:, 0:257], ALU.mult, ALU.add,
                                ['CT32', 'decbc', 'ps%d' % bu], ['CT32'])
                        if i < nt - 1:
                            cp(snaps[i + 1], CT32[:, h, :, :], ['CT32'], [snapk[i + 1]], eng='act')
                        if h == 0 and i == 0:
                            dump('CT0' + kind, CT32[:, 0, :, :], ['CT32']); dump('kw' + kind, kw_[:], [kk_])
                        yield
                    for i in range(nt):
                        tsl = slice(i * 128, (i + 1) * 128)
                        bP = bankM()
                        mm(bP, PS[bP][:, 0:257], [(sDbL[i][:], vm[:, i, :])] + [(qiT[:, ec, tsl], snaps[i][:, ec, :]) for ec in range(2)],
                           ['sDb%d' % i, 'vm', 'qiT', snapk[i]])
                        finishA(i, bP)
                        yield
                    for i in range(nt):
                        finishB(i)
                        yield
                    cp(CTb[:, h, :, :], CT32[:, h, :, :], ['CT32'], ['CTb'], eng='act')
                    yield
                else:
                    i = 0
                    tsl = slice(0, 128)
                    kw = kwL[0]
                    sDb = sDbL[0]
                    bP = 6
                    for ec in range(2):
                        cp(qz[:, ec, :].rearrange("p (j c) -> p j c", c=136)[:, :, 0:8],
                           qiT[:, ec, :].rearrange("p (j c) -> p j c", c=8), ['qiT'], ['qz'])
                    bk = bankM()
                    tr(bk, [(PS[bk][:, :].bitcast(BF16)[:, ec * 128:(ec + 1) * 128], kT[:, ec, tsl]) for ec in range(2)], identb[:], ['kT', 'identb'])
                    cp(kw[:], PS[bk][:, :].bitcast(BF16)[:, 0:256], ['ps%d' % bk], ['kw0'])
                    ts(wz, segm, colsTM[:, 0, 8 + h:9 + h], ALU.mult, ['CST', 'colsTM'], ['wz'])

                    NSLC = 6

                    def loadC(s_):
                        k_ = s_ % NSLC
                        sdma(C0[k_], sC[s_, h].rearrange("(e p) d -> p e d", p=128), [], ['C0_%d' % k_], 'C0_%d' % k_)
                    for s in range(NSLC):
                        loadC(s)
                    yield

                    def st1(s):
                        sl_ = s % NSL
                        sc_ = s % NSLC
                        for dc in range(2):
                            bt = bankM()
                            tr(bt, [(PS[bt][:, e2 * 128:(e2 + 1) * 128], C0[sc_][:, e2, dc * 128:(dc + 1) * 128]) for e2 in range(2)],
                               ident, ['C0_%d' % sc_, 'CST'])
                            cp(C0Tb[sl_][:, dc, 0:256], PS[bt][:, 0:256], ['ps%d' % bt], ['C0Tb%d' % sl_], eng='act')
                            cp(C0Tb[sl_][:, dc, 256:257], n0T[:, 2 * h + dc, s:s + 1], ['n0T'], ['C0Tb%d' % sl_])

                    def st2(s):
                        sl_ = s % NSL
                        S.op('pe', (lambda s_, sl__: (lambda pe: [pe.matmul(PS[6][:, 0:257], qz[:, 0, s_ * 128:(s_ + 1) * 128], C0Tb[sl__][:, 0, :], start=(s_ == 0), stop=False),
                                                                 pe.matmul(PS[6][:, 0:257], qz[:, 1, s_ * 128:(s_ + 1) * 128], C0Tb[sl__][:, 1, :], start=False, stop=False)][-1]))(s, sl_),
                             ['qz', 'C0Tb%d' % sl_], ['ps6'])
                        ts(kwz[s % 2], kw[:], wz[:, s:s + 1], ALU.mult, ['kw0', 'wz'], ['kwz%d' % (s % 2)])

                    def st3(s):
                        sl_ = s % NSLC
                        kz_ = kwz[s % 2]
                        for e2 in range(2):
                            bu = bankM()
                            mm(bu, PS[bu][:, 0:256], [(vm[:, 0, e2 * 128:(e2 + 1) * 128], kz_)], ['vm', 'kwz%d' % (s % 2)])
                            stt(Cout[s % 2][:, e2, :], C0[sl_][:, e2, :], decbc[:, h, s:s + 1], PS[bu][:, 0:256], ALU.mult, ALU.add,
                                ['C0_%d' % sl_, 'decbc', 'ps%d' % bu], ['Cout%d' % (s % 2)])
                        if s + NSLC < NS:
                            loadC(s + NSLC)
                        sdma(o_Cs[s, h].rearrange("(e p) d -> p e d", p=128), Cout[s % 2], ['Cout%d' % (s % 2)], [], 'Cout%d' % (s % 2))

                    for k_it in range(NS + 2):
                        if k_it < NS:
                            st1(k_it)
                        if 0 <= k_it - 1 < NS:
                            st2(k_it - 1)
                        if 0 <= k_it - 2 < NS:
                            st3(k_it - 2)
                        yield
                    bn = bankM()
                    mm(bn, PS[bn][0:NS, 0:256], [(wz, kw[:])], ['wz', 'kw0'])
                    stt(nnew[:, h * 256:(h + 1) * 256], n0[:, h * 256:(h + 1) * 256], dectm[:, h:h + 1], PS[bn][0:NS, 0:256],
                        ALU.mult, ALU.add, ['n0', 'dectm', 'ps%d' % bn], ['nnew'])
                    S.op('pe', lambda pe: pe.matmul(PS[6][:, 0:257], sDbL[0][:], vm[:, 0, :], start=False, stop=True), ['sDb0', 'vm'], ['ps6'])
                    finishA(0, 6)
                    yield
                    finishB(0)
                    yield

            def gla_head(h):
                def decay_chain():
                    b = bankG()
                    mm(b, P(b, T), [(wa2[:, h * 128:(h + 1) * 128], agT[0:16, :])], ['wa2', 'agT'])
                    act(t1[:, :], P(b, T), AF.Exp, ['ps%d' % b, 'negba'], ['gt1'], bias=negba[:, h:h + 1], scale=-1.0)
                    yield
                    act(t2[:, :], t1[:, :], AF.Ln, ['gt1'], ['gt2'], bias=1.0)
                    yield
                    for s in range(nseg):
                        sl = slice(s * L, (s + 1) * L)
                        scan(Cs[:, sl], onesf[:, 0:L], t2[:, sl], 0.0, ALU.mult, ALU.add, ['CST', 'gt2'], ['Cs'])
                        if kind == 'P' or s % 4 == 3:
                            yield
                    act(eal[:, 0:nseg], Cs[:, :].rearrange("p (n c) -> p n c", c=L)[:, :, L - 1], AF.Exp, ['Cs'], ['eal'], scale=-1.0 / 16)
                    yield
                    act(t1[:, :], Cs[:, :], AF.Exp, ['Cs'], ['gt1'], scale=-1.0 / 16)
                    yield
                    act(t2[:, :], Cs[:, :], AF.Exp, ['Cs'], ['gt2'], scale=1.0 / 16)
                    yield

                def vr_proj():
                    wv, wk = wload(w_in[:, C_VG + h * 256:C_VG + (h + 1) * 256], 8, 256)
                    for i in range(nt):
                        b = bankG()
                        mm(b, P(b, 256), [(hT[:, k, i * 128:(i + 1) * 128], wv[:, k, :]) for k in range(8)], HK + [wk])
                        cp(vg[:, i, :], P(b, 256), ['ps%d' % b], ['vg'], eng='act')
                        yield
                    wv, wk = wload(w_in[:, C_RG + h * 256:C_RG + (h + 1) * 256], 8, 256)
                    for i in range(nt):
                        b = bankG()
                        mm(b, P(b, 256), [(hT[:, k, i * 128:(i + 1) * 128], wv[:, k, :]) for k in range(8)], HK + [wk])
                        sig3(sgt2[:, :], P(b, 256), ['ps%d' % b], 'sgt2')
                        tt(sgt2[:, :], sgt2[:, :], gbc_g[:, h * 256:(h + 1) * 256], ALU.mult, ['sgt2', 'gbc_g'], ['sgt2'])
                        tt(rg[:, i, :], P(b, 256), sgt2[:, :], ALU.mult, ['ps%d' % b, 'sgt2'], ['rg'])
                        yield

                yield from interleave([decay_chain(), vr_proj()], [1, 1])
                wv, wk = wload(w_in[:, C_QG + h * 128:C_QG + (h + 1) * 128], 8, 128)
                b = bankG()
                mm(b, P(b, T), [(wv[:, k, :], hT[:, k, 0:T]) for k in range(8)], HK + [wk])
                stt(qgT[:, :], P(b, T), 128.0 ** -0.5, t1[:, :], ALU.mult, ALU.mult, ['ps%d' % b, 'gt1'], ['qgT'])
                yield
                wv, wk = wload(w_in[:, C_KG + h * 128:C_KG + (h + 1) * 128], 8, 128)
                b = bankG()
                mm(b, P(b, T), [(wv[:, k, :], hT[:, k, 0:T]) for k in range(8)], HK + [wk])
                tt(kgT[:, :], P(b, T), t2[:, :], ALU.mult, ['ps%d' % b, 'gt2'], ['kgT'])
                yield
                def gfinishA(i, b2):
                    sm_ = smgL[i % 2]; sq_ = 'sg%d_' % (i % 2); og_ = ogbL[i]; ok_ = 'ogb%d' % i
                    act(junkb[:], PS[b2][:, 0:256], AF.Square, ['ps%d' % b2], ['junkb', sq_ + '0'], accum=sm_[:, 0:1])
                    act(sm_[:, 1:2], sm_[:, 0:1], AF.Ln, [sq_ + '0', 'epsc'], [sq_ + '1'], bias=epsc[:], scale=1.0 / 256)
                    act(sm_[:, 2:3], sm_[:, 1:2], AF.Exp, [sq_ + '1'], [sq_ + '2'], scale=-0.5)
                    stt(og_[:], PS[b2][:, 0:256], sm_[:, 2:3], rg[:, i, :], ALU.mult, ALU.mult, ['ps%d' % b2, sq_ + '2', 'rg'], [ok_])

                def gfinishB(i):
                    tsl = slice(i * 128, (i + 1) * 128)
                    og_ = ogbL[i]; ok_ = 'ogb%d' % i
                    bt = bankG()
                    tr(bt, [(PS[bt][:, :].bitcast(BF16)[:, ec * 128:(ec + 1) * 128], og_[:, ec * 128:(ec + 1) * 128]) for ec in range(2)],
                       identb[:], [ok_, 'identb'])
                    cp(ogT[:, 2 * h:2 * h + 2, tsl], PS[bt][:, :].bitcast(BF16)[:, 0:256].rearrange("p (e c) -> p e c", c=128),
                       ['ps%d' % bt], ['ogT%d' % h], eng='act')

                for i in range(nt):
                    tsl = slice(i * 128, (i + 1) * 128)
                    b1 = bankG()
                    mm(b1, PS[b1][:, 0:128], [(kgT[:, tsl], qgT[:, tsl])], ['kgT', 'qgT'])
                    tt(attbL[i][:], PS[b1][:, 0:128], m01, ALU.mult, ['ps%d' % b1, 'CST'], ['attb%d' % i])
                    bk = bankG()
                    tr(bk, [(PS[bk][:, :].bitcast(BF16)[:, 0:128], kgT[:, tsl])], identb[:], ['kgT', 'identb'])
                    cp(kgtokL[i][:], PS[bk][:, :].bitcast(BF16)[:, 0:128], ['ps%d' % bk], ['kgtok%d' % i], eng='act')
                    yield
                if kind == 'P':
                    ssn = [Sb[:, h, :]] + [Ssnap[j] for j in range(nt - 1)]
                    ssk = ['Sb'] + ['Ssnap%d' % j for j in range(nt - 1)]
                    for i in range(nt):
                        bu = bankG()
                        mm(bu, PS[bu][:, 0:256], [(kgtokL[i][:], vg[:, i, :])], ['kgtok%d' % i, 'vg'])
                        tt(stmp[:], S32[:, h, :], PS[bu][:, 0:256], ALU.add, ['S32', 'ps%d' % bu], ['stmp'])
                        ts(S32[:, h, :], stmp[:], eal[:, i:i + 1], ALU.mult, ['stmp', 'eal'], ['S32'])
                        if i < nt - 1:
                            cp(ssn[i + 1], S32[:, h, :], ['S32'], [ssk[i + 1]], eng='act')
                        yield
                    for i in range(nt):
                        tsl = slice(i * 128, (i + 1) * 128)
                        b2 = bankG()
                        mm(b2, PS[b2][:, 0:256], [(attbL[i][:], vg[:, i, :]), (qgT[:, tsl], ssn[i])], ['attb%d' % i, 'vg', 'qgT', ssk[i]])
                        gfinishA(i, b2)
                        yield
                    for i in range(nt):
                        gfinishB(i)
                        yield
                    cp(Sb[:, h, :], S32[:, h, :], ['S32'], ['Sb'], eng='act')
                    yield
                else:
                    attb = attbL[0]; kgtok = kgtokL[0]
                    cp(qgz[:, :].rearrange("p (j c) -> p j c", c=136)[:, :, 0:8], qgT[:, :].rearrange("p (j c) -> p j c", c=8), ['qgT'], ['qgz'])
                    S.op('pe', lambda pe: pe.matmul(PS[7][:, 0:256], attbL[0][:], vg[:, 0, :], start=True, stop=False), ['attb0', 'vg'], ['ps7'])

                    def loadS(s_):
                        k_ = s_ % NSL
                        sdma(S0[k_], sS[s_, h], [], ['S0_%d' % k_], 'S0_%d' % k_)
                    for s in range(NSL):
                        loadS(s)
                    yield

                    def gs1(s):
                        sl_ = s % NSL
                        cp(S0b[sl_], S0[sl_], ['S0_%d' % sl_], ['S0b%d' % sl_], eng='act')
                        ts(kgz[s % 2], kgtok[:], segm[:, s:s + 1], ALU.mult, ['kgtok0', 'CST'], ['kgz%d' % (s % 2)])

                    def gs2(s):
                        sl_ = s % NSL
                        S.op('pe', (lambda s_, sl__: (lambda pe: pe.matmul(PS[7][:, 0:256], qgz[:, s_ * 128:(s_ + 1) * 128], S0b[sl__][:], start=False, stop=(s_ == NS - 1))))(s, sl_),
                             ['qgz', 'S0b%d' % sl_], ['ps7'])
                        bu = bankG()
                        mm(bu, PS[bu][:, 0:256], [(kgz[s % 2], vg[:, 0, :])], ['kgz%d' % (s % 2), 'vg'])
                        tt(stmp[:], S0[sl_], PS[bu][:, 0:256], ALU.add, ['S0_%d' % sl_, 'ps%d' % bu], ['stmp'])
                        ts(Sout[s % 2], stmp[:], eal[:, s:s + 1], ALU.mult, ['stmp', 'eal'], ['Sout%d' % (s % 2)])
                        if s + NSL < NS:
                            loadS(s + NSL)
                        sdma(o_Ss[s, h], Sout[s % 2], ['Sout%d' % (s % 2)], [], 'Sout%d' % (s % 2))

                    for k_it in range(NS + 1):
                        if k_it < NS:
                            gs1(k_it)
                        if 0 <= k_it - 1 < NS:
                            gs2(k_it - 1)
                        yield
                    gfinishA(0, 7)
                    yield
                    gfinishB(0)
                    yield

            wv, wk = wload(w_in[:, C_AG:C_AG + 16], 8, 16)
            b = bank()
            mm(b, PS[b][0:16, 0:T], [(wv[:, k, 0:16], hT[:, k, 0:T]) for k in range(8)], HK + [wk])
            cp(agT[0:16, :], PS[b][0:16, 0:T], ['ps%d' % b], ['agT'])
            for h in range(4):
                gens = [ml_head(h), gla_head(h)] + ([gates_g] if h == 0 else [])
                while gens:
                    for g_ in list(gens):
                        try:
                            next(g_)
                        except StopIteration:
                            gens.remove(g_)

            dump('hmT' + kind, hmT[:, :, 0:T], ['hmT%d' % h_ for h_ in range(4)])
            dump('ogT' + kind, ogT[:, :, 0:T], ['ogT%d' % h_ for h_ in range(4)])
            A.off = a_mark
            sgA = A.get('sgA', 4 * T).rearrange("p (f t) -> p f t", t=T)
            yA = A.get('yA', 4 * T).rearrange("p (f t) -> p f t", t=T)
            ytmp = A.get('ytmp', T)
            yT = A.get('yT', 8 * T, BF16).rearrange("p (f t) -> p f t", t=T)
            HM = ['hmT%d' % h for h in range(4)]
            OG = ['ogT%d' % h for h in range(4)]
            for g in range(2):
                for (gcol0, wp, src, sk_, first) in ((C_GA, w_pa, hmT, HM, True), (C_GB, w_pb, ogT, OG, False)):
                    wv, wk = wload(w_in[:, gcol0 + g * 512:gcol0 + (g + 1) * 512], 8, 512)
                    for fc in range(4):
                        b = bank()
                        mm(b, P(b, T), [(wv[:, k, fc * 128:(fc + 1) * 128], hT[:, k, 0:T]) for k in range(8)], HK + [wk])
                        act(sgA[:, fc, :], P(b, T), AF.Sigmoid, ['ps%d' % b], ['sgA'])
                    wv, wk = wload(wp[:, g * 512:(g + 1) * 512], 8, 512)
                    for fc in range(4):
                        b = bank()
                        mm(b, P(b, T), [(wv[:, k, fc * 128:(fc + 1) * 128], src[:, k, 0:T]) for k in range(8)], sk_ + [wk])
                        if first:
                            tt(yA[:, fc, :], P(b, T), sgA[:, fc, :], ALU.mult, ['ps%d' % b, 'sgA'], ['yA'])
                        else:
                            tt(ytmp[:, :], P(b, T), sgA[:, fc, :], ALU.mult, ['ps%d' % b, 'sgA'], ['ytmp'])
                            tt(yT[:, g * 4 + fc, :], ytmp[:, :], yA[:, fc, :], ALU.add, ['ytmp', 'yA'], ['yT'])
            dump('yT' + kind, yT, ['yT'])
            for g in range(2):
                wv, wk = wload(w_o[:, g * 512:(g + 1) * 512], 8, 512)
                for fc in range(4):
                    f = g * 4 + fc
                    b = bank()
                    mm(b, P(b, T), [(wv[:, k, fc * 128:(fc + 1) * 128], yT[:, k, :]) for k in range(8)], ['yT', wk])
                    tt(xT[:, f, 0:T], P(b, T), xT[:, f, 0:T], ALU.add, ['ps%d' % b, 'xT%d' % f], ['xT%d' % f])
                    norm_partial(f, T)

            if kind == 'S':
                sdma(o_ns, nnew[:], ['nnew'], [], 'st_nn')
                for k in range(8):
                    b = bank()
                    tr(b, [(PS[b][0:48, 0:128], uhS2[:, k, :, :].rearrange("p s j -> p (s j)"))], ident, ['uhS2', 'CST'])
                    cp(sc48[:, k * 128:(k + 1) * 128], PS[b][0:48, 0:128], ['ps%d' % b], ['yout'])
                sdma(o_convs, sc48[:], ['yout'], [], 'st_c')
            elif last:
                for k in range(8):
                    b = bank()
                    tr(b, [(PS[b][0:3, 0:128], uhist[:, k, :])], ident, ['uhist', 'CST'])
                    cp(sc48[0:3, k * 128:(k + 1) * 128], PS[b][0:3, 0:128], ['ps%d' % b], ['yout'])
                sdma(o_convp, sc48[0:3, :], ['yout'], [], 'st_c')
                for h in range(4):
                    for dc in range(2):
                        for e2 in range(2):
                            b = bank()
                            tr(b, [(PS[b][:, 0:128], CT32[:, h, dc, e2 * 128:(e2 + 1) * 128])], ident, ['CT32', 'CST'])
                            cp(Cn[0][:, e2, dc * 128:(dc + 1) * 128], PS[b][:, 0:128], ['ps%d' % b], ['Cn0'])
                        b = bank()
                        tr(b, [(PS[b][0:1, 0:128], CT32[:, h, dc, 256:257])], ident, ['CT32', 'CST'])
                        cp(nnew[0:1, h * 256 + dc * 128:h * 256 + (dc + 1) * 128], PS[b][0:1, 0:128], ['ps%d' % b], ['nnew'])
                    sdma(o_Cp[h].rearrange("(e p) d -> p e d", p=128), Cn[0][:], ['Cn0'], [], 'Cn0')
                    sdma(o_Sp[h], S32[:, h, :], ['S32'], [], 'st_sp')
                sdma(o_np.rearrange("(o h) d -> o (h d)", o=1), nnew[0:1, :], ['nnew'], [], 'st_nn')

        blocks = [('S', xs, ys, 128, False)] + [('P', xp[i * 512:(i + 1) * 512, :], yp[i * 512:(i + 1) * 512, :], 512, i == 3) for i in range(4)]
        for bi, (kind, src, dst, T, last) in enumerate(blocks):
            S.curT = T
            A = Arena(T)
            load_x(src, T, prefetched=(bi > 0))
            ffn(w1u, w1d, gc1, 'gc1', T, A)
            dump('x1' + kind, xT[:, :, 0:T], XK)
            A = Arena(T)
            mixer(T, kind, last, A)
            dump('x2' + kind, xT[:, :, 0:T], XK)
            if bi + 1 < len(blocks):
                prefetch_x(blocks[bi + 1][1], blocks[bi + 1][3])
                S.curT = T
            A = Arena(T)
            ffn(w2u, w2d, gc2, 'gc2', T, A, partial_done=True, next_norm=False)
            final_out(dst, T)
        S.finish()

        with nc.Block() as block:
            @block.tensor
            def _(e):
                for f in S.stream['pe']:
                    f(e)

            @block.scalar
            def _(e):
                for f in S.stream['act']:
                    f(e)

            @block.vector
            def _(e):
                for f in S.stream['dve']:
                    f(e)

            @block.gpsimd
            def _(e):
                for f in S.stream['pool']:
                    f(e)

            @block.sync
            def _(e):
                for f in S.stream['sp']:
                    f(e)
    return nc


def make_consts():
    c = np.zeros((128, 1792), np.float32)
    idx = np.arange(128)
    c[:, 0:128] = np.eye(128, dtype=np.float32)
    s, t = idx[:, None], idx[None, :]
    causal = s <= t
    same = (s // DS) == (t // DS)
    c[:, 128:256] = np.where(causal, 0.0, BIG)
    c[:, 256:384] = np.where(causal & same, 0.0, BIG)
    c[:, 384:512] = causal.astype(np.float32)
    c[:, 512:640] = (causal & same).astype(np.float32)
    c[:, 640:656] = ((idx[:, None] // DS) == np.arange(NS)[None, :]).astype(np.float32)
    for k in range(4):
        c[k, 656 + k * 128:656 + (k + 1) * 128] = 1.0
    c[:, 1280:1792] = 1.0
    return c


_NC_CACHE = {}


def kernel(x_prompt, x_sample, state_conv, state_mlstm_C, state_mlstm_n, state_mlstm_m, state_gla_S,
           g_ffn1, w_ffn1_up, w_ffn1_down, g_mix, w_in, conv_w, conv_b, w_mq, w_mk, b_if, g_mhead,
           w_a2, b_a, g_ghead, w_pa, w_pb, w_o, g_ffn2, w_ffn2_up, w_ffn2_down, g_final):
    f = lambda a: np.ascontiguousarray(np.asarray(a, dtype=np.float32))
    shared = {
        "g_ffn1": f(g_ffn1).reshape(D), "w_ffn1_up": f(w_ffn1_up).reshape(D, 2 * DFF), "w_ffn1_down": f(w_ffn1_down).reshape(DFF, D),
        "g_mix": f(g_mix).reshape(D), "w_in": f(w_in).reshape(D, 8216), "conv_w": f(conv_w).reshape(4, D), "conv_b": f(conv_b).reshape(D),
        "w_mq": f(w_mq).reshape(4, 256, 256), "w_mk": f(w_mk).reshape(4, 256, 256), "b_if": f(b_if).reshape(2, 4),
        "g_mhead": f(g_mhead).reshape(D), "w_a2": f(w_a2).reshape(16, 512), "b_a": f(b_a).reshape(512), "g_ghead": f(g_ghead).reshape(D),
        "w_pa": f(w_pa).reshape(D, D), "w_pb": f(w_pb).reshape(D, D), "w_o": f(w_o).reshape(D, D),
        "g_ffn2": f(g_ffn2).reshape(D), "w_ffn2_up": f(w_ffn2_up).reshape(D, 2 * DFF), "w_ffn2_down": f(w_ffn2_down).reshape(DFF, D),
        "g_final": f(g_final).reshape(D), "consts": make_consts(),
    }
    xp_, xs_ = f(x_prompt), f(x_sample)
    sc_, sC_, sn_, sm_, sS_ = f(state_conv)[0], f(state_mlstm_C)[0], f(state_mlstm_n)[0], f(state_mlstm_m)[0], f(state_gla_S)[0]
    in_maps = []
    for c in range(8):
        sl = slice(c * NS, (c + 1) * NS)
        m = dict(shared)
        m.update({
            "xp": xp_[c], "xs": xs_[sl].reshape(128, D), "sconv": sc_[sl].reshape(48, D), "sC": sC_[sl],
            "sn": sn_[sl].reshape(NS, D), "sm": sm_[sl], "sS": sS_[sl],
        })
        in_maps.append({k: np.ascontiguousarray(v) for k, v in m.items()})
    dbgm = bool(_NC_CACHE.get('debug'))
    key = 'nc_dbg' if dbgm else 'nc'
    if key not in _NC_CACHE:
        _NC_CACHE[key] = build_program(debug=dbgm)
    res = run_bass_kernel_spmd(_NC_CACHE[key], in_maps, core_ids=list(range(8)))
    R = res.results
    _NC_CACHE['raw'] = R if dbgm else None
    cat = lambda k: np.stack([np.asarray(r[k]) for r in R], 0)
    y_prompt = cat("yp").reshape(8, SEQ, D)
    y_sample = cat("ys").reshape(128, DS, D)
    conv_p = cat("conv_p").reshape(1, 8, 3, D)
    C_p = cat("C_p").reshape(1, 8, 4, 256, 256)
    n_p = cat("n_p").reshape(1, 8, 4, 256)
    m_p = cat("m_p").reshape(1, 8, 4)
    S_p = cat("S_p").reshape(1, 8, 4, 128, 256)
    conv_s = cat("conv_s").reshape(1, 128, 3, D)
    C_s = cat("C_s").reshape(1, 128, 4, 256, 256)
    n_s = cat("n_s").reshape(1, 128, 4, 256)
    m_s = cat("m_s").reshape(1, 128, 4)
    S_s = cat("S_s").reshape(1, 128, 4, 128, 256)
    outs = (y_prompt, y_sample, conv_p, C_p, n_p, m_p, S_p, conv_s, C_s, n_s, m_s, S_s)
    return tuple(np.ascontiguousarray(o, dtype=np.float32) for o in outs)
```
